# Optimizing a Trainium2 kernel written in Bass

```python
import jax, jax.numpy as jnp
from jax import lax
import numpy as np

D_MODEL = 1024
BATCH = 8
SEQ = 2048
DEPTH = 2
DEC_BATCH = 128
DEC_SEQ = 8
PAST_LEN = 16384
PAGE_SIZE = 128

SGU_WIDTH = D_MODEL
SGU_GROUPS = 4
SGU_CHUNK = 128
MLSTM_WIDTH = D_MODEL
MLSTM_HEADS = 4
MLSTM_DK = MLSTM_WIDTH // MLSTM_HEADS
MLSTM_DV = MLSTM_WIDTH // MLSTM_HEADS
MLSTM_CHUNK = 128
POOL_WIDTH = D_MODEL
POOL_WINDOWS = (2, 4, 8, 16)
POOL_GROUPS = 4
POOL_BUF = 15
N_BRANCH = 3
EPS = 1e-6

IN_SIZES = (N_BRANCH * D_MODEL, SGU_WIDTH, SGU_WIDTH, SGU_WIDTH,
            MLSTM_WIDTH, MLSTM_WIDTH, MLSTM_WIDTH, MLSTM_WIDTH, MLSTM_WIDTH,
            MLSTM_HEADS, MLSTM_HEADS, POOL_WIDTH, POOL_WIDTH)
N_IN = sum(IN_SIZES)

kernel_name = "gated_sgu_mlstm_pool_decoder_step"


def rmsnorm(x, g):
    xf = x.astype(jnp.float32)
    y = xf * lax.rsqrt(jnp.mean(xf * xf, axis=-1, keepdims=True) + EPS)
    return (y * g.astype(jnp.float32)).astype(x.dtype)


def layernorm(x, g, b):
    xf = x.astype(jnp.float32)
    mu = jnp.mean(xf, axis=-1, keepdims=True)
    var = jnp.mean(jnp.square(xf - mu), axis=-1, keepdims=True)
    y = (xf - mu) * lax.rsqrt(var + EPS)
    return (y * g.astype(jnp.float32) + b.astype(jnp.float32)).astype(x.dtype)


def head_layernorm(h, g):
    B, T, W = h.shape
    hf = h.astype(jnp.float32).reshape(B, T, MLSTM_HEADS, W // MLSTM_HEADS)
    mu = jnp.mean(hf, axis=-1, keepdims=True)
    var = jnp.mean(jnp.square(hf - mu), axis=-1, keepdims=True)
    y = ((hf - mu) * lax.rsqrt(var + EPS)).reshape(B, T, W)
    return (y * g.astype(jnp.float32)).astype(h.dtype)


def chunk_spatial_gate(v, w_s, b_s):
    B, T, W = v.shape
    L = min(T, SGU_CHUNK)
    n_chunks = T // L
    vr = v.reshape(B, n_chunks, L, SGU_GROUPS, W // SGU_GROUPS)
    w = jnp.tril(w_s[:, :L, :L]).astype(v.dtype)
    s = jnp.einsum('gij,bnjgc->bnigc', w, vr) + b_s[:, :L].T.astype(v.dtype)[None, None, :, :, None]
    return s.reshape(B, T, W)


def mlstm(q, k, v, ig, fg, C0, n0, m0):
    B, T, _ = q.shape
    L = min(T, MLSTM_CHUNK)
    n_chunks = T // L

    def heads(a, d):
        return a.astype(jnp.float32).reshape(B, n_chunks, L, MLSTM_HEADS, d).transpose(1, 0, 3, 2, 4)

    def gates(a):
        return a.reshape(B, n_chunks, L, MLSTM_HEADS).transpose(1, 0, 3, 2)

    qh = heads(q, MLSTM_DK)
    kh = heads(k, MLSTM_DK) * (MLSTM_DK ** -0.5)
    vh = heads(v, MLSTM_DV)
    igh = gates(ig.astype(jnp.float32))
    lfh = gates(jax.nn.log_sigmoid(fg.astype(jnp.float32)))
    causal = jnp.tril(jnp.ones((L, L), dtype=bool))

    def step(carry, inp):
        C, n, m = carry
        qc, kc, vc, ic, lc = inp
        b = jnp.cumsum(lc, axis=-1)
        d_log = b[..., :, None] - b[..., None, :] + ic[..., None, :]
        d_log = jnp.where(causal, d_log, -jnp.inf)
        inter = b + m[..., None]
        m_t = jnp.maximum(inter, jnp.max(d_log, axis=-1))
        w_intra = jnp.exp(d_log - m_t[..., None])
        w_inter = jnp.exp(inter - m_t)
        s = jnp.einsum('bhtd,bhsd->bhts', qc, kc) * w_intra
        num = w_inter[..., None] * jnp.einsum('bhtd,bhde->bhte', qc, C) + jnp.einsum('bhts,bhse->bhte', s, vc)
        den = w_inter * jnp.einsum('bhtd,bhd->bht', qc, n) + jnp.sum(s, axis=-1)
        h = num / jnp.maximum(jnp.abs(den), jnp.exp(-m_t))[..., None]
        b_end = b[..., -1]
        log_end = b_end[..., None] - b + ic
        m_new = jnp.maximum(b_end + m, jnp.max(log_end, axis=-1))
        decay = jnp.exp(b_end + m - m_new)
        w_end = jnp.exp(log_end - m_new[..., None])
        kw = kc * w_end[..., None]
        C_new = decay[..., None, None] * C + jnp.einsum('bhsd,bhse->bhde', kw, vc)
        n_new = decay[..., None] * n + jnp.sum(kw, axis=2)
        return (C_new, n_new, m_new), h

    init = (C0.astype(jnp.float32), n0.astype(jnp.float32), m0.astype(jnp.float32))
    (C, n, m), hs = lax.scan(step, init, (qh, kh, vh, igh, lfh))
    h = hs.transpose(1, 0, 3, 2, 4).reshape(B, T, MLSTM_HEADS * MLSTM_DV)
    return h.astype(q.dtype), C, n, m


def multiscale_pool(p, buf, n_past):
    B, T, W = p.shape
    wg = W // POOL_GROUPS
    full = jnp.concatenate([buf.astype(p.dtype), p], axis=1)
    cs = jnp.cumsum(full.astype(jnp.float32), axis=1)
    cs = jnp.concatenate([jnp.zeros((B, 1, W), jnp.float32), cs], axis=1)
    end = cs[:, POOL_BUF + 1:]
    pos = jnp.arange(T) + n_past
    pf = p.astype(jnp.float32)
    outs = []
    for g, w in enumerate(POOL_WINDOWS):
        sl = slice(g * wg, (g + 1) * wg)
        start = cs[:, POOL_BUF + 1 - w:POOL_BUF + 1 - w + T, sl]
        cnt = jnp.minimum(pos + 1, w).astype(jnp.float32)
        outs.append((end[..., sl] - start) / cnt[None, :, None] - pf[..., sl])
    return jnp.concatenate(outs, axis=-1).astype(p.dtype), full[:, -POOL_BUF:]


def mixer_layer(x, c, st_C, st_n, st_m, pool_buf, n_past,
                w_mod, b_mod, norm_g, w_in, b_if, sgu_ln_g, sgu_ln_b, w_sgu, b_sgu,
                mlstm_norm_g, w_pool, pool_scale, w_br_a, w_br_b, w_br_c, w_out):
    B, T, _ = x.shape
    mod = jax.nn.silu(c) @ w_mod + b_mod
    shift, scale, gate = jnp.split(mod, 3, axis=-1)
    h = rmsnorm(x, norm_g) * (1 + scale[:, None]) + shift[:, None]
    proj = h @ w_in
    split_idx = np.cumsum(IN_SIZES)[:-1].tolist()
    (mg, u, va, za, q, k, vb, ob, zb, ig, fg, p, zc) = jnp.split(proj, split_idx, axis=-1)

    u = jax.nn.gelu(u)
    va = layernorm(jax.nn.gelu(va), sgu_ln_g, sgu_ln_b)
    ya = u * chunk_spatial_gate(va, w_sgu, b_sgu) * jax.nn.silu(za)

    hb, C_new, n_new, m_new = mlstm(q, k, vb, ig + b_if[:MLSTM_HEADS], fg + b_if[MLSTM_HEADS:], st_C, st_n, st_m)
    hb = jax.nn.sigmoid(ob) * hb
    yb = head_layernorm(hb, mlstm_norm_g) * jax.nn.silu(zb)

    pooled, new_buf = multiscale_pool(p, pool_buf, n_past)
    pg = pooled.reshape(B, T, POOL_GROUPS, POOL_WIDTH // POOL_GROUPS)
    pm = jnp.einsum('btgc,gcd->btgd', pg, w_pool).reshape(B, T, POOL_WIDTH)
    yc = pm * pool_scale * jax.nn.silu(zc)

    g_a, g_b, g_c = jnp.split(jax.nn.sigmoid(mg), 3, axis=-1)
    merged = g_a * (ya @ w_br_a) + g_b * (yb @ w_br_b) + g_c * (yc @ w_br_c)
    x = x + gate[:, None] * (merged @ w_out)
    return x, C_new, n_new, m_new, new_buf, va


def setup_inputs(seed: int = 0) -> dict:
    key = jax.random.key(seed)
    ks = jax.random.split(key, 26)
    f32 = jnp.float32

    def nrm(k, shape, s):
        return jax.random.normal(k, shape, f32) * s

    wg = POOL_WIDTH // POOL_GROUPS
    b_if = jnp.concatenate([nrm(ks[9], (DEPTH, MLSTM_HEADS), 0.1),
                            3.0 + nrm(ks[10], (DEPTH, MLSTM_HEADS), 0.1)], axis=-1)
    return {
        "x_prompt": nrm(ks[0], (BATCH, SEQ, D_MODEL), 1.0),
        "x_sample": nrm(ks[1], (DEC_BATCH, DEC_SEQ, D_MODEL), 1.0),
        "c_prompt": nrm(ks[2], (BATCH, D_MODEL), 1.0),
        "c_sample": nrm(ks[3], (DEC_BATCH, D_MODEL), 1.0),
        "state_mlstm_C": nrm(ks[4], (DEPTH, DEC_BATCH, MLSTM_HEADS, MLSTM_DK, MLSTM_DV), 0.1),
        "state_mlstm_n": nrm(ks[5], (DEPTH, DEC_BATCH, MLSTM_HEADS, MLSTM_DK), 0.1),
        "state_mlstm_m": nrm(ks[6], (DEPTH, DEC_BATCH, MLSTM_HEADS), 1.0),
        "state_pool": nrm(ks[7], (DEPTH, DEC_BATCH, POOL_BUF, POOL_WIDTH), 1.0),
        "w_mod": nrm(ks[8], (DEPTH, D_MODEL, 3 * D_MODEL), 0.5 * D_MODEL ** -0.5),
        "b_mod": nrm(ks[11], (DEPTH, 3 * D_MODEL), 0.02),
        "norm_g": 1.0 + nrm(ks[12], (DEPTH, D_MODEL), 0.02),
        "w_in": nrm(ks[13], (DEPTH, D_MODEL, N_IN), D_MODEL ** -0.5),
        "b_if": b_if,
        "sgu_ln_g": 1.0 + nrm(ks[14], (DEPTH, SGU_WIDTH), 0.02),
        "sgu_ln_b": nrm(ks[15], (DEPTH, SGU_WIDTH), 0.02),
        "w_sgu": nrm(ks[16], (DEPTH, SGU_GROUPS, SGU_CHUNK, SGU_CHUNK), 0.5 * SGU_CHUNK ** -0.5),
        "b_sgu": 1.0 + nrm(ks[17], (DEPTH, SGU_GROUPS, SGU_CHUNK), 0.02),
        "mlstm_norm_g": 1.0 + nrm(ks[18], (DEPTH, MLSTM_WIDTH), 0.02),
        "w_pool": nrm(ks[19], (DEPTH, POOL_GROUPS, wg, wg), wg ** -0.5),
        "pool_scale": 1.0 + nrm(ks[20], (DEPTH, POOL_WIDTH), 0.02),
        "w_br_a": nrm(ks[21], (DEPTH, SGU_WIDTH, D_MODEL), SGU_WIDTH ** -0.5),
        "w_br_b": nrm(ks[22], (DEPTH, MLSTM_WIDTH, D_MODEL), MLSTM_WIDTH ** -0.5),
        "w_br_c": nrm(ks[23], (DEPTH, POOL_WIDTH, D_MODEL), POOL_WIDTH ** -0.5),
        "w_out": nrm(ks[24], (DEPTH, D_MODEL, D_MODEL), D_MODEL ** -0.5),
        "final_norm_g": 1.0 + nrm(ks[25], (D_MODEL,), 0.02),
    }


def reference(x_prompt, x_sample, c_prompt, c_sample, state_mlstm_C, state_mlstm_n, state_mlstm_m, state_pool,
              w_mod, b_mod, norm_g, w_in, b_if, sgu_ln_g, sgu_ln_b, w_sgu, b_sgu, mlstm_norm_g,
              w_pool, pool_scale, w_br_a, w_br_b, w_br_c, w_out, final_norm_g):
    B = x_prompt.shape[0]
    xp, xs = x_prompt, x_sample
    Cp_l, np_l, mp_l, bp_l = [], [], [], []
    Cs_l, ns_l, ms_l, bs_l, vs_l = [], [], [], [], []
    for l in range(DEPTH):
        lw = (w_mod[l], b_mod[l], norm_g[l], w_in[l], b_if[l], sgu_ln_g[l], sgu_ln_b[l], w_sgu[l], b_sgu[l],
              mlstm_norm_g[l], w_pool[l], pool_scale[l], w_br_a[l], w_br_b[l], w_br_c[l], w_out[l])
        C0 = jnp.zeros((B, MLSTM_HEADS, MLSTM_DK, MLSTM_DV), jnp.float32)
        n0 = jnp.zeros((B, MLSTM_HEADS, MLSTM_DK), jnp.float32)
        m0 = jnp.zeros((B, MLSTM_HEADS), jnp.float32)
        buf0 = jnp.zeros((B, POOL_BUF, POOL_WIDTH), xp.dtype)
        xp, Cp, np_, mp, bp, _ = mixer_layer(xp, c_prompt, C0, n0, m0, buf0, 0, *lw)
        xs, Cs, ns, ms, bs, vs = mixer_layer(xs, c_sample, state_mlstm_C[l], state_mlstm_n[l], state_mlstm_m[l],
                                             state_pool[l], PAST_LEN, *lw)
        Cp_l.append(Cp); np_l.append(np_); mp_l.append(mp); bp_l.append(bp)
        Cs_l.append(Cs); ns_l.append(ns); ms_l.append(ms); bs_l.append(bs); vs_l.append(vs)
    y_prompt = rmsnorm(xp, final_norm_g)
    y_sample = rmsnorm(xs, final_norm_g)
    return (y_prompt, y_sample,
            jnp.stack(Cp_l), jnp.stack(np_l), jnp.stack(mp_l), jnp.stack(bp_l),
            jnp.stack(Cs_l), jnp.stack(ns_l), jnp.stack(ms_l), jnp.stack(bs_l), jnp.stack(vs_l))
```

```python
import numpy as np
import concourse.bass as bass
import concourse.mybir as mybir
from concourse.bass_utils import run_bass_kernel_spmd
from contextlib import ExitStack

F32 = mybir.dt.float32
BF16 = mybir.dt.bfloat16
AF = mybir.ActivationFunctionType
ALU = mybir.AluOpType
AX = mybir.AxisListType

D = 1024
NT = 17
NIN = 13320
EPS = 1e-6
WINS = (2, 4, 8, 16)
GROUPS = [[4 * i + 1, 4 * i + 2, 4 * i + 3, 4 * i + 4] for i in range(4)] + [[0]]
GMAX = 4
NW = 2
NBLK = 40


class Buf:
    __slots__ = ("name", "w", "r")

    def __init__(self, name):
        self.name = name
        self.w = None
        self.r = {}


class Prog:
    ENGS = ("pe", "act", "dve", "pool", "sp")

    def __init__(self, nc, es, n_dsem=24, n_fixed=8, n_sw=8):
        self.nc = nc
        self.ops = {e: [] for e in self.ENGS}
        self.sem = {e: es.enter_context(nc.semaphore("s_" + e)) for e in self.ENGS}
        self.cnt = {e: 0 for e in self.ENGS}
        self.waited = {e: {} for e in self.ENGS}
        self.dsem = [es.enter_context(nc.semaphore("d%d" % i)) for i in range(n_dsem + n_fixed + n_sw)]
        self.dcnt = [0] * (n_dsem + n_fixed + n_sw)
        self.drr = 0
        self.swrr = 0
        self.n_rr = n_dsem
        self.n_fixed = n_fixed
        self.n_sw = n_sw
        self.bufs = {}

    def buf(self, name):
        b = self.bufs.get(name)
        if b is None:
            b = Buf(name)
            self.bufs[name] = b
        return b

    def _semof(self, k):
        return self.sem[k] if isinstance(k, str) else self.dsem[k[1]]

    def _deps(self, eng, reads, writes, extra=()):
        deps = {}

        def add(k, v):
            if deps.get(k, 0) < v:
                deps[k] = v
        for b in reads:
            if b.w is not None:
                add(*b.w)
        for b in writes:
            if b.w is not None:
                add(*b.w)
            for k, v in b.r.items():
                add(k, v)
        for k, v in extra:
            add(k, v)
        waits = []
        wd = self.waited[eng]
        for k, v in deps.items():
            if wd.get(k, 0) >= v:
                continue
            wd[k] = v
            waits.append((k, v))
        return waits

    def _mark(self, tok, reads, writes):
        k, v = tok
        for b in reads:
            if b.r.get(k, 0) < v:
                b.r[k] = v
        for b in writes:
            b.w = tok
            b.r = {}

    def emit(self, eng, fn, reads=(), writes=()):
        waits = self._deps(eng, reads, writes)
        self.cnt[eng] += 1
        tok = (eng, self.cnt[eng])
        self._mark(tok, reads, writes)
        self.ops[eng].append((waits, fn, None))

    def dma(self, q, out, in_, reads=(), writes=(), dsem=None, **kw):
        if dsem is not None:
            i = self.n_rr + dsem
        elif q == "pool":
            i = self.n_rr + self.n_fixed + self.swrr
            self.swrr = (self.swrr + 1) % self.n_sw
        else:
            i = self.drr
            self.drr = (self.drr + 1) % self.n_rr
        extra = [(("d", i), self.dcnt[i])] if self.dcnt[i] > 0 else []
        waits = self._deps(q, reads, writes, extra)
        self.dcnt[i] += 16
        tok = (("d", i), self.dcnt[i])
        self._mark(tok, reads, writes)

        def fn(e, out=out, in_=in_, kw=kw):
            return e.dma_start(out=out, in_=in_, **kw)
        self.ops[q].append((waits, fn, i))

    def finish(self):
        waits = [(("d", i), c) for i, c in enumerate(self.dcnt) if c > 0]
        waits += [(e, self.cnt[e]) for e in self.ENGS if e != "sp" and self.cnt[e] > 0]
        self.ops["sp"].append((waits, None, None))

    def replay(self, block):
        prog = self

        def run(ename, e):
            for waits, fn, di in prog.ops[ename]:
                for k, v in waits:
                    e.wait_ge(prog._semof(k), v)
                if fn is None:
                    continue
                ins = fn(e)
                if di is None:
                    ins.then_inc(prog.sem[ename], 1)
                else:
                    ins.then_inc(prog.dsem[di], 16)

        @block.tensor
        def _(e):
            run("pe", e)

        @block.scalar
        def _(e):
            run("act", e)

        @block.vector
        def _(e):
            run("dve", e)

        @block.gpsimd
        def _(e):
            run("pool", e)

        @block.sync
        def _(e):
            run("sp", e)


def make_consts():
    t = np.arange(128)
    c = {}
    c["identf"] = np.eye(128, dtype=np.float32)
    c["onesf"] = np.ones((128, 128), np.float32)
    seq = t // 8
    valid = []
    for kind in range(2):
        v = (t[:, None] <= t[None, :])
        if kind == 1:
            v = v & (seq[:, None] == seq[None, :])
        valid.append(v)
    c["tri"] = np.stack([v.astype(np.float32) for v in valid])
    c["maskneg"] = np.stack([np.where(v.T, 0.0, -1e30).astype(np.float32) for v in valid])
    c["bigmask"] = np.stack([np.where(v, 0.0, 1e30).astype(np.float32) for v in valid])
    last = [np.full(128, 127), seq * 8 + 7]
    c["sellast"] = np.stack([(t[None, :] == last[k][:, None]).astype(np.float32) for k in range(2)])
    c["seqcol"] = (seq[:, None] == np.arange(16)[None, :]).astype(np.float32)
    c["lastsel"] = (t[:, None] == (np.arange(16) * 8 + 7)[None, :]).astype(np.float32)
    selmod = np.zeros((2, 17, 128), np.float32)
    selmod[0, 0, :] = 1.0
    selmod[1, 1 + seq, t] = 1.0
    c["selmod"] = selmod
    bandcur = np.zeros((2, 4, 128, 128), np.float32)
    bandprev = np.zeros((4, 128, 128), np.float32)
    bandbuf = np.zeros((2, 4, 128, 128), np.float32)
    for wi, w in enumerate(WINS):
        s = t[:, None]
        tt = t[None, :]
        bandcur[0, wi] = ((s <= tt) & (s > tt - w))
        bandcur[1, wi] = ((s <= tt) & (s > tt - w) & (seq[:, None] == seq[None, :]))
        bandprev[wi] = ((s - 128) > (tt - w))
        for ab in range(2):
            for bb in range(8):
                for r in range(15):
                    row = bb * 15 + r
                    for i in range(8):
                        tok = (ab * 8 + bb) * 8 + i
                        if (r - 15) > (i - w):
                            bandbuf[ab, wi, row, tok] = 1.0
    c["bandcur"] = bandcur
    c["bandprev"] = bandprev
    c["bandbuf"] = bandbuf
    rc = np.zeros((3, 128, 4), np.float32)
    for wi, w in enumerate(WINS):
        rc[0, :, wi] = 1.0 / w
        rc[1, :, wi] = 1.0 / np.minimum(t + 1, w)
        rc[2, :, wi] = 1.0 / w
    c["rc"] = rc
    return c


_CONSTS = None


def build_nc():
    nc = bass.Bass("TRN2", target_bir_lowering=False)
    es = ExitStack()
    with es:
        din = lambda name, shape: nc.dram_tensor(name, list(shape), F32, kind="ExternalInput").ap()
        dout = lambda name, shape: nc.dram_tensor(name, list(shape), F32, kind="ExternalOutput").ap()
        xin = din("xin", (NT, 128, D))
        cc = din("cc", (17, D))
        w_mod = din("w_mod", (2, D, 3 * D))
        b_modT = din("b_modT", (2, 128, 16))
        b_modg = din("b_modg", (2, 1, D))
        norm_gT = din("norm_gT", (2, 128, 8))
        w_in = din("w_in", (2, D, NIN))
        b_if = din("b_if", (2, 1, 8))
        sgu_ln_g = din("sgu_ln_g", (2, 1, D))
        sgu_ln_b = din("sgu_ln_b", (2, 1, D))
        wsgu = din("wsgu", (2, 2, 4, 128, 128))
        bsgu = din("bsgu", (2, 2, 128, 4))
        mnorm_gT = din("mnorm_gT", (2, 128, 8))
        w_pool = din("w_pool", (2, 4, 256, 256))
        pscaleT = din("pscaleT", (2, 128, 8))
        w_br = [din("w_br_a", (2, D, D)), din("w_br_b", (2, D, D)), din("w_br_c", (2, D, D))]
        w_out = din("w_out", (2, D, D))
        fin_g = din("fin_g", (1, D))
        C0 = din("C0", (2, 16, 4, 256, 256))
        n0 = din("n0", (2, 16, D))
        n0tok = din("n0tok", (2, 128, D))
        m0tok = din("m0tok", (2, 128, 4))
        pool0 = din("pool0", (2, 2, 120, D))
        pool0raw = din("pool0raw", (2, 16, 15, D))
        cst = {k: din("c_" + k, v.shape) for k, v in make_consts().items()}
        y = dout("y", (NT, 128, D))
        oCp = dout("oCp", (2, 4, 256, 256))
        onp = dout("onp", (2, 4, 256))
        omp = dout("omp", (2, 4))
        obp = dout("obp", (2, 15, D))
        oCs = dout("oCs", (2, 16, 4, 256, 256))
        ons = dout("ons", (2, 16, D))
        oms = dout("oms", (2, 16, 4))
        obs = dout("obs", (2, 16, 15, D))
        ovs = dout("ovs", (2, 128, D))

        P = Prog(nc, es)
        B = P.buf
        sb = lambda name, shape, dt=F32: es.enter_context(nc.sbuf_tensor(name, list(shape), dt))
        ps = lambda name, shape, dt=F32: es.enter_context(nc.psum_tensor(name, list(shape), dt))

        xg = sb("xg", (128, GMAX, D))
        hT = sb("hT", (128, 8, GMAX * 128), BF16)
        F1 = sb("F1", (128, GMAX, D))
        VV = sb("VV", (128, GMAX, D), BF16)
        GT = sb("GT", (128, GMAX, 512))
        MG = sb("MG", (128, GMAX, D))
        QB = sb("QB", (128, GMAX, D), BF16)
        KB = sb("KB", (128, GMAX, D), BF16)
        GP = sb("GP", (128, GMAX, 8))
        YT = sb("YT", (128, 8, GMAX * 128), BF16)
        wring = [sb("wr%d" % i, (128, 8, 512), BF16) for i in range(NW)]
        tmpA = [sb("tmpA%d" % i, (128, 512)) for i in range(2)]
        xnbs = [sb("xnb%d" % i, (128, D), BF16) for i in range(2)]
        ybfs = [sb("ybf%d" % i, (128, D), BF16) for i in range(2)]
        Cst = [sb("Cst%d" % l, (128, 2, 4, 256)) for l in range(2)]
        Cbf = [sb("Cbf%d" % l, (128, 2, 4, 256), BF16) for l in range(2)]
        nst = [sb("nst%d" % l, (128, 2, 4)) for l in range(2)]
        nbf = [sb("nbf%d" % l, (128, 2, 4), BF16) for l in range(2)]
        mrep = [sb("mrep%d" % l, (128, 4)) for l in range(2)]
        pprev = [sb("pprev%d" % l, (128, D), BF16) for l in range(2)]
        QT = sb("QT", (128, 8, 128), BF16)
        KT = sb("KT", (128, 8, 128), BF16)
        QsT = sb("QsT", (128, 8, 128), BF16)
        SWT = sb("SWT", (128, 512), BF16)
        WTt = sb("WTt", (128, 512))
        DG = sb("DG", (128, 512))
        AM = sb("AM", (128, 512))
        sm = sb("sm", (128, 64))
        ZQ = [Cbf[0][:, i].rearrange("p h (a t) -> p (h a) t", a=2) for i in range(2)]
        KWb = [Cbf[1][:, i].rearrange("p h e -> p (h e)") for i in range(2)]
        t4 = lambda ap: ap.rearrange("p (h e) -> p h e", h=4)
        C0f = [t4(xg[:, 1, :]), t4(xg[:, 2, :]), t4(xg[:, 3, :]), t4(F1[:, 1, :])]
        C0b = [t4(VV[:, 1, :]), t4(VV[:, 2, :]), t4(VV[:, 3, :])]
        Cstage = [t4(F1[:, 2, :]), t4(F1[:, 3, :]), t4(MG[:, 1, :])]
        fence_t = sb("fence_t", (128, 1))
        DEC = sb("DEC", (128, 64))
        WL = sb("WL", (128, 64))
        s16 = sb("s16", (16, 16))
        pbufA = QB[:, 1, :]
        pbufB = QB[:, 2, :]
        identf = sb("identf", (128, 128))
        identb = sb("identb", (128, 128), BF16)
        onesf = sb("onesf", (128, 128))
        onesb = sb("onesb", (128, 128), BF16)
        tri = sb("tri", (128, 2, 128))
        maskneg = sb("maskneg", (128, 2, 128))
        bigmask = sb("bigmask", (128, 2, 128))
        sellast = sb("sellast", (128, 2, 128))
        seqcol = sb("seqcol", (128, 16))
        seqcolb = sb("seqcolb", (128, 16), BF16)
        lastsel = sb("lastsel", (128, 16))
        selmod = sb("selmod", (49, 2, 128))
        bandcur = sb("bandcur", (128, 2, 4, 128), BF16)
        bandprev = sb("bandprev", (128, 4, 128), BF16)
        bandbuf = sb("bandbuf", (128, 2, 4, 128), BF16)
        rc = sb("rc", (128, 3, 4))
        WTs = sb("WTs", (128, 2, 2, 4, 128), BF16)
        bsg = sb("bsg", (128, 2, 2, 4))
        lng = sb("lng", (128, D))
        lnb = sb("lnb", (128, D))
        bifb = sb("bifb", (128, 2, 8))
        nrmgT = sb("nrmgT", (128, 2, 8))
        mngT = sb("mngT", (128, 2, 8))
        pscT = sb("pscT", (128, 2, 8))
        bmT = sb("bmT", (128, 2, 16))
        AT = sb("AT", (128, 2, 8, 17))
        SH = sb("SH", (128, 2, 8, 17))
        gate17 = sb("gate17", (49, D))
        gbc = sb("gbc", (128, D))
        ccT = sb("ccT", (128, 8, 17), BF16)
        epsc = sb("epsc", (128, 1))
        tmpscr = sb("tmpscr", (128, D))
        lnst = sb("lnst", (128, 16, 6))
        lnmv = sb("lnmv", (128, 16, 2))
        lnrs = sb("lnrs", (128, 16))
        rs = sb("rs", (128, 4, 4))
        sm2 = sb("sm2", (128, 4, 12))
        pm = [ps("pm%d" % i, (128, 512)) for i in range(3)]
        pt = [ps("pt%d" % i, (128, 8, 128), BF16) for i in range(2)]
        pa = [ps("pa%d" % i, (128, 512)) for i in range(3)]
        ptf = [pt[i][:].rearrange("p c t -> p (c t)").bitcast(F32) for i in range(2)]
        Bpm = [B("pm%d" % i) for i in range(3)]
        Bpt = [B("pt%d" % i) for i in range(2)]
        Bpa = [B("pa%d" % i) for i in range(3)]
        rr = {"pm": 0, "pt": 0, "w": 0, "tmp": 0}

        def nxt(key, n):
            i = rr[key]
            rr[key] = (i + 1) % n
            return i

        def act(out, in_, func, reads, writes, **kw_):
            P.emit("act", lambda e: e.activation(out=out, in_=in_, func=func, **kw_), reads, writes)

        def tt(out, in0, in1, op, reads, writes, eng="dve"):
            P.emit(eng, lambda e: e.tensor_tensor(out=out, in0=in0, in1=in1, op=op), reads, writes)

        def ts(out, in0, s1, s2, op0, op1, reads, writes, eng="dve"):
            if op1 is None:
                P.emit(eng, lambda e: e.tensor_scalar(out=out, in0=in0, scalar1=s1, scalar2=None, op0=op0), reads, writes)
            else:
                P.emit(eng, lambda e: e.tensor_scalar(out=out, in0=in0, scalar1=s1, scalar2=s2, op0=op0, op1=op1), reads, writes)

        def stt(out, in0, scalar, in1, op0, op1, reads, writes):
            P.emit("dve", lambda e: e.scalar_tensor_tensor(out=out, in0=in0, scalar=scalar, in1=in1, op0=op0, op1=op1), reads, writes)

        def cp(out, in_, reads, writes, eng="dve"):
            if eng == "act":
                P.emit("act", lambda e: e.activation(out=out, in_=in_, func=AF.Copy), reads, writes)
            else:
                P.emit(eng, lambda e: e.tensor_copy(out=out, in_=in_), reads, writes)

        def red(out, in_, op, reads, writes):
            P.emit("dve", lambda e: e.tensor_reduce(out=out, in_=in_, axis=AX.X, op=op), reads, writes)

        def mmg(mms, reads, writes):
            def fn(e, mms=mms):
                for (o, l, r, st, sp) in mms:
                    ins = e.matmul(o, lhsT=l, rhs=r, start=st, stop=sp)
                return ins
            P.emit("pe", fn, reads, writes)

        def transposes(pti, src_fn, n, reads):
            def fn(e):
                for c in range(n):
                    ins = e.transpose(out=pt[pti][:, c, :], in_=src_fn(c), identity=identb[:])
                return ins
            P.emit("pe", fn, list(reads) + [B("identb")], [Bpt[pti]])

        def bcast3(ap2, n):
            return ap2.unsqueeze(2).to_broadcast([ap2.shape[0], ap2.shape[1], n])

        def ld(dst, src, name, q="sp"):
            P.dma(q, dst, src, writes=[B(name)])
        ld(identf[:], cst["identf"], "identf")
        ld(onesf[:], cst["onesf"], "onesf")
        ld(identb[:], cst["identf"], "identb", "pool")
        ld(onesb[:], cst["onesf"], "onesb", "pool")
        ld(tri[:], cst["tri"].rearrange("k s t -> s k t"), "tri")
        ld(maskneg[:], cst["maskneg"].rearrange("k s t -> s k t"), "maskneg")
        ld(bigmask[:], cst["bigmask"].rearrange("k s t -> s k t"), "bigmask")
        ld(sellast[:], cst["sellast"].rearrange("k s t -> s k t"), "sellast")
        ld(seqcol[:], cst["seqcol"], "seqcol")
        ld(seqcolb[:], cst["seqcol"], "seqcolb", "pool")
        ld(lastsel[:], cst["lastsel"], "lastsel")
        ld(selmod[0:17], cst["selmod"].rearrange("k r t -> r k t"), "selmod")
        ld(selmod[32:49], cst["selmod"].rearrange("k r t -> r k t"), "selmod")
        ld(bandcur[:].rearrange("p k w t -> p (k w) t"), cst["bandcur"].rearrange("k w s t -> s (k w) t"), "bandcur", "pool")
        ld(bandprev[:], cst["bandprev"].rearrange("w s t -> s w t"), "bandprev", "pool")
        ld(bandbuf[:].rearrange("p k w t -> p (k w) t"), cst["bandbuf"].rearrange("k w s t -> s (k w) t"), "bandbuf", "pool")
        ld(rc[:], cst["rc"].rearrange("k t w -> t k w"), "rc")
        ld(bsg[:].rearrange("p l k g -> p (l k) g"), bsgu.rearrange("l k p g -> p (l k) g"), "bsg")
        for l in range(2):
            ld(bifb[:, l, :], b_if[l].to_broadcast([128, 8]), "bifb")
        ld(nrmgT[:], norm_gT.rearrange("l p c -> p l c"), "nrmgT")
        ld(mngT[:], mnorm_gT.rearrange("l p c -> p l c"), "mngT")
        ld(pscT[:], pscaleT.rearrange("l p c -> p l c"), "pscT")
        ld(bmT[:], b_modT.rearrange("l p c -> p l c"), "bmT")
        P.emit("pool", lambda e: e.memset(epsc[:], EPS), writes=[B("epsc")])
        for l in range(2):
            P.emit("pool", lambda e, l=l: e.memset(Cst[l][:], 0.0), writes=[B("Cst%d" % l)])
            P.emit("pool", lambda e, l=l: e.memset(Cbf[l][:], 0.0), writes=[B("Cbf%d" % l)])
            P.emit("pool", lambda e, l=l: e.memset(nst[l][:], 0.0), writes=[B("nst%d" % l)])
            P.emit("pool", lambda e, l=l: e.memset(nbf[l][:], 0.0), writes=[B("nbf%d" % l)])
            P.emit("pool", lambda e, l=l: e.memset(mrep[l][:], 0.0), writes=[B("mrep%d" % l)])
            P.emit("pool", lambda e, l=l: e.memset(pprev[l][:], 0.0), writes=[B("pprev%d" % l)])

        wscr = nc.dram_tensor("wscr", [2, NBLK, 128, 8, 512], BF16, kind="Internal").ap()
        cur = {"l": None, "blk": 0, "first": True}

        def wload(src_ap, ncols, pre=False):
            i = nxt("w", NW)
            slot, bslot = wring[i], B("wr%d" % i)
            if cur["l"] is None:
                P.dma("pool", slot[:, :, 0:ncols], src_ap.rearrange("(k p) n -> p k n", p=128), writes=[bslot], dsem=NW + i)
                return slot, bslot
            l_, blk = cur["l"], cur["blk"]
            cur["blk"] += 1
            assert blk < NBLK
            bscr = B("wscr_%d_%d" % (l_, blk))
            if cur["first"]:
                src = src_ap if pre else src_ap.rearrange("(k p) n -> p k n", p=128)
                P.dma("pool", slot[:, :, 0:ncols], src, writes=[bslot], dsem=NW + i)
                P.dma("sp", wscr[l_, blk][:, :, 0:ncols], slot[:, :, 0:ncols], reads=[bslot], writes=[bscr])
            else:
                P.dma("sp", slot[:, :, 0:ncols], wscr[l_, blk][:, :, 0:ncols], reads=[bscr], writes=[bslot], dsem=i)
            return slot, bslot

        for l in range(2):
            for kind in range(2):
                for g in range(4):
                    t_ = tmpA[nxt("tmp", 2)]
                    bt = B(t_.name)
                    P.dma("sp", t_[:, 0:128], wsgu[l, kind, g], writes=[bt])
                    pi = nxt("pm", 3)
                    P.emit("pe", lambda e, pi=pi, t_=t_: e.transpose(out=pm[pi][:, 0:128], in_=t_[:, 0:128], identity=identf[:]), [bt, B("identf")], [Bpm[pi]])
                    tt(WTs[:, l, kind, g, :], pm[pi][:, 0:128], tri[:, kind, :], ALU.mult, [B("tri")], [B("WTs"), Bpm[pi]])

        ccs = tmpscr[0:17, :]
        ccb = xnbs[0][0:17, :]
        bg17 = F1[0:17, 0, :]
        P.dma("sp", ccs, cc, writes=[B("tmpscr")])
        act(ccb, ccs, AF.Silu, [B("tmpscr")], [B("xnb0")])

        def cctr(e):
            for c in range(8):
                ins = e.transpose(out=pt[0][:, c, 0:17], in_=ccb[:, c * 128:(c + 1) * 128], identity=identb[0:17, 0:17])
            return ins
        P.emit("pe", cctr, [B("xnb0"), B("identb")], [Bpt[0]])
        cp(ccT[:], pt[0][:, :, 0:17], [], [B("ccT"), Bpt[0]])
        for l in range(2):
            bg17l = F1[32 * l:32 * l + 17, 0, :]
            P.dma("sp", bg17l, b_modg[l].to_broadcast([17, D]), writes=[B("F1_0")])
            for blk in range(6):
                wt, bw = wload(w_mod[l][:, blk * 512:(blk + 1) * 512], 512)
                if blk < 4:
                    for j in range(4):
                        ch = blk * 4 + j
                        pi = nxt("pm", 3)
                        mmg([(pm[pi][:, 0:17], wt[:, k, j * 128:(j + 1) * 128], ccT[:, k, :], k == 0, k == 7) for k in range(8)],
                            [bw, B("ccT")], [Bpm[pi]])
                        if ch < 8:
                            ts(SH[:, l, ch, :], pm[pi][:, 0:17], bmT[:, l, ch:ch + 1], None, ALU.add, None, [B("bmT")], [B("SH"), Bpm[pi]])
                        else:
                            c8 = ch - 8
                            ts(AT[:, l, c8, :], pm[pi][:, 0:17], bmT[:, l, ch:ch + 1], 1.0, ALU.add, ALU.add, [B("bmT")], [B("AT"), Bpm[pi]])
                            ts(AT[:, l, c8, :], AT[:, l, c8, :], nrmgT[:, l, c8:c8 + 1], None, ALU.mult, None, [B("nrmgT"), B("AT")], [B("AT")])
                else:
                    cb = blk - 4
                    pi = nxt("pm", 3)
                    mmg([(pm[pi][32 * l:32 * l + 17, :], ccT[:, k, :], wt[:, k, :], k == 0, k == 7) for k in range(8)], [bw, B("ccT")], [Bpm[pi]])
                    tt(gate17[32 * l:32 * l + 17, cb * 512:(cb + 1) * 512], pm[pi][32 * l:32 * l + 17, :], bg17l[:, cb * 512:(cb + 1) * 512], ALU.add, [B("F1_0")], [B("gate17"), Bpm[pi]])

        def rms_stats(src, reads, slot):
            Br = B("rs%d" % slot)
            act(tmpscr[:], src, AF.Square, reads, [B("tmpscr"), Br], accum_out=rs[:, slot, 0:1])
            act(rs[:, slot, 1:2], rs[:, slot, 0:1], AF.Sqrt, [Br, B("epsc")], [Br], scale=1.0 / D, bias=epsc[:])
            P.emit("dve", lambda e: e.reciprocal(out=rs[:, slot, 2:3], in_=rs[:, slot, 1:2]), [Br], [Br])
            return rs[:, slot, 2:3], Br

        def pipe2(n, stA, stB):
            for i in range(n + 1):
                if i < n:
                    stA(i)
                if i >= 1:
                    stB(i - 1)

        def to_T(dst, gi, src_bf, src_bufs, scale_ap=None, scale_bufs=()):
            pi = nxt("pt", 2)
            transposes(pi, lambda c: src_bf[:, c * 128:(c + 1) * 128], 8, src_bufs)
            dv = dst[:, :, gi * 128:(gi + 1) * 128]
            dbuf = B("YT_%d" % gi)
            if scale_ap is None:
                cp(dv, pt[pi][:], [], [dbuf, Bpt[pi]])
            else:
                tt(dv, pt[pi][:], bcast3(scale_ap, 128), ALU.mult, list(scale_bufs), [dbuf, Bpt[pi]])
            return dbuf

        def make_h_A(l, gi, kind):
            xb_ = B("xg%d" % gi)
            rstd, Br = rms_stats(xg[:, gi, :], [xb_], gi)
            xnb, Bx = xnbs[gi % 2], B("xnb%d" % (gi % 2))
            ts(xnb[:], xg[:, gi, :], rstd, None, ALU.mult, None, [xb_, Br], [Bx])

        def make_h_B(l, gi, kind):
            xnb, Bx = xnbs[gi % 2], B("xnb%d" % (gi % 2))
            pi = nxt("pt", 2)
            transposes(pi, lambda c: xnb[:, c * 128:(c + 1) * 128], 8, [Bx])
            hb_ = B("hT_%d" % gi)
            if kind == 0:
                for c in range(4):
                    act(hT[:, c, gi * 128:(gi + 1) * 128], pt[pi][:, c, :], AF.Identity, [B("AT"), B("SH")], [hb_, Bpt[pi]],
                        scale=AT[:, l, c, 0:1], bias=SH[:, l, c, 0:1])
                t_ = tmpA[nxt("tmp", 2)]
                tv = t_[:].rearrange("p (c t) -> p c t", c=4)
                tt(tv, pt[pi][:, 4:8, :], AT[:, l, 4:8, 0:1].to_broadcast([128, 4, 128]), ALU.mult, [B("AT")], [B(t_.name), Bpt[pi]])
                tt(hT[:, 4:8, gi * 128:(gi + 1) * 128], tv, SH[:, l, 4:8, 0:1].to_broadcast([128, 4, 128]), ALU.add, [B("SH"), B(t_.name)], [hb_])
            else:
                for c in range(8):
                    t_ = tmpA[nxt("tmp", 2)]
                    tv = t_[:, 0:128].rearrange("p (b i) -> p b i", i=8)
                    tt(tv, pt[pi][:, c, :].rearrange("p (b i) -> p b i", i=8), bcast3(AT[:, l, c, 1:17], 8), ALU.mult, [B("AT")], [B(t_.name), Bpt[pi]])
                    tt(hT[:, c, gi * 128:(gi + 1) * 128].rearrange("p (b i) -> p b i", i=8), tv, bcast3(SH[:, l, c, 1:17], 8), ALU.add, [B("SH"), B(t_.name)], [hb_])

        def proj(l, col0, ncols, tiles_gi, evac, src=None, srcT=None, srcbufs=None):
            W = w_in[l] if src is None else src
            T_ = hT if srcT is None else srcT
            nblk = (ncols + 511) // 512
            for cb in range(nblk):
                n = min(512, ncols - cb * 512)
                wt, bw = wload(W[:, col0 + cb * 512: col0 + cb * 512 + n], n)
                for gi in tiles_gi:
                    pi = nxt("pm", 3)
                    tb = (B("hT_%d" % gi) if srcbufs is None else srcbufs[gi])
                    mmg([(pm[pi][:, 0:n], T_[:, k, gi * 128:(gi + 1) * 128], wt[:, k, 0:n], k == 0, k == 7) for k in range(8)],
                        [bw, tb], [Bpm[pi]])
                    evac(gi, cb, pm[pi], Bpm[pi], n)

        def mlstm_head(l, gi, kind):
            Bg = B("GP%d" % gi)
            h4 = lambda ap: ap.rearrange("p (h s) -> p h s", h=4)
            m4 = lambda m: m[:, kind, :].unsqueeze(1).to_broadcast([128, 4, 128])
            ig = GP[:, gi, 0:4]
            fg = GP[:, gi, 4:8]
            lsp, A_, CM = sm2[:, gi, 0:4], sm2[:, gi, 4:8], sm2[:, gi, 8:12]
            Bl, Ba, Bc = B("h_lsp%d" % gi), B("h_a%d" % gi), B("h_cm%d" % gi)
            psm, Bps = pa[2], Bpa[2]
            c0 = 32 * gi
            act(lsp, fg, AF.Exp, [Bg], [Bl], scale=-1.0)
            act(lsp, lsp, AF.Ln, [Bl], [Bl], bias=1.0)
            mmg([(psm[:, c0:c0 + 4], tri[:, kind, :], lsp, True, True), (psm[:, c0 + 4:c0 + 8], onesf[:], lsp, True, True)],
                [B("tri"), B("onesf"), Bl], [Bps])
            tt(A_, ig, psm[:, c0:c0 + 4], ALU.add, [Bg], [Ba, Bps])
            tt(h4(DG[:]), identf[:].unsqueeze(1).to_broadcast([128, 4, 128]), bcast3(A_, 128), ALU.mult, [B("identf"), Ba], [B("DG")])
            mmg([(pa[0][:], onesf[:], DG[:], True, True)], [B("onesf"), B("DG")], [Bpa[0]])
            tt(h4(AM[:]), h4(pa[0][:]), m4(maskneg), ALU.add, [B("maskneg")], [B("AM"), Bpa[0]])
            red(CM, h4(AM[:]), ALU.max, [B("AM")], [Bc])

        def mlstm_tile(l, gi, tile, kind):
            first = (tile == 1)
            Bq, Bk, Bv, Bg = B("QB%d" % gi), B("KB%d" % gi), B("VV%d" % gi), B("GP%d" % gi)
            S = lambda a, b_: sm[:, a:b_]
            Qs = xnbs[gi % 2]
            kw = ybfs[gi % 2]
            BQs, Bkw = B("xnb%d" % (gi % 2)), B("ybf%d" % (gi % 2))
            h4 = lambda ap: ap.rearrange("p (h s) -> p h s", h=4)
            m4 = lambda m: m[:, kind, :].unsqueeze(1).to_broadcast([128, 4, 128])
            A_, CM = sm2[:, gi, 4:8], sm2[:, gi, 8:12]
            Ba, Bc = B("h_a%d" % gi), B("h_cm%d" % gi)
            psm, Bps = pa[2], Bpa[2]
            c0 = 32 * gi
            use_inter = not (kind == 0 and first)
            if kind == 0:
                cp(S(48, 52), mrep[l][:], [B("mrep%d" % l)], [B("sm_mprev")])
            else:
                P.dma("sp", S(48, 52), m0tok[l], writes=[B("sm_mprev")])
            tt(S(12, 16), CM, S(48, 52), ALU.max, [Bc, B("sm_mprev")], [B("sm_g")])
            tt(S(20, 24), S(48, 52), S(12, 16), ALU.subtract, [B("sm_mprev"), B("sm_g")], [B("sm_winter")])
            act(S(20, 24), S(20, 24), AF.Exp, [B("sm_winter")], [B("sm_winter")])
            pi = nxt("pt", 2)
            transposes(pi, lambda c: QB[:, gi, c * 128:(c + 1) * 128], 8, [Bq])
            cp(QT[:], pt[pi][:], [], [B("QT"), Bpt[pi]])
            pi = nxt("pt", 2)
            transposes(pi, lambda c: KB[:, gi, c * 128:(c + 1) * 128], 8, [Bk])
            cp(KT[:], pt[pi][:], [], [B("KT"), Bpt[pi]], eng="act")
            mmg([(pm[2][:, h * 128:(h + 1) * 128], KT[:, 2 * h + hf, :], QT[:, 2 * h + hf, :], hf == 0, hf == 1) for h in range(4) for hf in range(2)],
                [B("KT"), B("QT")], [Bpm[2]])
            if use_inter:
                tt(Qs[:].rearrange("p (h d) -> p h d", h=4), QB[:, gi, :].rearrange("p (h d) -> p h d", h=4), bcast3(S(20, 24), 256), ALU.mult,
                   [Bq, B("sm_winter")], [BQs])
            tt(h4(DG[:]), identf[:].unsqueeze(1).to_broadcast([128, 4, 128]), bcast3(S(12, 16), 128), ALU.mult,
               [B("identf"), B("sm_g")], [B("DG")])
            if use_inter:
                pi = nxt("pt", 2)
                transposes(pi, lambda c: Qs[:, c * 128:(c + 1) * 128], 8, [BQs])
                cp(QsT[:], pt[pi][:], [], [B("QsT"), Bpt[pi]], eng="act")
            mms = []
            for h in range(4):
                mms.append((pa[1][:, h * 128:(h + 1) * 128], onesf[:], DG[:, h * 128:(h + 1) * 128], h == 0, False))
            for h in range(4):
                mms.append((pa[1][:, h * 128:(h + 1) * 128], identf[:], bigmask[:, kind, :], False, h == 3))
            mmg(mms, [B("onesf"), B("DG"), B("identf"), B("bigmask")], [Bpa[1]])
            for h in range(4):
                act(WTt[:, h * 128:(h + 1) * 128], pa[1][:, h * 128:(h + 1) * 128], AF.Exp, [Ba], [B("WTt"), Bpa[1]],
                    scale=-1.0, bias=sm2[:, gi, 4 + h:5 + h])
            tt(SWT[:], pm[2][:], WTt[:], ALU.mult, [B("WTt")], [B("SWT"), Bpm[2]])
            tt(h4(AM[:]), h4(pa[1][:]), m4(sellast), ALU.mult, [B("sellast")], [B("AM"), Bpa[1]])
            red(S(16, 20), h4(AM[:]), ALU.add, [B("AM")], [B("sm_glast")])
            if kind == 0:
                tt(mrep[l][:], S(16, 20), psm[:, c0 + 4:c0 + 8], ALU.subtract, [B("sm_glast"), B("sm_mprev")], [B("mrep%d" % l), Bps])
                tt(S(40, 44), S(48, 52), S(16, 20), ALU.subtract, [B("sm_mprev"), B("sm_glast")], [B("sm_decay")])
                act(S(40, 44), S(40, 44), AF.Exp, [B("sm_decay")], [B("sm_decay")])
                tt(S(44, 48), A_, S(16, 20), ALU.subtract, [Ba, B("sm_glast")], [B("sm_wend")])
                act(S(44, 48), S(44, 48), AF.Exp, [B("sm_wend")], [B("sm_wend")])
                tt(kw[:].rearrange("p (h d) -> p h d", h=4), KB[:, gi, :].rearrange("p (h d) -> p h d", h=4), bcast3(S(44, 48), 256), ALU.mult,
                   [Bk, B("sm_wend")], [Bkw])
            NB = [(pm[0], Bpm[0]), (pm[1], Bpm[1]), (pa[0], Bpa[0]), (pa[1], Bpa[1])]
            for h in range(4):
                bank, bbuf = NB[h]
                o = bank[:, 0:256]
                mms = []
                rds = [B("SWT"), Bv]
                if kind == 0 and use_inter:
                    mms.append((o, SWT[:, h * 128:(h + 1) * 128], VV[:, gi, h * 256:(h + 1) * 256], True, False))
                    mms.append((o, QsT[:, 2 * h, :], Cbf[l][:, 0, h, :], False, False))
                    mms.append((o, QsT[:, 2 * h + 1, :], Cbf[l][:, 1, h, :], False, True))
                    rds += [B("QsT"), B("Cbf%d" % l)]
                elif kind == 0:
                    mms.append((o, SWT[:, h * 128:(h + 1) * 128], VV[:, gi, h * 256:(h + 1) * 256], True, True))
                else:
                    mms.append((o, SWT[:, h * 128:(h + 1) * 128], VV[:, gi, h * 256:(h + 1) * 256], True, False))
                mmg(mms, rds, [bbuf])
            if kind == 1:
                tt(S(44, 48), A_, S(16, 20), ALU.subtract, [Ba, B("sm_glast")], [B("sm_wend")])
                act(S(44, 48), S(44, 48), AF.Exp, [B("sm_wend")], [B("sm_wend")])
                tt(kw[:].rearrange("p (h d) -> p h d", h=4), KB[:, gi, :].rearrange("p (h d) -> p h d", h=4), bcast3(S(44, 48), 256), ALU.mult,
                   [Bk, B("sm_wend")], [Bkw])
                tt(WL[:].rearrange("p (b h) -> p b h", h=4), lastsel[:].unsqueeze(2).to_broadcast([128, 16, 4]), S(20, 24).unsqueeze(1).to_broadcast([128, 16, 4]), ALU.mult,
                   [B("lastsel"), B("sm_winter")], [B("WL")])
                mmg([(pm[2][:, 0:64], onesf[:], WL[:], True, True)], [B("onesf"), B("WL")], [Bpm[2]])
                cp(DEC[:], pm[2][:, 0:64], [], [B("DEC"), Bpm[2]])
                for zi in range(2):
                    P.emit("pool", lambda e, zi=zi: e.memset(ZQ[zi][:], 0.0), [], [B("ZQ%d" % zi)])
                UB = [(pm[2], Bpm[2]), (ptf[0], Bpt[0]), (ptf[1], Bpt[1])]
                rnd = 0
                for b in range(16):
                    zi = b % 2
                    if b >= 2:
                        P.emit("pool", lambda e, zi=zi, b=b: e.memset(ZQ[zi][:, :, (b - 2) * 8:(b - 1) * 8], 0.0), [], [B("ZQ%d" % zi)])
                    cp(ZQ[zi][:, :, b * 8:(b + 1) * 8], QsT[:, :, b * 8:(b + 1) * 8], [B("QsT")], [B("ZQ%d" % zi)], eng="pool")
                    ts(KWb[zi][:], kw[:], seqcol[:, b:b + 1], None, ALU.mult, None, [Bkw, B("seqcol")], [B("KWb%d" % zi)])
                    for k in range(2):
                        fi = (2 * b + k) % len(C0f)
                        ci = (2 * b + k) % len(C0b)
                        si = (2 * b + k) % len(Cstage)
                        P.dma("pool", C0f[fi][:], C0[l, b][:, k * 128:(k + 1) * 128, :].rearrange("h p e -> p h e"), writes=[B("C0f%d" % fi)])
                        cp(C0b[ci][:], C0f[fi][:], [B("C0f%d" % fi)], [B("C0b%d" % ci)], eng="act")
                        for h in range(4):
                            bank, bbuf = NB[h]
                            mmg([(bank[:, 0:256], ZQ[zi][:, 2 * h + k, :], C0b[ci][:, h, :], False, (b == 15 and k == 1))],
                                [B("ZQ%d" % zi), B("C0b%d" % ci)], [bbuf])
                        for pr in range(2):
                            ub, ubuf = UB[rnd % 3]
                            rnd += 1
                            mms = []
                            for hh in range(2):
                                h = pr * 2 + hh
                                mms.append((ub[:, hh * 256:(hh + 1) * 256], KWb[zi][:, h * 256 + k * 128: h * 256 + k * 128 + 128], VV[:, gi, h * 256:(h + 1) * 256], True, True))
                            mmg(mms, [B("KWb%d" % zi), Bv], [ubuf])
                            for hh in range(2):
                                h = pr * 2 + hh
                                stt(Cstage[si][:, h, :], C0f[fi][:, h, :], DEC[:, b * 4 + h: b * 4 + h + 1], ub[:, hh * 256:(hh + 1) * 256], ALU.mult, ALU.add,
                                    [B("DEC"), B("C0f%d" % fi)], [B("Cstage%d" % si), ubuf])
                        P.dma("sp", oCs[l, b][:, k * 128:(k + 1) * 128, :].rearrange("h p e -> p h e"), Cstage[si][:], reads=[B("Cstage%d" % si)])
            mms = []
            rds = [B("SWT"), B("onesb")]
            for h in range(4):
                o = psm[:, 8 + h:9 + h]
                if kind == 0 and use_inter:
                    mms.append((o, SWT[:, h * 128:(h + 1) * 128], onesb[:, 0:1], True, False))
                    mms.append((o, QsT[:, 2 * h, :], nbf[l][:, 0, h:h + 1], False, False))
                    mms.append((o, QsT[:, 2 * h + 1, :], nbf[l][:, 1, h:h + 1], False, True))
                    rds += [B("QsT"), B("nbf%d" % l)]
                else:
                    mms.append((o, SWT[:, h * 128:(h + 1) * 128], onesb[:, 0:1], True, True))
            mmg(mms, rds, [Bps])
            cp(S(32, 36), psm[:, 8:12], [], [B("sm_den"), Bps])
            if kind == 1:
                P.dma("sp", tmpscr[:], n0tok[l], writes=[B("tmpscr")])
                tt(tmpscr[:], tmpscr[:], QB[:, gi, :], ALU.mult, [Bq, B("tmpscr")], [B("tmpscr")])
                red(S(52, 56), tmpscr[:].rearrange("p (h d) -> p h d", h=4), ALU.add, [B("tmpscr")], [B("sm_qn")])
                tt(S(52, 56), S(52, 56), S(20, 24), ALU.mult, [B("sm_qn"), B("sm_winter")], [B("sm_qn")])
                tt(S(32, 36), S(32, 36), S(52, 56), ALU.add, [B("sm_den"), B("sm_qn")], [B("sm_den")])
            tt(S(24, 28), S(12, 16), psm[:, c0:c0 + 4], ALU.subtract, [B("sm_g")], [B("sm_mt"), Bps])
            act(S(28, 32), S(24, 28), AF.Exp, [B("sm_mt")], [B("sm_emt")], scale=-1.0)
            stt(S(36, 40), S(32, 36), -1.0, S(32, 36), ALU.mult, ALU.max, [B("sm_den")], [B("sm_r")])
            tt(S(36, 40), S(36, 40), S(28, 32), ALU.max, [B("sm_r"), B("sm_emt")], [B("sm_r")])
            P.emit("dve", lambda e: e.reciprocal(out=S(36, 40), in_=S(36, 40)), [B("sm_r")], [B("sm_r")])
            Bf = B("F1_%d" % gi)
            for h in range(4):
                bank, bbuf = NB[h]
                act(F1[:, gi, h * 256:(h + 1) * 256], bank[:, 0:256], AF.Copy, [B("sm_r")], [Bf, bbuf], scale=S(36 + h, 37 + h))
            if kind == 0:
                UBk = [(ptf[0], Bpt[0]), (ptf[1], Bpt[1]), (pm[2], Bpm[2])]
                rnd = 0
                for hf in range(2):
                    for pr in range(2):
                        ub, ubuf = UBk[rnd % 3]
                        rnd += 1
                        mms = []
                        for hh in range(2):
                            h = pr * 2 + hh
                            mms.append((ub[:, hh * 256:(hh + 1) * 256], kw[:, h * 256 + hf * 128: h * 256 + hf * 128 + 128], VV[:, gi, h * 256:(h + 1) * 256], True, True))
                        mmg(mms, [Bkw, Bv], [ubuf])
                        for hh in range(2):
                            h = pr * 2 + hh
                            stt(Cst[l][:, hf, h, :], Cst[l][:, hf, h, :], S(40 + h, 41 + h), ub[:, hh * 256:(hh + 1) * 256], ALU.mult, ALU.add,
                                [B("sm_decay"), B("Cst%d" % l)], [B("Cst%d" % l), ubuf])
                cp(Cbf[l][:], Cst[l][:], [B("Cst%d" % l)], [B("Cbf%d" % l)], eng="pool")
                mmg([(psm[:, 16 + hf * 4 + h: 17 + hf * 4 + h], kw[:, h * 256 + hf * 128: h * 256 + hf * 128 + 128], onesb[:, 0:1], True, True)
                     for hf in range(2) for h in range(4)], [Bkw, B("onesb")], [Bps])
                tt(nst[l][:], nst[l][:], S(40, 44).unsqueeze(1).to_broadcast([128, 2, 4]), ALU.mult, [B("nst%d" % l), B("sm_decay")], [B("nst%d" % l)])
                tt(nst[l][:], nst[l][:], psm[:, 16:24].rearrange("p (k h) -> p k h", k=2), ALU.add, [B("nst%d" % l)], [B("nst%d" % l), Bps])
                cp(nbf[l][:], nst[l][:], [B("nst%d" % l)], [B("nbf%d" % l)])
            else:
                mmg([(pa[0][0:16, 0:4], lastsel[:], S(20, 24), True, True), (pa[0][0:16, 4:8], lastsel[:], S(24, 28), True, True)],
                    [B("lastsel"), B("sm_winter"), B("sm_mt")], [Bpa[0]])
                cp(s16[:, 0:8], pa[0][0:16, 0:8], [], [B("s16"), Bpa[0]])
                P.dma("sp", oms[l], s16[:, 4:8], reads=[B("s16")])
                n16 = tmpscr[0:16, :]
                P.dma("sp", n16, n0[l], writes=[B("tmpscr")])
                tt(n16.rearrange("p (h d) -> p h d", h=4), n16.rearrange("p (h d) -> p h d", h=4), s16[:, 0:4].unsqueeze(2).to_broadcast([16, 4, 256]), ALU.mult,
                   [B("tmpscr"), B("s16")], [B("tmpscr")])
                for cb in range(2):
                    mmg([(pa[1][0:16, :], seqcolb[:], kw[:, cb * 512:(cb + 1) * 512], True, True)], [B("seqcolb"), Bkw], [Bpa[1]])
                    tt(n16[:, cb * 512:(cb + 1) * 512], n16[:, cb * 512:(cb + 1) * 512], pa[1][0:16, :], ALU.add, [B("tmpscr")], [B("tmpscr"), Bpa[1]])
                P.dma("sp", ons[l], n16, reads=[B("tmpscr")])

        def layer(l, tiles, kind):
            G = len(tiles)
            gis = list(range(G))
            cur["l"] = l
            cur["blk"] = 0
            cur["first"] = (tiles[0] == 1)
            P.dma("sp", lng[:], sgu_ln_g[l].to_broadcast([128, D]), writes=[B("lng")])
            P.dma("sp", lnb[:], sgu_ln_b[l].to_broadcast([128, D]), writes=[B("lnb")])
            for cb in range(2):
                pi = nxt("pm", 3)
                mmg([(pm[pi][:], selmod[32 * l:32 * l + 17, kind, :], gate17[32 * l:32 * l + 17, cb * 512:(cb + 1) * 512], True, True)], [B("selmod"), B("gate17")], [Bpm[pi]])
                cp(gbc[:, cb * 512:(cb + 1) * 512], pm[pi][:], [], [B("gbc"), Bpm[pi]], eng="act")
            pipe2(G, lambda gi: make_h_A(l, gi, kind), lambda gi: make_h_B(l, gi, kind))
            def ev_va(gi, cb, bank, bb, n):
                act(F1[:, gi, cb * 512:(cb + 1) * 512], bank[:], AF.Gelu_apprx_tanh, [], [B("F1_%d" % gi), bb])
            proj(l, 4096, 1024, gis, ev_va)
            for gi in gis:
                Bf = B("F1_%d" % gi)
                P.emit("dve", lambda e, gi=gi: e.bn_stats(out=lnst[:, 2 * gi, :], in_=F1[:, gi, 0:512]), [Bf], [B("lnst")])
                P.emit("dve", lambda e, gi=gi: e.bn_stats(out=lnst[:, 2 * gi + 1, :], in_=F1[:, gi, 512:1024]), [Bf], [B("lnst")])
                P.emit("dve", lambda e, gi=gi: e.bn_aggr(out=lnmv[:, gi, :], in_=lnst[:, 2 * gi:2 * gi + 2, :]), [B("lnst")], [B("lnmv")])
            act(lnrs[:, 0:G], lnmv[:, 0:G, 1], AF.Sqrt, [B("lnmv"), B("epsc")], [B("lnrs")], bias=epsc[:])
            P.emit("dve", lambda e: e.reciprocal(out=lnrs[:, 0:G], in_=lnrs[:, 0:G]), [B("lnrs")], [B("lnrs")])
            def sgu_A(gi):
                Bf = B("F1_%d" % gi)
                ts(F1[:, gi, :], F1[:, gi, :], lnmv[:, gi, 0:1], lnrs[:, gi:gi + 1], ALU.subtract, ALU.mult, [Bf, B("lnmv"), B("lnrs")], [Bf])
                tt(F1[:, gi, :], F1[:, gi, :], lng[:], ALU.mult, [Bf, B("lng")], [Bf])
                tt(F1[:, gi, :], F1[:, gi, :], lnb[:], ALU.add, [Bf, B("lnb")], [Bf])
                if kind == 1:
                    P.dma("sp", ovs[l], F1[:, gi, :], reads=[Bf])
                cp(VV[:, gi, :], F1[:, gi, :], [Bf], [B("VV%d" % gi)], eng="act")

            def sgu_B(gi):
                Bf = B("F1_%d" % gi)
                for cb in range(2):
                    pi = nxt("pm", 3)
                    mmg([(pm[pi][:, gg * 256:(gg + 1) * 256], WTs[:, l, kind, cb * 2 + gg, :], VV[:, gi, (cb * 2 + gg) * 256:(cb * 2 + gg + 1) * 256], True, True) for gg in range(2)],
                        [B("WTs"), B("VV%d" % gi)], [Bpm[pi]])
                    for gg in range(2):
                        g4 = cb * 2 + gg
                        act(F1[:, gi, g4 * 256:(g4 + 1) * 256], pm[pi][:, gg * 256:(gg + 1) * 256], AF.Identity, [B("bsg")], [Bf, Bpm[pi]],
                            bias=bsg[:, l, kind, g4:g4 + 1])

            def ev_q(gi, cb, bank, bb, n):
                cp(QB[:, gi, cb * 512:(cb + 1) * 512], bank[:], [], [B("QB%d" % gi), bb], eng="act")

            def ev_k(gi, cb, bank, bb, n):
                act(KB[:, gi, cb * 512:(cb + 1) * 512], bank[:], AF.Copy, [], [B("KB%d" % gi), bb], scale=1.0 / 16.0)
            steps = []
            for i in range(G + 1):
                if i < G:
                    steps.append(lambda i=i: sgu_A(i))
                if i >= 1:
                    steps.append(lambda i=i: sgu_B(i - 1))
            qpos, kpos = min(2, len(steps) - 1), min(5, len(steps))
            for j, st in enumerate(steps):
                if j == qpos:
                    proj(l, 6144, 1024, gis, ev_q)
                if j == kpos:
                    proj(l, 7168, 1024, gis, ev_k)
                st()
            if kpos >= len(steps):
                proj(l, 7168, 1024, gis, ev_k)

            def ev_mul(func):
                def ev(gi, cb, bank, bb, n):
                    t_ = tmpA[nxt("tmp", 2)]
                    act(t_[:], bank[:], func, [], [B(t_.name), bb])
                    tt(F1[:, gi, cb * 512:(cb + 1) * 512], F1[:, gi, cb * 512:(cb + 1) * 512], t_[:], ALU.mult, [B(t_.name), B("F1_%d" % gi)], [B("F1_%d" % gi)])
                return ev
            proj(l, 3072, 1024, gis, ev_mul(AF.Gelu_apprx_tanh))
            proj(l, 5120, 1024, gis, ev_mul(AF.Silu))

            def gate_proj(bi, cb):
                def ev_gate(gi, cb_, bank, bb, n):
                    act(GT[:, gi, :], bank[:], AF.Sigmoid, [], [B("GT%d" % gi), bb])
                proj(l, bi * 1024 + cb * 512, 512, gis, ev_gate)

            def branch_out(bi, scaleT=None, scale_bufs=(), gate0_done=False):
                ybufs = {}
                for gi in gis:
                    ybf, By = ybfs[gi % 2], B("ybf%d" % (gi % 2))
                    cp(ybf[:], F1[:, gi, :], [B("F1_%d" % gi)], [By], eng="act")
                    ybufs[gi] = to_T(YT, gi, ybf, [By], scaleT, scale_bufs)
                for cb in range(2):
                    if not (cb == 0 and gate0_done):
                        gate_proj(bi, cb)

                    def ev_br(gi, cb_, bank, bb, n, cb=cb):
                        mgv = MG[:, gi, cb * 512:(cb + 1) * 512]
                        if bi == 0:
                            tt(mgv, bank[:], GT[:, gi, :], ALU.mult, [B("GT%d" % gi)], [B("MG%d" % gi), bb])
                        else:
                            t_ = tmpA[nxt("tmp", 2)]
                            tt(t_[:], bank[:], GT[:, gi, :], ALU.mult, [B("GT%d" % gi)], [B(t_.name), bb])
                            tt(mgv, mgv, t_[:], ALU.add, [B(t_.name), B("MG%d" % gi)], [B("MG%d" % gi)])
                    proj(l, cb * 512, 512, gis, ev_br, src=w_br[bi][l], srcT=YT, srcbufs=ybufs)
            branch_out(0)

            def ev_v(gi, cb, bank, bb, n):
                cp(VV[:, gi, cb * 512:(cb + 1) * 512], bank[:], [], [B("VV%d" % gi), bb])

            def ev_g(gi, cb, bank, bb, n):
                tt(GP[:, gi, :], bank[:, 0:8], bifb[:, l, :], ALU.add, [B("bifb")], [B("GP%d" % gi), bb])
            proj(l, 8192, 1024, gis, ev_v)
            proj(l, 11264, 8, gis, ev_g)
            for gi in gis:
                mlstm_head(l, gi, kind)
            for gi in gis:
                mlstm_tile(l, gi, tiles[gi], kind)
            proj(l, 9216, 1024, gis, ev_mul(AF.Sigmoid))
            gate_proj(1, 0)
            for gi in gis:
                Bf = B("F1_%d" % gi)
                for h in range(4):
                    u = gi * 4 + h
                    P.emit("dve", lambda e, gi=gi, h=h, u=u: e.bn_stats(out=lnst[:, u, :], in_=F1[:, gi, h * 256:(h + 1) * 256]), [Bf], [B("lnst")])
                    P.emit("dve", lambda e, u=u: e.bn_aggr(out=lnmv[:, u, :], in_=lnst[:, u, :]), [B("lnst")], [B("lnmv")])
            act(lnrs[:, 0:4 * G], lnmv[:, 0:4 * G, 1], AF.Sqrt, [B("lnmv"), B("epsc")], [B("lnrs")], bias=epsc[:])
            P.emit("dve", lambda e: e.reciprocal(out=lnrs[:, 0:4 * G], in_=lnrs[:, 0:4 * G]), [B("lnrs")], [B("lnrs")])
            for gi in gis:
                Bf = B("F1_%d" % gi)
                for h in range(4):
                    u = gi * 4 + h
                    ts(F1[:, gi, h * 256:(h + 1) * 256], F1[:, gi, h * 256:(h + 1) * 256], lnmv[:, u, 0:1], lnrs[:, u:u + 1], ALU.subtract, ALU.mult,
                       [Bf, B("lnmv"), B("lnrs")], [Bf])
            proj(l, 10240, 1024, gis, ev_mul(AF.Silu))
            branch_out(1, mngT[:, l, :], [B("mngT")], gate0_done=True)

            def ev_p(gi, cb, bank, bb, n):
                cp(F1[:, gi, cb * 512:(cb + 1) * 512], bank[:], [], [B("F1_%d" % gi), bb], eng="act")
            proj(l, 11272, 1024, gis, ev_p)
            wpl, Bwpl = wload(w_pool[l].rearrange("g (k p) o -> p (g k) o", p=128), 256, pre=True)
            gate_proj(2, 0)
            if kind == 1:
                P.dma("pool", pbufA[0:120, :], pool0[l, 0], writes=[B("pbufA")])
                P.dma("pool", pbufB[0:120, :], pool0[l, 1], writes=[B("pbufB")])
            def pool_A(gi):
                Bf = B("F1_%d" % gi)
                tile = tiles[gi]
                ybf, By = ybfs[gi % 2], B("ybf%d" % (gi % 2))
                xnb, Bx = xnbs[gi % 2], B("xnb%d" % (gi % 2))
                QTp, Bqt = (QT, B("QT")) if gi % 2 == 0 else (KT, B("KT"))
                if gi == 0:
                    prv, Bprv = pprev[l], B("pprev%d" % l)
                else:
                    prv, Bprv = ybfs[(gi - 1) % 2], B("ybf%d" % ((gi - 1) % 2))
                cp(ybf[:], F1[:, gi, :], [Bf], [By], eng="act")
                if kind == 1:
                    for b in range(16):
                        P.dma("sp", obs[l, b, 7:15, :], F1[b * 8:(b + 1) * 8, gi, :], reads=[Bf])
                    P.dma("sp", obs[l][:, 0:7, :], pool0raw[l][:, 8:15, :])
                elif tile == 16:
                    P.dma("sp", obp[l], F1[113:128, gi, :], reads=[Bf])
                rci = 0 if kind == 1 else (1 if tile == 1 else 2)
                for cb in range(2):
                    pi = nxt("pm", 3)
                    mms = []
                    rds = [B("bandcur"), By]
                    for gg in range(2):
                        wi = cb * 2 + gg
                        cs = slice(wi * 256, (wi + 1) * 256)
                        o = pm[pi][:, gg * 256:(gg + 1) * 256]
                        if kind == 0:
                            if tile == 1:
                                mms.append((o, bandcur[:, 0, wi, :], ybf[:, cs], True, True))
                            else:
                                mms.append((o, bandcur[:, 0, wi, :], ybf[:, cs], True, False))
                                mms.append((o, bandprev[:, wi, :], prv[:, cs], False, True))
                                rds += [B("bandprev"), Bprv]
                        else:
                            mms.append((o, bandcur[:, 1, wi, :], ybf[:, cs], True, False))
                            mms.append((o, bandbuf[0:120, 0, wi, :], pbufA[0:120, cs], False, False))
                            mms.append((o, bandbuf[0:120, 1, wi, :], pbufB[0:120, cs], False, True))
                            rds += [B("bandbuf"), B("pbufA"), B("pbufB")]
                    mmg(mms, rds, [Bpm[pi]])
                    for gg in range(2):
                        wi = cb * 2 + gg
                        cs = slice(wi * 256, (wi + 1) * 256)
                        stt(xnb[:, cs], pm[pi][:, gg * 256:(gg + 1) * 256], rc[:, rci, wi:wi + 1], F1[:, gi, cs], ALU.mult, ALU.subtract,
                            [B("rc"), Bf], [Bx, Bpm[pi]])

            def pool_B(gi):
                Bf = B("F1_%d" % gi)
                tile = tiles[gi]
                ybf, By = ybfs[gi % 2], B("ybf%d" % (gi % 2))
                xnb, Bx = xnbs[gi % 2], B("xnb%d" % (gi % 2))
                QTp, Bqt = (QT, B("QT")) if gi % 2 == 0 else (KT, B("KT"))
                if kind == 0 and gi == G - 1:
                    cp(pprev[l][:], ybf[:], [By], [B("pprev%d" % l)], eng="act")
                pi = nxt("pt", 2)
                transposes(pi, lambda c, xnb=xnb: xnb[:, c * 128:(c + 1) * 128], 8, [Bx])
                cp(QTp[:], pt[pi][:], [], [Bqt, Bpt[pi]])
                for cb in range(2):
                    pj = nxt("pm", 3)
                    mmg([(pm[pj][:, gg * 256:(gg + 1) * 256], QTp[:, (cb * 2 + gg) * 2 + k, :], wpl[:, (cb * 2 + gg) * 2 + k, 0:256], k == 0, k == 1) for gg in range(2) for k in range(2)],
                        [Bqt, Bwpl], [Bpm[pj]])
                    cp(F1[:, gi, cb * 512:(cb + 1) * 512], pm[pj][:], [], [Bf, Bpm[pj]], eng="act")
            pipe2(G, pool_A, pool_B)
            proj(l, 12296, 1024, gis, ev_mul(AF.Silu))
            branch_out(2, pscT[:, l, :], [B("pscT")], gate0_done=True)

            mbufs = {}
            for gi in gis:
                ybf, By = ybfs[gi % 2], B("ybf%d" % (gi % 2))
                cp(ybf[:], MG[:, gi, :], [B("MG%d" % gi)], [By], eng="act")
                mbufs[gi] = to_T(YT, gi, ybf, [By])

            def ev_out(gi, cb, bank, bb, n):
                t_ = tmpA[nxt("tmp", 2)]
                tt(t_[:], bank[:], gbc[:, cb * 512:(cb + 1) * 512], ALU.mult, [B("gbc")], [B(t_.name), bb])
                xv = xg[:, gi, cb * 512:(cb + 1) * 512]
                tt(xv, xv, t_[:], ALU.add, [B(t_.name), B("xg%d" % gi)], [B("xg%d" % gi)])
            proj(l, 0, 1024, gis, ev_out, src=w_out[l], srcT=YT, srcbufs=mbufs)

        def prompt_state_out():
            for l in range(2):
                for k in range(2):
                    P.dma("sp", oCp[l][:, k * 128:(k + 1) * 128, :].rearrange("h p e -> p h e"), Cst[l][:, k], reads=[B("Cst%d" % l)])
                    P.dma("sp", onp[l][:, k * 128:(k + 1) * 128].rearrange("h p -> p h"), nst[l][:, k, :], reads=[B("nst%d" % l)], allow_slow_non_contiguous=True)
                P.dma("sp", omp[l:l + 1, :], mrep[l][0:1, :], reads=[B("mrep%d" % l)])
            fb = ["Cst0", "Cst1", "Cbf0", "Cbf1", "pprev0", "pprev1", "QB1", "QB2", "ZQ0", "ZQ1", "KWb0", "KWb1", "C0f0", "C0f1", "C0f2", "C0f3",
                  "C0b0", "C0b1", "C0b2", "Cstage0", "Cstage1", "Cstage2", "pbufA", "pbufB", "xg1", "xg2", "xg3", "F1_1", "F1_2", "F1_3", "MG1", "VV1", "VV2", "VV3"]
            P.emit("pool", lambda e: e.memset(fence_t[:], 0.0), [], [B(n) for n in fb])

        for tiles in GROUPS:
            kind = 1 if tiles[0] == 0 else 0
            if kind == 1:
                prompt_state_out()
            for gi, tile in enumerate(tiles):
                P.dma("sp", xg[:, gi, :], xin[tile], writes=[B("xg%d" % gi)])
            for l in range(2):
                layer(l, tiles, kind)
            P.dma("sp", lng[:], fin_g.to_broadcast([128, D]), writes=[B("lng")])
            for gi, tile in enumerate(tiles):
                xb_ = B("xg%d" % gi)
                rstd, Br = rms_stats(xg[:, gi, :], [xb_], gi)
                stt(tmpscr[:], xg[:, gi, :], rstd, lng[:], ALU.mult, ALU.mult, [xb_, Br, B("lng")], [B("tmpscr")])
                P.dma("sp", y[tile], tmpscr[:], reads=[B("tmpscr")])
        P.finish()
        with nc.Block() as block:
            P.replay(block)
    return nc


def _host_inputs(inp, core):
    f = lambda a: np.ascontiguousarray(a, dtype=np.float32)
    bs = slice(core * 16, core * 16 + 16)
    m = {}
    xs = inp["x_sample"][bs].reshape(1, 128, D)
    xp = inp["x_prompt"][core].reshape(16, 128, D)
    m["xin"] = f(np.concatenate([xs, xp], axis=0))
    m["cc"] = f(np.concatenate([inp["c_prompt"][core:core + 1], inp["c_sample"][bs]], axis=0))
    m["w_mod"] = f(inp["w_mod"])
    bm = np.asarray(inp["b_mod"])
    m["b_modT"] = f(bm[:, :2048].reshape(2, 16, 128).transpose(0, 2, 1))
    m["b_modg"] = f(bm[:, 2048:].reshape(2, 1, D))
    tT = lambda a: f(np.asarray(a).reshape(2, 8, 128).transpose(0, 2, 1))
    m["norm_gT"] = tT(inp["norm_g"])
    m["w_in"] = f(inp["w_in"])
    m["b_if"] = f(np.asarray(inp["b_if"]).reshape(2, 1, 8))
    m["sgu_ln_g"] = f(np.asarray(inp["sgu_ln_g"]).reshape(2, 1, D))
    m["sgu_ln_b"] = f(np.asarray(inp["sgu_ln_b"]).reshape(2, 1, D))
    ws = np.asarray(inp["w_sgu"])
    wsg = np.zeros((2, 2, 4, 128, 128), np.float32)
    wsg[:, 0] = ws
    for b in range(16):
        wsg[:, 1, :, b * 8:(b + 1) * 8, b * 8:(b + 1) * 8] = ws[:, :, :8, :8]
    m["wsgu"] = wsg
    bsu = np.asarray(inp["b_sgu"])
    bsg = np.zeros((2, 2, 128, 4), np.float32)
    bsg[:, 0] = bsu.transpose(0, 2, 1)
    bsg[:, 1] = np.tile(bsu[:, :, :8], (1, 1, 16)).transpose(0, 2, 1)
    m["bsgu"] = bsg
    m["mnorm_gT"] = tT(inp["mlstm_norm_g"])
    m["w_pool"] = f(inp["w_pool"])
    m["pscaleT"] = tT(inp["pool_scale"])
    m["w_br_a"] = f(inp["w_br_a"])
    m["w_br_b"] = f(inp["w_br_b"])
    m["w_br_c"] = f(inp["w_br_c"])
    m["w_out"] = f(inp["w_out"])
    m["fin_g"] = f(np.asarray(inp["final_norm_g"]).reshape(1, D))
    m["C0"] = f(inp["state_mlstm_C"][:, bs])
    n0 = np.asarray(inp["state_mlstm_n"])[:, bs].reshape(2, 16, D)
    m["n0"] = f(n0)
    m["n0tok"] = f(np.repeat(n0, 8, axis=1))
    m["m0tok"] = f(np.repeat(np.asarray(inp["state_mlstm_m"])[:, bs], 8, axis=1))
    sp = np.asarray(inp["state_pool"])[:, bs]
    m["pool0"] = f(sp.reshape(2, 2, 120, D))
    m["pool0raw"] = f(sp)
    global _CONSTS
    if _CONSTS is None:
        _CONSTS = make_consts()
    for k, v in _CONSTS.items():
        m["c_" + k] = v
    return m


_NC = None


def kernel(**inputs):
    global _NC
    inp = {k: np.asarray(v) for k, v in inputs.items()}
    if _NC is None:
        _NC = build_nc()
    in_maps = [_host_inputs(inp, c) for c in range(8)]
    res = run_bass_kernel_spmd(_NC, in_maps, core_ids=list(range(8)))
    R = res.results
    y_prompt = np.stack([r["y"][1:].reshape(2048, D) for r in R])
    y_sample = np.concatenate([r["y"][0].reshape(16, 8, D) for r in R], axis=0)
    Cp = np.stack([r["oCp"] for r in R], axis=1)
    npp = np.stack([r["onp"] for r in R], axis=1)
    mp = np.stack([r["omp"] for r in R], axis=1)
    bp = np.stack([r["obp"] for r in R], axis=1)
    Cs = np.concatenate([r["oCs"] for r in R], axis=1)
    ns = np.concatenate([r["ons"].reshape(2, 16, 4, 256) for r in R], axis=1)
    ms = np.concatenate([r["oms"] for r in R], axis=1)
    bsn = np.concatenate([r["obs"] for r in R], axis=1)
    vs = np.concatenate([r["ovs"].reshape(2, 16, 8, D) for r in R], axis=1)
    outs = (y_prompt, y_sample, Cp, npp, mp, bp, Cs, ns, ms, bsn, vs)
    return tuple(np.ascontiguousarray(o, dtype=np.float32) for o in outs)
```

```python
import numpy as np
import concourse.bass as bass
import concourse.mybir as mybir
from concourse.bass_utils import run_bass_kernel_spmd
from contextlib import ExitStack

F32 = mybir.dt.float32
BF16 = mybir.dt.bfloat16
AF = mybir.ActivationFunctionType
ALU = mybir.AluOpType
AX = mybir.AxisListType

D = 1024
NT = 17
NIN = 13320
EPS = 1e-6
WINS = (2, 4, 8, 16)
GROUPS = [[4 * i + 1, 4 * i + 2, 4 * i + 3, 4 * i + 4] for i in range(4)] + [[0]]
GMAX = 4
NW = 2
NBLK = 40


class Buf:
    __slots__ = ("name", "w", "r")

    def __init__(self, name):
        self.name = name
        self.w = None
        self.r = {}


class Prog:
    ENGS = ("pe", "act", "dve", "pool", "sp")

    def __init__(self, nc, es, n_dsem=24, n_fixed=12, n_sw=8):
        self.nc = nc
        self.ops = {e: [] for e in self.ENGS}
        self.sem = {e: es.enter_context(nc.semaphore("s_" + e)) for e in self.ENGS}
        self.cnt = {e: 0 for e in self.ENGS}
        self.waited = {e: {} for e in self.ENGS}
        self.dsem = [es.enter_context(nc.semaphore("d%d" % i)) for i in range(n_dsem + n_fixed + n_sw)]
        self.dcnt = [0] * (n_dsem + n_fixed + n_sw)
        self.drr = 0
        self.swrr = 0
        self.n_rr = n_dsem
        self.n_fixed = n_fixed
        self.n_sw = n_sw
        self.bufs = {}

    def buf(self, name):
        b = self.bufs.get(name)
        if b is None:
            b = Buf(name)
            self.bufs[name] = b
        return b

    def _semof(self, k):
        return self.sem[k] if isinstance(k, str) else self.dsem[k[1]]

    def _deps(self, eng, reads, writes, extra=()):
        deps = {}

        def add(k, v):
            if deps.get(k, 0) < v:
                deps[k] = v
        for b in reads:
            if b.w is not None:
                add(*b.w)
        for b in writes:
            if b.w is not None:
                add(*b.w)
            for k, v in b.r.items():
                add(k, v)
        for k, v in extra:
            add(k, v)
        waits = []
        wd = self.waited[eng]
        for k, v in deps.items():
            if wd.get(k, 0) >= v:
                continue
            wd[k] = v
            waits.append((k, v))
        return waits

    def _mark(self, tok, reads, writes):
        k, v = tok
        for b in reads:
            if b.r.get(k, 0) < v:
                b.r[k] = v
        for b in writes:
            b.w = tok
            b.r = {}

    def emit(self, eng, fn, reads=(), writes=()):
        waits = self._deps(eng, reads, writes)
        self.cnt[eng] += 1
        tok = (eng, self.cnt[eng])
        self._mark(tok, reads, writes)
        self.ops[eng].append((waits, fn, None))

    def dma(self, q, out, in_, reads=(), writes=(), dsem=None, **kw):
        if dsem is not None:
            i = self.n_rr + dsem
        elif q == "pool":
            i = self.n_rr + self.n_fixed + self.swrr
            self.swrr = (self.swrr + 1) % self.n_sw
        else:
            i = self.drr
            self.drr = (self.drr + 1) % self.n_rr
        extra = [(("d", i), self.dcnt[i])] if self.dcnt[i] > 0 else []
        waits = self._deps(q, reads, writes, extra)
        self.dcnt[i] += 16
        tok = (("d", i), self.dcnt[i])
        self._mark(tok, reads, writes)

        def fn(e, out=out, in_=in_, kw=kw):
            return e.dma_start(out=out, in_=in_, **kw)
        self.ops[q].append((waits, fn, i))

    def finish(self):
        waits = [(("d", i), c) for i, c in enumerate(self.dcnt) if c > 0]
        waits += [(e, self.cnt[e]) for e in self.ENGS if e != "sp" and self.cnt[e] > 0]
        self.ops["sp"].append((waits, None, None))

    def replay(self, block):
        prog = self

        def run(ename, e):
            for waits, fn, di in prog.ops[ename]:
                for k, v in waits:
                    e.wait_ge(prog._semof(k), v)
                if fn is None:
                    continue
                ins = fn(e)
                if di is None:
                    ins.then_inc(prog.sem[ename], 1)
                else:
                    ins.then_inc(prog.dsem[di], 16)

        @block.tensor
        def _(e):
            run("pe", e)

        @block.scalar
        def _(e):
            run("act", e)

        @block.vector
        def _(e):
            run("dve", e)

        @block.gpsimd
        def _(e):
            run("pool", e)

        @block.sync
        def _(e):
            run("sp", e)


def make_consts():
    t = np.arange(128)
    c = {}
    c["identf"] = np.eye(128, dtype=np.float32)
    c["onesf"] = np.ones((128, 128), np.float32)
    seq = t // 8
    valid = []
    for kind in range(2):
        v = (t[:, None] <= t[None, :])
        if kind == 1:
            v = v & (seq[:, None] == seq[None, :])
        valid.append(v)
    c["tri"] = np.stack([v.astype(np.float32) for v in valid])
    c["maskneg"] = np.stack([np.where(v.T, 0.0, -1e30).astype(np.float32) for v in valid])
    c["bigmask"] = np.stack([np.where(v, 0.0, 1e30).astype(np.float32) for v in valid])
    last = [np.full(128, 127), seq * 8 + 7]
    c["sellast"] = np.stack([(t[None, :] == last[k][:, None]).astype(np.float32) for k in range(2)])
    c["seqcol"] = (seq[:, None] == np.arange(16)[None, :]).astype(np.float32)
    c["lastsel"] = (t[:, None] == (np.arange(16) * 8 + 7)[None, :]).astype(np.float32)
    selmod = np.zeros((2, 17, 128), np.float32)
    selmod[0, 0, :] = 1.0
    selmod[1, 1 + seq, t] = 1.0
    c["selmod"] = selmod
    bandcur = np.zeros((2, 4, 128, 128), np.float32)
    bandprev = np.zeros((4, 128, 128), np.float32)
    bandbuf = np.zeros((2, 4, 128, 128), np.float32)
    for wi, w in enumerate(WINS):
        s = t[:, None]
        tt = t[None, :]
        bandcur[0, wi] = ((s <= tt) & (s > tt - w))
        bandcur[1, wi] = ((s <= tt) & (s > tt - w) & (seq[:, None] == seq[None, :]))
        bandprev[wi] = ((s - 128) > (tt - w))
        for ab in range(2):
            for bb in range(8):
                for r in range(15):
                    row = bb * 15 + r
                    for i in range(8):
                        tok = (ab * 8 + bb) * 8 + i
                        if (r - 15) > (i - w):
                            bandbuf[ab, wi, row, tok] = 1.0
    c["bandcur"] = bandcur
    c["bandprev"] = bandprev
    c["bandbuf"] = bandbuf
    rc = np.zeros((3, 128, 4), np.float32)
    for wi, w in enumerate(WINS):
        rc[0, :, wi] = 1.0 / w
        rc[1, :, wi] = 1.0 / np.minimum(t + 1, w)
        rc[2, :, wi] = 1.0 / w
    c["rc"] = rc
    return c


_CONSTS = None


def build_nc():
    nc = bass.Bass("TRN2", target_bir_lowering=False)
    es = ExitStack()
    with es:
        din = lambda name, shape: nc.dram_tensor(name, list(shape), F32, kind="ExternalInput").ap()
        dout = lambda name, shape: nc.dram_tensor(name, list(shape), F32, kind="ExternalOutput").ap()
        xin = din("xin", (NT, 128, D))
        cc = din("cc", (17, D))
        w_mod = din("w_mod", (2, D, 3 * D))
        b_modT = din("b_modT", (2, 128, 16))
        b_modg = din("b_modg", (2, 1, D))
        norm_gT = din("norm_gT", (2, 128, 8))
        w_in = din("w_in", (2, D, NIN))
        b_if = din("b_if", (2, 1, 8))
        sgu_ln_g = din("sgu_ln_g", (2, 1, D))
        sgu_ln_b = din("sgu_ln_b", (2, 1, D))
        wsgu = din("wsgu", (2, 2, 4, 128, 128))
        bsgu = din("bsgu", (2, 2, 128, 4))
        mnorm_gT = din("mnorm_gT", (2, 128, 8))
        w_pool = din("w_pool", (2, 4, 256, 256))
        pscaleT = din("pscaleT", (2, 128, 8))
        w_br = [din("w_br_a", (2, D, D)), din("w_br_b", (2, D, D)), din("w_br_c", (2, D, D))]
        w_out = din("w_out", (2, D, D))
        fin_g = din("fin_g", (1, D))
        C0 = din("C0", (2, 16, 4, 256, 256))
        n0 = din("n0", (2, 16, D))
        n0tok = din("n0tok", (2, 128, D))
        m0tok = din("m0tok", (2, 128, 4))
        pool0 = din("pool0", (2, 2, 120, D))
        pool0raw = din("pool0raw", (2, 16, 15, D))
        cst = {k: din("c_" + k, v.shape) for k, v in make_consts().items()}
        y = dout("y", (NT, 128, D))
        oCp = dout("oCp", (2, 4, 256, 256))
        onp = dout("onp", (2, 4, 256))
        omp = dout("omp", (2, 4))
        obp = dout("obp", (2, 15, D))
        oCs = dout("oCs", (2, 16, 4, 256, 256))
        ons = dout("ons", (2, 16, D))
        oms = dout("oms", (2, 16, 4))
        obs = dout("obs", (2, 16, 15, D))
        ovs = dout("ovs", (2, 128, D))

        P = Prog(nc, es)
        B = P.buf
        sb = lambda name, shape, dt=F32: es.enter_context(nc.sbuf_tensor(name, list(shape), dt))
        ps = lambda name, shape, dt=F32: es.enter_context(nc.psum_tensor(name, list(shape), dt))

        xg = sb("xg", (128, GMAX, D))
        hT = sb("hT", (128, 8, GMAX * 128), BF16)
        F1 = sb("F1", (128, GMAX, D))
        VV = sb("VV", (128, GMAX, D), BF16)
        GT = sb("GT", (128, GMAX, 512))
        MG = sb("MG", (128, GMAX, D))
        QB = sb("QB", (128, GMAX, D), BF16)
        KB = sb("KB", (128, GMAX, D), BF16)
        GP = sb("GP", (128, GMAX, 8))
        YT = sb("YT", (128, 8, GMAX * 128), BF16)
        wring = [sb("wr%d" % i, (128, 8, 512), BF16) for i in range(NW)]
        tmpA = [sb("tmpA%d" % i, (128, 512)) for i in range(2)]
        xnbs = [sb("xnb%d" % i, (128, D), BF16) for i in range(2)]
        ybfs = [sb("ybf%d" % i, (128, D), BF16) for i in range(2)]
        Cst = [sb("Cst%d" % l, (128, 2, 4, 256)) for l in range(2)]
        Cbf = [sb("Cbf%d" % l, (128, 2, 4, 256), BF16) for l in range(2)]
        nst = [sb("nst%d" % l, (128, 2, 4)) for l in range(2)]
        nbf = [sb("nbf%d" % l, (128, 2, 4), BF16) for l in range(2)]
        mrep = [sb("mrep%d" % l, (128, 4)) for l in range(2)]
        pprev = [sb("pprev%d" % l, (128, D), BF16) for l in range(2)]
        QT = sb("QT", (128, 8, 128), BF16)
        KT = sb("KT", (128, 8, 128), BF16)
        QsT = sb("QsT", (128, 8, 128), BF16)
        SWT = sb("SWT", (128, 512), BF16)
        WTt = sb("WTt", (128, 512))
        DG = sb("DG", (128, 512))
        AM = sb("AM", (128, 512))
        sm = sb("sm", (128, 64))
        ZQ = [Cbf[0][:, i].rearrange("p h (a t) -> p (h a) t", a=2) for i in range(2)]
        KWb = [Cbf[1][:, i].rearrange("p h e -> p (h e)") for i in range(2)]
        t4 = lambda ap: ap.rearrange("p (h e) -> p h e", h=4)
        C0f = [t4(xg[:, 3, :]), t4(F1[:, 3, :]), t4(MG[:, 3, :])]
        C0b = [t4(VV[:, 1, :]), t4(VV[:, 2, :]), t4(VV[:, 3, :])]
        Cstage = [t4(GT[:, 1:3, :].rearrange("p a d -> p (a d)")), t4(GT[:, 3, :].rearrange("p d -> p d") if False else KB[:, 1:3, :].rearrange("p a d -> p (a d)").bitcast(F32))]
        fence_t = sb("fence_t", (128, 1))
        DEC = sb("DEC", (128, 64))
        WL = sb("WL", (128, 64))
        s16 = sb("s16", (16, 16))
        pbufA = QB[:, 1, :]
        pbufB = QB[:, 2, :]
        identf = sb("identf", (128, 128))
        identb = sb("identb", (128, 128), BF16)
        onesf = sb("onesf", (128, 128))
        onesb = sb("onesb", (128, 128), BF16)
        tri = sb("tri", (128, 2, 128))
        maskneg = sb("maskneg", (128, 2, 128))
        bigmask = sb("bigmask", (128, 2, 128))
        sellast = sb("sellast", (128, 2, 128))
        seqcol = sb("seqcol", (128, 16))
        seqcolb = sb("seqcolb", (128, 16), BF16)
        lastsel = sb("lastsel", (128, 16))
        selmod = sb("selmod", (49, 2, 128))
        bandcur = sb("bandcur", (128, 2, 4, 128), BF16)
        bandprev = sb("bandprev", (128, 4, 128), BF16)
        bandbuf = sb("bandbuf", (128, 2, 4, 128), BF16)
        rc = sb("rc", (128, 3, 4))
        WTs = sb("WTs", (128, 2, 2, 4, 128), BF16)
        bsg = sb("bsg", (128, 2, 2, 4))
        lng = sb("lng", (128, D))
        lnb = sb("lnb", (128, D))
        bifb = sb("bifb", (128, 2, 8))
        nrmgT = sb("nrmgT", (128, 2, 8))
        mngT = sb("mngT", (128, 2, 8))
        pscT = sb("pscT", (128, 2, 8))
        bmT = sb("bmT", (128, 2, 16))
        AT = sb("AT", (128, 2, 8, 17))
        SH = sb("SH", (128, 2, 8, 17))
        gate17 = sb("gate17", (49, D))
        gbc = sb("gbc", (128, D))
        ccT = sb("ccT", (128, 8, 17), BF16)
        epsc = sb("epsc", (128, 1))
        tmpscr = sb("tmpscr", (128, D))
        lnst = sb("lnst", (128, 16, 6))
        lnmv = sb("lnmv", (128, 16, 2))
        lnrs = sb("lnrs", (128, 16))
        rs = sb("rs", (128, 4, 4))
        sm2 = sb("sm2", (128, 4, 12))
        pm = [ps("pm%d" % i, (128, 512)) for i in range(3)]
        pt = [ps("pt%d" % i, (128, 8, 128), BF16) for i in range(2)]
        pa = [ps("pa%d" % i, (128, 512)) for i in range(3)]
        ptf = [pt[i][:].rearrange("p c t -> p (c t)").bitcast(F32) for i in range(2)]
        Bpm = [B("pm%d" % i) for i in range(3)]
        Bpt = [B("pt%d" % i) for i in range(2)]
        Bpa = [B("pa%d" % i) for i in range(3)]
        rr = {"pm": 0, "pt": 0, "w": 0, "tmp": 0}

        def nxt(key, n):
            i = rr[key]
            rr[key] = (i + 1) % n
            return i

        def act(out, in_, func, reads, writes, **kw_):
            P.emit("act", lambda e: e.activation(out=out, in_=in_, func=func, **kw_), reads, writes)

        def tt(out, in0, in1, op, reads, writes, eng="dve"):
            P.emit(eng, lambda e: e.tensor_tensor(out=out, in0=in0, in1=in1, op=op), reads, writes)

        def ts(out, in0, s1, s2, op0, op1, reads, writes, eng="dve"):
            if op1 is None:
                P.emit(eng, lambda e: e.tensor_scalar(out=out, in0=in0, scalar1=s1, scalar2=None, op0=op0), reads, writes)
            else:
                P.emit(eng, lambda e: e.tensor_scalar(out=out, in0=in0, scalar1=s1, scalar2=s2, op0=op0, op1=op1), reads, writes)

        def stt(out, in0, scalar, in1, op0, op1, reads, writes):
            P.emit("dve", lambda e: e.scalar_tensor_tensor(out=out, in0=in0, scalar=scalar, in1=in1, op0=op0, op1=op1), reads, writes)

        def cp(out, in_, reads, writes, eng="dve"):
            if eng == "act":
                P.emit("act", lambda e: e.activation(out=out, in_=in_, func=AF.Copy), reads, writes)
            else:
                P.emit(eng, lambda e: e.tensor_copy(out=out, in_=in_), reads, writes)

        def red(out, in_, op, reads, writes):
            P.emit("dve", lambda e: e.tensor_reduce(out=out, in_=in_, axis=AX.X, op=op), reads, writes)

        def mmg(mms, reads, writes):
            def fn(e, mms=mms):
                for (o, l, r, st, sp) in mms:
                    ins = e.matmul(o, lhsT=l, rhs=r, start=st, stop=sp)
                return ins
            P.emit("pe", fn, reads, writes)

        def transposes(pti, src_fn, n, reads):
            def fn(e):
                for c in range(n):
                    ins = e.transpose(out=pt[pti][:, c, :], in_=src_fn(c), identity=identb[:])
                return ins
            P.emit("pe", fn, list(reads) + [B("identb")], [Bpt[pti]])

        def bcast3(ap2, n):
            return ap2.unsqueeze(2).to_broadcast([ap2.shape[0], ap2.shape[1], n])

        def ld(dst, src, name, q="sp"):
            P.dma(q, dst, src, writes=[B(name)])
        ld(identf[:], cst["identf"], "identf")
        ld(onesf[:], cst["onesf"], "onesf")
        ld(identb[:], cst["identf"], "identb", "pool")
        ld(onesb[:], cst["onesf"], "onesb", "pool")
        ld(tri[:], cst["tri"].rearrange("k s t -> s k t"), "tri")
        ld(maskneg[:], cst["maskneg"].rearrange("k s t -> s k t"), "maskneg")
        ld(bigmask[:], cst["bigmask"].rearrange("k s t -> s k t"), "bigmask")
        ld(sellast[:], cst["sellast"].rearrange("k s t -> s k t"), "sellast")
        ld(seqcol[:], cst["seqcol"], "seqcol")
        ld(seqcolb[:], cst["seqcol"], "seqcolb", "pool")
        ld(lastsel[:], cst["lastsel"], "lastsel")
        ld(selmod[0:17], cst["selmod"].rearrange("k r t -> r k t"), "selmod")
        ld(selmod[32:49], cst["selmod"].rearrange("k r t -> r k t"), "selmod")
        ld(bandcur[:].rearrange("p k w t -> p (k w) t"), cst["bandcur"].rearrange("k w s t -> s (k w) t"), "bandcur", "pool")
        ld(bandprev[:], cst["bandprev"].rearrange("w s t -> s w t"), "bandprev", "pool")
        ld(bandbuf[:].rearrange("p k w t -> p (k w) t"), cst["bandbuf"].rearrange("k w s t -> s (k w) t"), "bandbuf", "pool")
        ld(rc[:], cst["rc"].rearrange("k t w -> t k w"), "rc")
        ld(bsg[:].rearrange("p l k g -> p (l k) g"), bsgu.rearrange("l k p g -> p (l k) g"), "bsg")
        for l in range(2):
            ld(bifb[:, l, :], b_if[l].to_broadcast([128, 8]), "bifb")
        ld(nrmgT[:], norm_gT.rearrange("l p c -> p l c"), "nrmgT")
        ld(mngT[:], mnorm_gT.rearrange("l p c -> p l c"), "mngT")
        ld(pscT[:], pscaleT.rearrange("l p c -> p l c"), "pscT")
        ld(bmT[:], b_modT.rearrange("l p c -> p l c"), "bmT")
        P.emit("pool", lambda e: e.memset(epsc[:], EPS), writes=[B("epsc")])
        for l in range(2):
            P.emit("pool", lambda e, l=l: e.memset(Cst[l][:], 0.0), writes=[B("Cst%d" % l)])
            P.emit("pool", lambda e, l=l: e.memset(Cbf[l][:], 0.0), writes=[B("Cbf%d" % l)])
            P.emit("pool", lambda e, l=l: e.memset(nst[l][:], 0.0), writes=[B("nst%d" % l)])
            P.emit("pool", lambda e, l=l: e.memset(nbf[l][:], 0.0), writes=[B("nbf%d" % l)])
            P.emit("pool", lambda e, l=l: e.memset(mrep[l][:], 0.0), writes=[B("mrep%d" % l)])
            P.emit("pool", lambda e, l=l: e.memset(pprev[l][:], 0.0), writes=[B("pprev%d" % l)])

        wscr = nc.dram_tensor("wscr", [2, NBLK, 128, 8, 512], BF16, kind="Internal").ap()
        cur = {"l": None, "blk": 0, "first": True, "nw": NW}
        def _slot(t3):
            return t3.rearrange("p a d -> p (a d)").bitcast(BF16).rearrange("p (k n) -> p k n", k=8)
        wring_x = list(wring) + [_slot(MG[:, 1:3, :]), _slot(xg[:, 1:3, :]), _slot(F1[:, 1:3, :])]

        def wload(src_ap, ncols, pre=False):
            i = rr["w"]
            rr["w"] = (i + 1) % cur["nw"]
            slot, bslot = wring_x[i], B("wr%d" % i)
            if cur["l"] is None:
                P.dma("pool", slot[:, :, 0:ncols], src_ap.rearrange("(k p) n -> p k n", p=128), writes=[bslot], dsem=NW + i)
                return slot, bslot
            l_, blk = cur["l"], cur["blk"]
            cur["blk"] += 1
            assert blk < NBLK
            bscr = B("wscr_%d_%d" % (l_, blk))
            if cur["first"]:
                src = src_ap if pre else src_ap.rearrange("(k p) n -> p k n", p=128)
                P.dma("pool", slot[:, :, 0:ncols], src, writes=[bslot], dsem=NW + i)
                P.dma("sp", wscr[l_, blk][:, :, 0:ncols], slot[:, :, 0:ncols], reads=[bslot], writes=[bscr])
            else:
                P.dma("sp", slot[:, :, 0:ncols], wscr[l_, blk][:, :, 0:ncols], reads=[bscr], writes=[bslot], dsem=(i if i < NW else 2 * NW + i))
            return slot, bslot

        for l in range(2):
            for kind in range(2):
                for g in range(4):
                    t_ = tmpA[nxt("tmp", 2)]
                    bt = B(t_.name)
                    P.dma("sp", t_[:, 0:128], wsgu[l, kind, g], writes=[bt])
                    pi = nxt("pm", 3)
                    P.emit("pe", lambda e, pi=pi, t_=t_: e.transpose(out=pm[pi][:, 0:128], in_=t_[:, 0:128], identity=identf[:]), [bt, B("identf")], [Bpm[pi]])
                    tt(WTs[:, l, kind, g, :], pm[pi][:, 0:128], tri[:, kind, :], ALU.mult, [B("tri")], [B("WTs"), Bpm[pi]])

        ccs = tmpscr[0:17, :]
        ccb = xnbs[0][0:17, :]
        bg17 = F1[0:17, 0, :]
        P.dma("sp", ccs, cc, writes=[B("tmpscr")])
        act(ccb, ccs, AF.Silu, [B("tmpscr")], [B("xnb0")])

        def cctr(e):
            for c in range(8):
                ins = e.transpose(out=pt[0][:, c, 0:17], in_=ccb[:, c * 128:(c + 1) * 128], identity=identb[0:17, 0:17])
            return ins
        P.emit("pe", cctr, [B("xnb0"), B("identb")], [Bpt[0]])
        cp(ccT[:], pt[0][:, :, 0:17], [], [B("ccT"), Bpt[0]])
        for l in range(2):
            bg17l = F1[32 * l:32 * l + 17, 0, :]
            P.dma("sp", bg17l, b_modg[l].to_broadcast([17, D]), writes=[B("F1_0")])
            for blk in range(6):
                wt, bw = wload(w_mod[l][:, blk * 512:(blk + 1) * 512], 512)
                if blk < 4:
                    for j in range(4):
                        ch = blk * 4 + j
                        pi = nxt("pm", 3)
                        mmg([(pm[pi][:, 0:17], wt[:, k, j * 128:(j + 1) * 128], ccT[:, k, :], k == 0, k == 7) for k in range(8)],
                            [bw, B("ccT")], [Bpm[pi]])
                        if ch < 8:
                            ts(SH[:, l, ch, :], pm[pi][:, 0:17], bmT[:, l, ch:ch + 1], None, ALU.add, None, [B("bmT")], [B("SH"), Bpm[pi]])
                        else:
                            c8 = ch - 8
                            ts(AT[:, l, c8, :], pm[pi][:, 0:17], bmT[:, l, ch:ch + 1], 1.0, ALU.add, ALU.add, [B("bmT")], [B("AT"), Bpm[pi]])
                            ts(AT[:, l, c8, :], AT[:, l, c8, :], nrmgT[:, l, c8:c8 + 1], None, ALU.mult, None, [B("nrmgT"), B("AT")], [B("AT")])
                else:
                    cb = blk - 4
                    pi = nxt("pm", 3)
                    mmg([(pm[pi][32 * l:32 * l + 17, :], ccT[:, k, :], wt[:, k, :], k == 0, k == 7) for k in range(8)], [bw, B("ccT")], [Bpm[pi]])
                    tt(gate17[32 * l:32 * l + 17, cb * 512:(cb + 1) * 512], pm[pi][32 * l:32 * l + 17, :], bg17l[:, cb * 512:(cb + 1) * 512], ALU.add, [B("F1_0")], [B("gate17"), Bpm[pi]])

        def rms_stats(src, reads, slot):
            Br = B("rs%d" % slot)
            act(tmpscr[:], src, AF.Square, reads, [B("tmpscr"), Br], accum_out=rs[:, slot, 0:1])
            act(rs[:, slot, 1:2], rs[:, slot, 0:1], AF.Sqrt, [Br, B("epsc")], [Br], scale=1.0 / D, bias=epsc[:])
            P.emit("dve", lambda e: e.reciprocal(out=rs[:, slot, 2:3], in_=rs[:, slot, 1:2]), [Br], [Br])
            return rs[:, slot, 2:3], Br

        def pipe2(n, stA, stB):
            for i in range(n + 1):
                if i < n:
                    stA(i)
                if i >= 1:
                    stB(i - 1)

        def to_T(dst, gi, src_bf, src_bufs, scale_ap=None, scale_bufs=()):
            pi = nxt("pt", 2)
            transposes(pi, lambda c: src_bf[:, c * 128:(c + 1) * 128], 8, src_bufs)
            dv = dst[:, :, gi * 128:(gi + 1) * 128]
            dbuf = B("YT_%d" % gi)
            if scale_ap is None:
                cp(dv, pt[pi][:], [], [dbuf, Bpt[pi]])
            else:
                tt(dv, pt[pi][:], bcast3(scale_ap, 128), ALU.mult, list(scale_bufs), [dbuf, Bpt[pi]])
            return dbuf

        def make_h_A(l, gi, kind):
            xb_ = B("xg%d" % gi)
            rstd, Br = rms_stats(xg[:, gi, :], [xb_], gi)
            xnb, Bx = xnbs[gi % 2], B("xnb%d" % (gi % 2))
            ts(xnb[:], xg[:, gi, :], rstd, None, ALU.mult, None, [xb_, Br], [Bx])

        def make_h_B(l, gi, kind):
            xnb, Bx = xnbs[gi % 2], B("xnb%d" % (gi % 2))
            pi = nxt("pt", 2)
            transposes(pi, lambda c: xnb[:, c * 128:(c + 1) * 128], 8, [Bx])
            hb_ = B("hT_%d" % gi)
            if kind == 0:
                for c in range(4):
                    act(hT[:, c, gi * 128:(gi + 1) * 128], pt[pi][:, c, :], AF.Identity, [B("AT"), B("SH")], [hb_, Bpt[pi]],
                        scale=AT[:, l, c, 0:1], bias=SH[:, l, c, 0:1])
                t_ = tmpA[nxt("tmp", 2)]
                tv = t_[:].rearrange("p (c t) -> p c t", c=4)
                tt(tv, pt[pi][:, 4:8, :], AT[:, l, 4:8, 0:1].to_broadcast([128, 4, 128]), ALU.mult, [B("AT")], [B(t_.name), Bpt[pi]])
                tt(hT[:, 4:8, gi * 128:(gi + 1) * 128], tv, SH[:, l, 4:8, 0:1].to_broadcast([128, 4, 128]), ALU.add, [B("SH"), B(t_.name)], [hb_])
            else:
                for c in range(8):
                    t_ = tmpA[nxt("tmp", 2)]
                    tv = t_[:, 0:128].rearrange("p (b i) -> p b i", i=8)
                    tt(tv, pt[pi][:, c, :].rearrange("p (b i) -> p b i", i=8), bcast3(AT[:, l, c, 1:17], 8), ALU.mult, [B("AT")], [B(t_.name), Bpt[pi]])
                    tt(hT[:, c, gi * 128:(gi + 1) * 128].rearrange("p (b i) -> p b i", i=8), tv, bcast3(SH[:, l, c, 1:17], 8), ALU.add, [B("SH"), B(t_.name)], [hb_])

        def proj(l, col0, ncols, tiles_gi, evac, src=None, srcT=None, srcbufs=None):
            W = w_in[l] if src is None else src
            T_ = hT if srcT is None else srcT
            nblk = (ncols + 511) // 512
            for cb in range(nblk):
                n = min(512, ncols - cb * 512)
                wt, bw = wload(W[:, col0 + cb * 512: col0 + cb * 512 + n], n)
                for gi in tiles_gi:
                    pi = nxt("pm", 3)
                    tb = (B("hT_%d" % gi) if srcbufs is None else srcbufs[gi])
                    mmg([(pm[pi][:, 0:n], T_[:, k, gi * 128:(gi + 1) * 128], wt[:, k, 0:n], k == 0, k == 7) for k in range(8)],
                        [bw, tb], [Bpm[pi]])
                    evac(gi, cb, pm[pi], Bpm[pi], n)

        def mlstm_head(l, gi, kind):
            Bg = B("GP%d" % gi)
            h4 = lambda ap: ap.rearrange("p (h s) -> p h s", h=4)
            m4 = lambda m: m[:, kind, :].unsqueeze(1).to_broadcast([128, 4, 128])
            ig = GP[:, gi, 0:4]
            fg = GP[:, gi, 4:8]
            lsp, A_, CM = sm2[:, gi, 0:4], sm2[:, gi, 4:8], sm2[:, gi, 8:12]
            Bl, Ba, Bc = B("h_lsp%d" % gi), B("h_a%d" % gi), B("h_cm%d" % gi)
            psm, Bps = pa[2], Bpa[2]
            c0 = 32 * gi
            act(lsp, fg, AF.Exp, [Bg], [Bl], scale=-1.0)
            act(lsp, lsp, AF.Ln, [Bl], [Bl], bias=1.0)
            mmg([(psm[:, c0:c0 + 4], tri[:, kind, :], lsp, True, True), (psm[:, c0 + 4:c0 + 8], onesf[:], lsp, True, True)],
                [B("tri"), B("onesf"), Bl], [Bps])
            tt(A_, ig, psm[:, c0:c0 + 4], ALU.add, [Bg], [Ba, Bps])
            tt(h4(DG[:]), identf[:].unsqueeze(1).to_broadcast([128, 4, 128]), bcast3(A_, 128), ALU.mult, [B("identf"), Ba], [B("DG")])
            mmg([(pa[0][:], onesf[:], DG[:], True, True)], [B("onesf"), B("DG")], [Bpa[0]])
            tt(h4(AM[:]), h4(pa[0][:]), m4(maskneg), ALU.add, [B("maskneg")], [B("AM"), Bpa[0]])
            red(CM, h4(AM[:]), ALU.max, [B("AM")], [Bc])

        def mlstm_tile(l, gi, tile, kind):
            first = (tile == 1)
            Bq, Bk, Bv, Bg = B("QB%d" % gi), B("KB%d" % gi), B("VV%d" % gi), B("GP%d" % gi)
            S = lambda a, b_: sm[:, a:b_]
            Qs = xnbs[gi % 2]
            kw = ybfs[gi % 2]
            BQs, Bkw = B("xnb%d" % (gi % 2)), B("ybf%d" % (gi % 2))
            h4 = lambda ap: ap.rearrange("p (h s) -> p h s", h=4)
            m4 = lambda m: m[:, kind, :].unsqueeze(1).to_broadcast([128, 4, 128])
            A_, CM = sm2[:, gi, 4:8], sm2[:, gi, 8:12]
            Ba, Bc = B("h_a%d" % gi), B("h_cm%d" % gi)
            psm, Bps = pa[2], Bpa[2]
            c0 = 32 * gi
            use_inter = not (kind == 0 and first)
            if kind == 0:
                cp(S(48, 52), mrep[l][:], [B("mrep%d" % l)], [B("sm_mprev")])
            else:
                P.dma("sp", S(48, 52), m0tok[l], writes=[B("sm_mprev")])
            tt(S(12, 16), CM, S(48, 52), ALU.max, [Bc, B("sm_mprev")], [B("sm_g")])
            tt(S(20, 24), S(48, 52), S(12, 16), ALU.subtract, [B("sm_mprev"), B("sm_g")], [B("sm_winter")])
            act(S(20, 24), S(20, 24), AF.Exp, [B("sm_winter")], [B("sm_winter")])
            pi = nxt("pt", 2)
            transposes(pi, lambda c: QB[:, gi, c * 128:(c + 1) * 128], 8, [Bq])
            cp(QT[:], pt[pi][:], [], [B("QT"), Bpt[pi]])
            pi = nxt("pt", 2)
            transposes(pi, lambda c: KB[:, gi, c * 128:(c + 1) * 128], 8, [Bk])
            cp(KT[:], pt[pi][:], [], [B("KT"), Bpt[pi]], eng="act")
            mmg([(pm[2][:, h * 128:(h + 1) * 128], KT[:, 2 * h + hf, :], QT[:, 2 * h + hf, :], hf == 0, hf == 1) for h in range(4) for hf in range(2)],
                [B("KT"), B("QT")], [Bpm[2]])
            if use_inter:
                tt(Qs[:].rearrange("p (h d) -> p h d", h=4), QB[:, gi, :].rearrange("p (h d) -> p h d", h=4), bcast3(S(20, 24), 256), ALU.mult,
                   [Bq, B("sm_winter")], [BQs])
            tt(h4(DG[:]), identf[:].unsqueeze(1).to_broadcast([128, 4, 128]), bcast3(S(12, 16), 128), ALU.mult,
               [B("identf"), B("sm_g")], [B("DG")])
            if use_inter:
                pi = nxt("pt", 2)
                transposes(pi, lambda c: Qs[:, c * 128:(c + 1) * 128], 8, [BQs])
                cp(QsT[:], pt[pi][:], [], [B("QsT"), Bpt[pi]], eng="act")
            mms = []
            for h in range(4):
                mms.append((pa[1][:, h * 128:(h + 1) * 128], onesf[:], DG[:, h * 128:(h + 1) * 128], h == 0, False))
            for h in range(4):
                mms.append((pa[1][:, h * 128:(h + 1) * 128], identf[:], bigmask[:, kind, :], False, h == 3))
            mmg(mms, [B("onesf"), B("DG"), B("identf"), B("bigmask")], [Bpa[1]])
            for h in range(4):
                act(WTt[:, h * 128:(h + 1) * 128], pa[1][:, h * 128:(h + 1) * 128], AF.Exp, [Ba], [B("WTt"), Bpa[1]],
                    scale=-1.0, bias=sm2[:, gi, 4 + h:5 + h])
            tt(SWT[:], pm[2][:], WTt[:], ALU.mult, [B("WTt")], [B("SWT"), Bpm[2]])
            tt(h4(AM[:]), h4(pa[1][:]), m4(sellast), ALU.mult, [B("sellast")], [B("AM"), Bpa[1]])
            red(S(16, 20), h4(AM[:]), ALU.add, [B("AM")], [B("sm_glast")])
            if kind == 0:
                tt(mrep[l][:], S(16, 20), psm[:, c0 + 4:c0 + 8], ALU.subtract, [B("sm_glast"), B("sm_mprev")], [B("mrep%d" % l), Bps])
                tt(S(40, 44), S(48, 52), S(16, 20), ALU.subtract, [B("sm_mprev"), B("sm_glast")], [B("sm_decay")])
                act(S(40, 44), S(40, 44), AF.Exp, [B("sm_decay")], [B("sm_decay")])
                tt(S(44, 48), A_, S(16, 20), ALU.subtract, [Ba, B("sm_glast")], [B("sm_wend")])
                act(S(44, 48), S(44, 48), AF.Exp, [B("sm_wend")], [B("sm_wend")])
                tt(kw[:].rearrange("p (h d) -> p h d", h=4), KB[:, gi, :].rearrange("p (h d) -> p h d", h=4), bcast3(S(44, 48), 256), ALU.mult,
                   [Bk, B("sm_wend")], [Bkw])
            NB = [(pm[0], Bpm[0]), (pm[1], Bpm[1]), (pa[0], Bpa[0]), (pa[1], Bpa[1])]
            for h in range(4):
                bank, bbuf = NB[h]
                o = bank[:, 0:256]
                mms = []
                rds = [B("SWT"), Bv]
                if kind == 0 and use_inter:
                    mms.append((o, SWT[:, h * 128:(h + 1) * 128], VV[:, gi, h * 256:(h + 1) * 256], True, False))
                    mms.append((o, QsT[:, 2 * h, :], Cbf[l][:, 0, h, :], False, False))
                    mms.append((o, QsT[:, 2 * h + 1, :], Cbf[l][:, 1, h, :], False, True))
                    rds += [B("QsT"), B("Cbf%d" % l)]
                elif kind == 0:
                    mms.append((o, SWT[:, h * 128:(h + 1) * 128], VV[:, gi, h * 256:(h + 1) * 256], True, True))
                else:
                    mms.append((o, SWT[:, h * 128:(h + 1) * 128], VV[:, gi, h * 256:(h + 1) * 256], True, False))
                mmg(mms, rds, [bbuf])
            if kind == 1:
                tt(S(44, 48), A_, S(16, 20), ALU.subtract, [Ba, B("sm_glast")], [B("sm_wend")])
                act(S(44, 48), S(44, 48), AF.Exp, [B("sm_wend")], [B("sm_wend")])
                tt(kw[:].rearrange("p (h d) -> p h d", h=4), KB[:, gi, :].rearrange("p (h d) -> p h d", h=4), bcast3(S(44, 48), 256), ALU.mult,
                   [Bk, B("sm_wend")], [Bkw])
                tt(WL[:].rearrange("p (b h) -> p b h", h=4), lastsel[:].unsqueeze(2).to_broadcast([128, 16, 4]), S(20, 24).unsqueeze(1).to_broadcast([128, 16, 4]), ALU.mult,
                   [B("lastsel"), B("sm_winter")], [B("WL")])
                mmg([(pm[2][:, 0:64], onesf[:], WL[:], True, True)], [B("onesf"), B("WL")], [Bpm[2]])
                cp(DEC[:], pm[2][:, 0:64], [], [B("DEC"), Bpm[2]])
                for zi in range(2):
                    P.emit("pool", lambda e, zi=zi: e.memset(ZQ[zi][:], 0.0), [], [B("ZQ%d" % zi)])
                UB = [(pm[2], Bpm[2]), (ptf[0], Bpt[0]), (ptf[1], Bpt[1])]
                rnd = 0
                for b in range(16):
                    zi = b % 2
                    if b >= 2:
                        P.emit("pool", lambda e, zi=zi, b=b: e.memset(ZQ[zi][:, :, (b - 2) * 8:(b - 1) * 8], 0.0), [], [B("ZQ%d" % zi)])
                    cp(ZQ[zi][:, :, b * 8:(b + 1) * 8], QsT[:, :, b * 8:(b + 1) * 8], [B("QsT")], [B("ZQ%d" % zi)], eng="pool")
                    ts(KWb[zi][:], kw[:], seqcol[:, b:b + 1], None, ALU.mult, None, [Bkw, B("seqcol")], [B("KWb%d" % zi)])
                    for k in range(2):
                        fi = (2 * b + k) % len(C0f)
                        ci = (2 * b + k) % len(C0b)
                        si = (2 * b + k) % len(Cstage)
                        P.dma("pool", C0f[fi][:], C0[l, b][:, k * 128:(k + 1) * 128, :].rearrange("h p e -> p h e"), writes=[B("C0f%d" % fi)])
                        cp(C0b[ci][:], C0f[fi][:], [B("C0f%d" % fi)], [B("C0b%d" % ci)], eng="act")
                        for h in range(4):
                            bank, bbuf = NB[h]
                            mmg([(bank[:, 0:256], ZQ[zi][:, 2 * h + k, :], C0b[ci][:, h, :], False, (b == 15 and k == 1))],
                                [B("ZQ%d" % zi), B("C0b%d" % ci)], [bbuf])
                        for pr in range(2):
                            ub, ubuf = UB[rnd % 3]
                            rnd += 1
                            mms = []
                            for hh in range(2):
                                h = pr * 2 + hh
                                mms.append((ub[:, hh * 256:(hh + 1) * 256], KWb[zi][:, h * 256 + k * 128: h * 256 + k * 128 + 128], VV[:, gi, h * 256:(h + 1) * 256], True, True))
                            mmg(mms, [B("KWb%d" % zi), Bv], [ubuf])
                            for hh in range(2):
                                h = pr * 2 + hh
                                stt(Cstage[si][:, h, :], C0f[fi][:, h, :], DEC[:, b * 4 + h: b * 4 + h + 1], ub[:, hh * 256:(hh + 1) * 256], ALU.mult, ALU.add,
                                    [B("DEC"), B("C0f%d" % fi)], [B("Cstage%d" % si), ubuf])
                        P.dma("sp", oCs[l, b][:, k * 128:(k + 1) * 128, :].rearrange("h p e -> p h e"), Cstage[si][:], reads=[B("Cstage%d" % si)])
            mms = []
            rds = [B("SWT"), B("onesb")]
            for h in range(4):
                o = psm[:, 8 + h:9 + h]
                if kind == 0 and use_inter:
                    mms.append((o, SWT[:, h * 128:(h + 1) * 128], onesb[:, 0:1], True, False))
                    mms.append((o, QsT[:, 2 * h, :], nbf[l][:, 0, h:h + 1], False, False))
                    mms.append((o, QsT[:, 2 * h + 1, :], nbf[l][:, 1, h:h + 1], False, True))
                    rds += [B("QsT"), B("nbf%d" % l)]
                else:
                    mms.append((o, SWT[:, h * 128:(h + 1) * 128], onesb[:, 0:1], True, True))
            mmg(mms, rds, [Bps])
            cp(S(32, 36), psm[:, 8:12], [], [B("sm_den"), Bps])
            if kind == 1:
                P.dma("sp", tmpscr[:], n0tok[l], writes=[B("tmpscr")])
                tt(tmpscr[:], tmpscr[:], QB[:, gi, :], ALU.mult, [Bq, B("tmpscr")], [B("tmpscr")])
                red(S(52, 56), tmpscr[:].rearrange("p (h d) -> p h d", h=4), ALU.add, [B("tmpscr")], [B("sm_qn")])
                tt(S(52, 56), S(52, 56), S(20, 24), ALU.mult, [B("sm_qn"), B("sm_winter")], [B("sm_qn")])
                tt(S(32, 36), S(32, 36), S(52, 56), ALU.add, [B("sm_den"), B("sm_qn")], [B("sm_den")])
            tt(S(24, 28), S(12, 16), psm[:, c0:c0 + 4], ALU.subtract, [B("sm_g")], [B("sm_mt"), Bps])
            act(S(28, 32), S(24, 28), AF.Exp, [B("sm_mt")], [B("sm_emt")], scale=-1.0)
            stt(S(36, 40), S(32, 36), -1.0, S(32, 36), ALU.mult, ALU.max, [B("sm_den")], [B("sm_r")])
            tt(S(36, 40), S(36, 40), S(28, 32), ALU.max, [B("sm_r"), B("sm_emt")], [B("sm_r")])
            P.emit("dve", lambda e: e.reciprocal(out=S(36, 40), in_=S(36, 40)), [B("sm_r")], [B("sm_r")])
            Bf = B("F1_%d" % gi)
            for h in range(4):
                bank, bbuf = NB[h]
                act(F1[:, gi, h * 256:(h + 1) * 256], bank[:, 0:256], AF.Copy, [B("sm_r")], [Bf, bbuf], scale=S(36 + h, 37 + h))
            if kind == 0:
                UBk = [(ptf[0], Bpt[0]), (ptf[1], Bpt[1]), (pm[2], Bpm[2])]
                rnd = 0
                for hf in range(2):
                    for pr in range(2):
                        ub, ubuf = UBk[rnd % 3]
                        rnd += 1
                        mms = []
                        for hh in range(2):
                            h = pr * 2 + hh
                            mms.append((ub[:, hh * 256:(hh + 1) * 256], kw[:, h * 256 + hf * 128: h * 256 + hf * 128 + 128], VV[:, gi, h * 256:(h + 1) * 256], True, True))
                        mmg(mms, [Bkw, Bv], [ubuf])
                        for hh in range(2):
                            h = pr * 2 + hh
                            stt(Cst[l][:, hf, h, :], Cst[l][:, hf, h, :], S(40 + h, 41 + h), ub[:, hh * 256:(hh + 1) * 256], ALU.mult, ALU.add,
                                [B("sm_decay"), B("Cst%d" % l)], [B("Cst%d" % l), ubuf])
                cp(Cbf[l][:], Cst[l][:], [B("Cst%d" % l)], [B("Cbf%d" % l)], eng="pool")
                mmg([(psm[:, 16 + hf * 4 + h: 17 + hf * 4 + h], kw[:, h * 256 + hf * 128: h * 256 + hf * 128 + 128], onesb[:, 0:1], True, True)
                     for hf in range(2) for h in range(4)], [Bkw, B("onesb")], [Bps])
                tt(nst[l][:], nst[l][:], S(40, 44).unsqueeze(1).to_broadcast([128, 2, 4]), ALU.mult, [B("nst%d" % l), B("sm_decay")], [B("nst%d" % l)])
                tt(nst[l][:], nst[l][:], psm[:, 16:24].rearrange("p (k h) -> p k h", k=2), ALU.add, [B("nst%d" % l)], [B("nst%d" % l), Bps])
                cp(nbf[l][:], nst[l][:], [B("nst%d" % l)], [B("nbf%d" % l)])
            else:
                mmg([(pa[0][0:16, 0:4], lastsel[:], S(20, 24), True, True), (pa[0][0:16, 4:8], lastsel[:], S(24, 28), True, True)],
                    [B("lastsel"), B("sm_winter"), B("sm_mt")], [Bpa[0]])
                cp(s16[:, 0:8], pa[0][0:16, 0:8], [], [B("s16"), Bpa[0]])
                P.dma("sp", oms[l], s16[:, 4:8], reads=[B("s16")])
                n16 = tmpscr[0:16, :]
                P.dma("sp", n16, n0[l], writes=[B("tmpscr")])
                tt(n16.rearrange("p (h d) -> p h d", h=4), n16.rearrange("p (h d) -> p h d", h=4), s16[:, 0:4].unsqueeze(2).to_broadcast([16, 4, 256]), ALU.mult,
                   [B("tmpscr"), B("s16")], [B("tmpscr")])
                for cb in range(2):
                    mmg([(pa[1][0:16, :], seqcolb[:], kw[:, cb * 512:(cb + 1) * 512], True, True)], [B("seqcolb"), Bkw], [Bpa[1]])
                    tt(n16[:, cb * 512:(cb + 1) * 512], n16[:, cb * 512:(cb + 1) * 512], pa[1][0:16, :], ALU.add, [B("tmpscr")], [B("tmpscr"), Bpa[1]])
                P.dma("sp", ons[l], n16, reads=[B("tmpscr")])

        def layer(l, tiles, kind):
            G = len(tiles)
            gis = list(range(G))
            cur["l"] = l
            cur["blk"] = 0
            cur["first"] = (tiles[0] == 1)
            P.dma("sp", lng[:], sgu_ln_g[l].to_broadcast([128, D]), writes=[B("lng")])
            P.dma("sp", lnb[:], sgu_ln_b[l].to_broadcast([128, D]), writes=[B("lnb")])
            for cb in range(2):
                pi = nxt("pm", 3)
                mmg([(pm[pi][:], selmod[32 * l:32 * l + 17, kind, :], gate17[32 * l:32 * l + 17, cb * 512:(cb + 1) * 512], True, True)], [B("selmod"), B("gate17")], [Bpm[pi]])
                cp(gbc[:, cb * 512:(cb + 1) * 512], pm[pi][:], [], [B("gbc"), Bpm[pi]], eng="act")
            if l == 0:
                pipe2(G, lambda gi: make_h_A(l, gi, kind), lambda gi: make_h_B(l, gi, kind))
            def ev_va(gi, cb, bank, bb, n):
                act(F1[:, gi, cb * 512:(cb + 1) * 512], bank[:], AF.Gelu_apprx_tanh, [], [B("F1_%d" % gi), bb])
            proj(l, 4096, 1024, gis, ev_va)
            for gi in gis:
                Bf = B("F1_%d" % gi)
                P.emit("dve", lambda e, gi=gi: e.bn_stats(out=lnst[:, 2 * gi, :], in_=F1[:, gi, 0:512]), [Bf], [B("lnst")])
                P.emit("dve", lambda e, gi=gi: e.bn_stats(out=lnst[:, 2 * gi + 1, :], in_=F1[:, gi, 512:1024]), [Bf], [B("lnst")])
                P.emit("dve", lambda e, gi=gi: e.bn_aggr(out=lnmv[:, gi, :], in_=lnst[:, 2 * gi:2 * gi + 2, :]), [B("lnst")], [B("lnmv")])
            act(lnrs[:, 0:G], lnmv[:, 0:G, 1], AF.Sqrt, [B("lnmv"), B("epsc")], [B("lnrs")], bias=epsc[:])
            P.emit("dve", lambda e: e.reciprocal(out=lnrs[:, 0:G], in_=lnrs[:, 0:G]), [B("lnrs")], [B("lnrs")])
            def sgu_A(gi):
                Bf = B("F1_%d" % gi)
                ts(F1[:, gi, :], F1[:, gi, :], lnmv[:, gi, 0:1], lnrs[:, gi:gi + 1], ALU.subtract, ALU.mult, [Bf, B("lnmv"), B("lnrs")], [Bf])
                tt(F1[:, gi, :], F1[:, gi, :], lng[:], ALU.mult, [Bf, B("lng")], [Bf])
                tt(F1[:, gi, :], F1[:, gi, :], lnb[:], ALU.add, [Bf, B("lnb")], [Bf])
                if kind == 1:
                    P.dma("sp", ovs[l], F1[:, gi, :], reads=[Bf])
                cp(VV[:, gi, :], F1[:, gi, :], [Bf], [B("VV%d" % gi)], eng="act")

            def sgu_B(gi):
                Bf = B("F1_%d" % gi)
                for cb in range(2):
                    pi = nxt("pm", 3)
                    mmg([(pm[pi][:, gg * 256:(gg + 1) * 256], WTs[:, l, kind, cb * 2 + gg, :], VV[:, gi, (cb * 2 + gg) * 256:(cb * 2 + gg + 1) * 256], True, True) for gg in range(2)],
                        [B("WTs"), B("VV%d" % gi)], [Bpm[pi]])
                    for gg in range(2):
                        g4 = cb * 2 + gg
                        act(F1[:, gi, g4 * 256:(g4 + 1) * 256], pm[pi][:, gg * 256:(gg + 1) * 256], AF.Identity, [B("bsg")], [Bf, Bpm[pi]],
                            bias=bsg[:, l, kind, g4:g4 + 1])

            def ev_q(gi, cb, bank, bb, n):
                cp(QB[:, gi, cb * 512:(cb + 1) * 512], bank[:], [], [B("QB%d" % gi), bb], eng="act")

            def ev_k(gi, cb, bank, bb, n):
                act(KB[:, gi, cb * 512:(cb + 1) * 512], bank[:], AF.Copy, [], [B("KB%d" % gi), bb], scale=1.0 / 16.0)
            steps = []
            for i in range(G + 1):
                if i < G:
                    steps.append(lambda i=i: sgu_A(i))
                if i >= 1:
                    steps.append(lambda i=i: sgu_B(i - 1))
            qpos, kpos = min(2, len(steps) - 1), min(5, len(steps))
            for j, st in enumerate(steps):
                if j == qpos:
                    proj(l, 6144, 1024, gis, ev_q)
                if j == kpos:
                    proj(l, 7168, 1024, gis, ev_k)
                st()
            if kpos >= len(steps):
                proj(l, 7168, 1024, gis, ev_k)

            def ev_mul(func):
                def ev(gi, cb, bank, bb, n):
                    t_ = tmpA[nxt("tmp", 2)]
                    act(t_[:], bank[:], func, [], [B(t_.name), bb])
                    tt(F1[:, gi, cb * 512:(cb + 1) * 512], F1[:, gi, cb * 512:(cb + 1) * 512], t_[:], ALU.mult, [B(t_.name), B("F1_%d" % gi)], [B("F1_%d" % gi)])
                return ev
            proj(l, 3072, 1024, gis, ev_mul(AF.Gelu_apprx_tanh))
            proj(l, 5120, 1024, gis, ev_mul(AF.Silu))

            def gate_proj(bi, cb):
                def ev_gate(gi, cb_, bank, bb, n):
                    act(GT[:, gi, :], bank[:], AF.Sigmoid, [], [B("GT%d" % gi), bb])
                proj(l, bi * 1024 + cb * 512, 512, gis, ev_gate)

            def branch_out(bi, scaleT=None, scale_bufs=(), gate0_done=False):
                ybufs = {}
                for gi in gis:
                    ybf, By = ybfs[gi % 2], B("ybf%d" % (gi % 2))
                    cp(ybf[:], F1[:, gi, :], [B("F1_%d" % gi)], [By], eng="act")
                    ybufs[gi] = to_T(YT, gi, ybf, [By], scaleT, scale_bufs)
                for cb in range(2):
                    if not (cb == 0 and gate0_done):
                        gate_proj(bi, cb)

                    def ev_br(gi, cb_, bank, bb, n, cb=cb):
                        mgv = MG[:, gi, cb * 512:(cb + 1) * 512]
                        if bi == 0:
                            tt(mgv, bank[:], GT[:, gi, :], ALU.mult, [B("GT%d" % gi)], [B("MG%d" % gi), bb])
                        else:
                            t_ = tmpA[nxt("tmp", 2)]
                            tt(t_[:], bank[:], GT[:, gi, :], ALU.mult, [B("GT%d" % gi)], [B(t_.name), bb])
                            tt(mgv, mgv, t_[:], ALU.add, [B(t_.name), B("MG%d" % gi)], [B("MG%d" % gi)])
                    proj(l, cb * 512, 512, gis, ev_br, src=w_br[bi][l], srcT=YT, srcbufs=ybufs)
            branch_out(0)

            def ev_v(gi, cb, bank, bb, n):
                cp(VV[:, gi, cb * 512:(cb + 1) * 512], bank[:], [], [B("VV%d" % gi), bb])

            def ev_g(gi, cb, bank, bb, n):
                tt(GP[:, gi, :], bank[:, 0:8], bifb[:, l, :], ALU.add, [B("bifb")], [B("GP%d" % gi), bb])
            proj(l, 8192, 1024, gis, ev_v)
            proj(l, 11264, 8, gis, ev_g)
            for gi in gis:
                mlstm_head(l, gi, kind)
            for gi in gis:
                mlstm_tile(l, gi, tiles[gi], kind)
            proj(l, 9216, 1024, gis, ev_mul(AF.Sigmoid))
            gate_proj(1, 0)
            for gi in gis:
                Bf = B("F1_%d" % gi)
                for h in range(4):
                    u = gi * 4 + h
                    P.emit("dve", lambda e, gi=gi, h=h, u=u: e.bn_stats(out=lnst[:, u, :], in_=F1[:, gi, h * 256:(h + 1) * 256]), [Bf], [B("lnst")])
                    P.emit("dve", lambda e, u=u: e.bn_aggr(out=lnmv[:, u, :], in_=lnst[:, u, :]), [B("lnst")], [B("lnmv")])
            act(lnrs[:, 0:4 * G], lnmv[:, 0:4 * G, 1], AF.Sqrt, [B("lnmv"), B("epsc")], [B("lnrs")], bias=epsc[:])
            P.emit("dve", lambda e: e.reciprocal(out=lnrs[:, 0:4 * G], in_=lnrs[:, 0:4 * G]), [B("lnrs")], [B("lnrs")])
            for gi in gis:
                Bf = B("F1_%d" % gi)
                for h in range(4):
                    u = gi * 4 + h
                    ts(F1[:, gi, h * 256:(h + 1) * 256], F1[:, gi, h * 256:(h + 1) * 256], lnmv[:, u, 0:1], lnrs[:, u:u + 1], ALU.subtract, ALU.mult,
                       [Bf, B("lnmv"), B("lnrs")], [Bf])
            proj(l, 10240, 1024, gis, ev_mul(AF.Silu))
            branch_out(1, mngT[:, l, :], [B("mngT")], gate0_done=True)

            def ev_p(gi, cb, bank, bb, n):
                cp(F1[:, gi, cb * 512:(cb + 1) * 512], bank[:], [], [B("F1_%d" % gi), bb], eng="act")
            proj(l, 11272, 1024, gis, ev_p)
            wpl, Bwpl = wload(w_pool[l].rearrange("g (k p) o -> p (g k) o", p=128), 256, pre=True)
            gate_proj(2, 0)
            if kind == 1:
                P.dma("pool", pbufA[0:120, :], pool0[l, 0], writes=[B("pbufA")])
                P.dma("pool", pbufB[0:120, :], pool0[l, 1], writes=[B("pbufB")])
            def pool_A(gi):
                Bf = B("F1_%d" % gi)
                tile = tiles[gi]
                ybf, By = ybfs[gi % 2], B("ybf%d" % (gi % 2))
                xnb, Bx = xnbs[gi % 2], B("xnb%d" % (gi % 2))
                QTp, Bqt = (QT, B("QT")) if gi % 2 == 0 else (KT, B("KT"))
                if gi == 0:
                    prv, Bprv = pprev[l], B("pprev%d" % l)
                else:
                    prv, Bprv = ybfs[(gi - 1) % 2], B("ybf%d" % ((gi - 1) % 2))
                cp(ybf[:], F1[:, gi, :], [Bf], [By], eng="act")
                if kind == 1:
                    for b in range(16):
                        P.dma("sp", obs[l, b, 7:15, :], F1[b * 8:(b + 1) * 8, gi, :], reads=[Bf])
                    P.dma("sp", obs[l][:, 0:7, :], pool0raw[l][:, 8:15, :])
                elif tile == 16:
                    P.dma("sp", obp[l], F1[113:128, gi, :], reads=[Bf])
                rci = 0 if kind == 1 else (1 if tile == 1 else 2)
                for cb in range(2):
                    pi = nxt("pm", 3)
                    mms = []
                    rds = [B("bandcur"), By]
                    for gg in range(2):
                        wi = cb * 2 + gg
                        cs = slice(wi * 256, (wi + 1) * 256)
                        o = pm[pi][:, gg * 256:(gg + 1) * 256]
                        if kind == 0:
                            if tile == 1:
                                mms.append((o, bandcur[:, 0, wi, :], ybf[:, cs], True, True))
                            else:
                                mms.append((o, bandcur[:, 0, wi, :], ybf[:, cs], True, False))
                                mms.append((o, bandprev[:, wi, :], prv[:, cs], False, True))
                                rds += [B("bandprev"), Bprv]
                        else:
                            mms.append((o, bandcur[:, 1, wi, :], ybf[:, cs], True, False))
                            mms.append((o, bandbuf[0:120, 0, wi, :], pbufA[0:120, cs], False, False))
                            mms.append((o, bandbuf[0:120, 1, wi, :], pbufB[0:120, cs], False, True))
                            rds += [B("bandbuf"), B("pbufA"), B("pbufB")]
                    mmg(mms, rds, [Bpm[pi]])
                    for gg in range(2):
                        wi = cb * 2 + gg
                        cs = slice(wi * 256, (wi + 1) * 256)
                        stt(xnb[:, cs], pm[pi][:, gg * 256:(gg + 1) * 256], rc[:, rci, wi:wi + 1], F1[:, gi, cs], ALU.mult, ALU.subtract,
                            [B("rc"), Bf], [Bx, Bpm[pi]])

            def pool_B(gi):
                Bf = B("F1_%d" % gi)
                tile = tiles[gi]
                ybf, By = ybfs[gi % 2], B("ybf%d" % (gi % 2))
                xnb, Bx = xnbs[gi % 2], B("xnb%d" % (gi % 2))
                QTp, Bqt = (QT, B("QT")) if gi % 2 == 0 else (KT, B("KT"))
                if kind == 0 and gi == G - 1:
                    cp(pprev[l][:], ybf[:], [By], [B("pprev%d" % l)], eng="act")
                pi = nxt("pt", 2)
                transposes(pi, lambda c, xnb=xnb: xnb[:, c * 128:(c + 1) * 128], 8, [Bx])
                cp(QTp[:], pt[pi][:], [], [Bqt, Bpt[pi]])
                for cb in range(2):
                    pj = nxt("pm", 3)
                    mmg([(pm[pj][:, gg * 256:(gg + 1) * 256], QTp[:, (cb * 2 + gg) * 2 + k, :], wpl[:, (cb * 2 + gg) * 2 + k, 0:256], k == 0, k == 1) for gg in range(2) for k in range(2)],
                        [Bqt, Bwpl], [Bpm[pj]])
                    cp(F1[:, gi, cb * 512:(cb + 1) * 512], pm[pj][:], [], [Bf, Bpm[pj]], eng="act")
            pipe2(G, pool_A, pool_B)
            proj(l, 12296, 1024, gis, ev_mul(AF.Silu))
            branch_out(2, pscT[:, l, :], [B("pscT")], gate0_done=True)

            mbufs = {}
            for gi in gis:
                ybf, By = ybfs[gi % 2], B("ybf%d" % (gi % 2))
                cp(ybf[:], MG[:, gi, :], [B("MG%d" % gi)], [By], eng="act")
                mbufs[gi] = to_T(YT, gi, ybf, [By])

            if l == 1:
                P.dma("sp", lng[:], fin_g.to_broadcast([128, D]), writes=[B("lng")])

            def ev_out(gi, cb, bank, bb, n):
                t_ = tmpA[nxt("tmp", 2)]
                tt(t_[:], bank[:], gbc[:, cb * 512:(cb + 1) * 512], ALU.mult, [B("gbc")], [B(t_.name), bb])
                xv = xg[:, gi, cb * 512:(cb + 1) * 512]
                tt(xv, xv, t_[:], ALU.add, [B(t_.name), B("xg%d" % gi)], [B("xg%d" % gi)])
                if cb != 1:
                    return
                if l == 0:
                    make_h_A(1, gi, kind)
                    if gi >= 1:
                        make_h_B(1, gi - 1, kind)
                    if gi == G - 1:
                        make_h_B(1, gi, kind)
                else:
                    xb_ = B("xg%d" % gi)
                    rstd, Br = rms_stats(xg[:, gi, :], [xb_], gi)
                    stt(MG[:, gi, :], xg[:, gi, :], rstd, lng[:], ALU.mult, ALU.mult, [xb_, Br, B("lng")], [B("MG%d" % gi)])
                    P.dma("sp", y[tiles[gi]], MG[:, gi, :], reads=[B("MG%d" % gi)])
            proj(l, 0, 1024, gis, ev_out, src=w_out[l], srcT=YT, srcbufs=mbufs)

        def prompt_state_out():
            for l in range(2):
                for k in range(2):
                    P.dma("sp", oCp[l][:, k * 128:(k + 1) * 128, :].rearrange("h p e -> p h e"), Cst[l][:, k], reads=[B("Cst%d" % l)])
                    P.dma("sp", onp[l][:, k * 128:(k + 1) * 128].rearrange("h p -> p h"), nst[l][:, k, :], reads=[B("nst%d" % l)], allow_slow_non_contiguous=True)
                P.dma("sp", omp[l:l + 1, :], mrep[l][0:1, :], reads=[B("mrep%d" % l)])
            fb = ["Cst0", "Cst1", "Cbf0", "Cbf1", "pprev0", "pprev1", "QB1", "QB2", "ZQ0", "ZQ1", "KWb0", "KWb1", "C0f0", "C0f1", "C0f2",
                  "C0b0", "C0b1", "C0b2", "Cstage0", "Cstage1", "pbufA", "pbufB", "xg1", "xg2", "xg3", "F1_1", "F1_2", "F1_3", "MG1", "MG2", "MG3",
                  "VV1", "VV2", "VV3", "GT1", "GT2", "GT3", "KB1", "KB2", "wr0", "wr1", "wr2", "wr3", "wr4"]
            P.emit("pool", lambda e: e.memset(fence_t[:], 0.0), [], [B(n) for n in fb])

        for tiles in GROUPS:
            kind = 1 if tiles[0] == 0 else 0
            if kind == 1:
                prompt_state_out()
                cur["nw"] = len(wring_x)
            for gi, tile in enumerate(tiles):
                P.dma("sp", xg[:, gi, :], xin[tile], writes=[B("xg%d" % gi)])
            for l in range(2):
                layer(l, tiles, kind)
        P.finish()
        with nc.Block() as block:
            P.replay(block)
    return nc


def _host_inputs(inp, core):
    f = lambda a: np.ascontiguousarray(a, dtype=np.float32)
    bs = slice(core * 16, core * 16 + 16)
    m = {}
    xs = inp["x_sample"][bs].reshape(1, 128, D)
    xp = inp["x_prompt"][core].reshape(16, 128, D)
    m["xin"] = f(np.concatenate([xs, xp], axis=0))
    m["cc"] = f(np.concatenate([inp["c_prompt"][core:core + 1], inp["c_sample"][bs]], axis=0))
    m["w_mod"] = f(inp["w_mod"])
    bm = np.asarray(inp["b_mod"])
    m["b_modT"] = f(bm[:, :2048].reshape(2, 16, 128).transpose(0, 2, 1))
    m["b_modg"] = f(bm[:, 2048:].reshape(2, 1, D))
    tT = lambda a: f(np.asarray(a).reshape(2, 8, 128).transpose(0, 2, 1))
    m["norm_gT"] = tT(inp["norm_g"])
    m["w_in"] = f(inp["w_in"])
    m["b_if"] = f(np.asarray(inp["b_if"]).reshape(2, 1, 8))
    m["sgu_ln_g"] = f(np.asarray(inp["sgu_ln_g"]).reshape(2, 1, D))
    m["sgu_ln_b"] = f(np.asarray(inp["sgu_ln_b"]).reshape(2, 1, D))
    ws = np.asarray(inp["w_sgu"])
    wsg = np.zeros((2, 2, 4, 128, 128), np.float32)
    wsg[:, 0] = ws
    for b in range(16):
        wsg[:, 1, :, b * 8:(b + 1) * 8, b * 8:(b + 1) * 8] = ws[:, :, :8, :8]
    m["wsgu"] = wsg
    bsu = np.asarray(inp["b_sgu"])
    bsg = np.zeros((2, 2, 128, 4), np.float32)
    bsg[:, 0] = bsu.transpose(0, 2, 1)
    bsg[:, 1] = np.tile(bsu[:, :, :8], (1, 1, 16)).transpose(0, 2, 1)
    m["bsgu"] = bsg
    m["mnorm_gT"] = tT(inp["mlstm_norm_g"])
    m["w_pool"] = f(inp["w_pool"])
    m["pscaleT"] = tT(inp["pool_scale"])
    m["w_br_a"] = f(inp["w_br_a"])
    m["w_br_b"] = f(inp["w_br_b"])
    m["w_br_c"] = f(inp["w_br_c"])
    m["w_out"] = f(inp["w_out"])
    m["fin_g"] = f(np.asarray(inp["final_norm_g"]).reshape(1, D))
    m["C0"] = f(inp["state_mlstm_C"][:, bs])
    n0 = np.asarray(inp["state_mlstm_n"])[:, bs].reshape(2, 16, D)
    m["n0"] = f(n0)
    m["n0tok"] = f(np.repeat(n0, 8, axis=1))
    m["m0tok"] = f(np.repeat(np.asarray(inp["state_mlstm_m"])[:, bs], 8, axis=1))
    sp = np.asarray(inp["state_pool"])[:, bs]
    m["pool0"] = f(sp.reshape(2, 2, 120, D))
    m["pool0raw"] = f(sp)
    global _CONSTS
    if _CONSTS is None:
        _CONSTS = make_consts()
    for k, v in _CONSTS.items():
        m["c_" + k] = v
    return m


_NC = None


def kernel(**inputs):
    global _NC
    inp = {k: np.asarray(v) for k, v in inputs.items()}
    if _NC is None:
        _NC = build_nc()
    in_maps = [_host_inputs(inp, c) for c in range(8)]
    res = run_bass_kernel_spmd(_NC, in_maps, core_ids=list(range(8)))
    R = res.results
    y_prompt = np.stack([r["y"][1:].reshape(2048, D) for r in R])
    y_sample = np.concatenate([r["y"][0].reshape(16, 8, D) for r in R], axis=0)
    Cp = np.stack([r["oCp"] for r in R], axis=1)
    npp = np.stack([r["onp"] for r in R], axis=1)
    mp = np.stack([r["omp"] for r in R], axis=1)
    bp = np.stack([r["obp"] for r in R], axis=1)
    Cs = np.concatenate([r["oCs"] for r in R], axis=1)
    ns = np.concatenate([r["ons"].reshape(2, 16, 4, 256) for r in R], axis=1)
    ms = np.concatenate([r["oms"] for r in R], axis=1)
    bsn = np.concatenate([r["obs"] for r in R], axis=1)
    vs = np.concatenate([r["ovs"].reshape(2, 16, 8, D) for r in R], axis=1)
    outs = (y_prompt, y_sample, Cp, npp, mp, bp, Cs, ns, ms, bsn, vs)
    return tuple(np.ascontiguousarray(o, dtype=np.float32) for o in outs)
```

```python
import numpy as np
import concourse.bass as bass
import concourse.mybir as mybir
from concourse.bass_utils import run_bass_kernel_spmd
from contextlib import ExitStack

F32 = mybir.dt.float32
BF16 = mybir.dt.bfloat16
AF = mybir.ActivationFunctionType
ALU = mybir.AluOpType
AX = mybir.AxisListType

D = 1024
NT = 17
NIN = 13320
EPS = 1e-6
WINS = (2, 4, 8, 16)
GROUPS = [[4 * i + 1, 4 * i + 2, 4 * i + 3, 4 * i + 4] for i in range(4)] + [[0]]
GMAX = 4
NW = 2
NBLK = 40


class Buf:
    __slots__ = ("name", "w", "r")

    def __init__(self, name):
        self.name = name
        self.w = None
        self.r = {}


class Prog:
    ENGS = ("pe", "act", "dve", "pool", "sp")

    def __init__(self, nc, es, n_dsem=24, n_fixed=12, n_sw=8):
        self.nc = nc
        self.ops = {e: [] for e in self.ENGS}
        self.sem = {e: es.enter_context(nc.semaphore("s_" + e)) for e in self.ENGS}
        self.cnt = {e: 0 for e in self.ENGS}
        self.waited = {e: {} for e in self.ENGS}
        self.dsem = [es.enter_context(nc.semaphore("d%d" % i)) for i in range(n_dsem + n_fixed + n_sw)]
        self.dcnt = [0] * (n_dsem + n_fixed + n_sw)
        self.drr = 0
        self.swrr = 0
        self.n_rr = n_dsem
        self.n_fixed = n_fixed
        self.n_sw = n_sw
        self.bufs = {}

    def buf(self, name):
        b = self.bufs.get(name)
        if b is None:
            b = Buf(name)
            self.bufs[name] = b
        return b

    def _semof(self, k):
        return self.sem[k] if isinstance(k, str) else self.dsem[k[1]]

    def _deps(self, eng, reads, writes, extra=()):
        deps = {}

        def add(k, v):
            if deps.get(k, 0) < v:
                deps[k] = v
        for b in reads:
            if b.w is not None:
                add(*b.w)
        for b in writes:
            if b.w is not None:
                add(*b.w)
            for k, v in b.r.items():
                add(k, v)
        for k, v in extra:
            add(k, v)
        waits = []
        wd = self.waited[eng]
        for k, v in deps.items():
            if wd.get(k, 0) >= v:
                continue
            wd[k] = v
            waits.append((k, v))
        return waits

    def _mark(self, tok, reads, writes):
        k, v = tok
        for b in reads:
            if b.r.get(k, 0) < v:
                b.r[k] = v
        for b in writes:
            b.w = tok
            b.r = {}

    def emit(self, eng, fn, reads=(), writes=()):
        waits = self._deps(eng, reads, writes)
        self.cnt[eng] += 1
        tok = (eng, self.cnt[eng])
        self._mark(tok, reads, writes)
        self.ops[eng].append((waits, fn, None))

    def dma(self, q, out, in_, reads=(), writes=(), dsem=None, **kw):
        if dsem is not None:
            i = self.n_rr + dsem
        elif q == "pool":
            i = self.n_rr + self.n_fixed + self.swrr
            self.swrr = (self.swrr + 1) % self.n_sw
        else:
            i = self.drr
            self.drr = (self.drr + 1) % self.n_rr
        extra = [(("d", i), self.dcnt[i])] if self.dcnt[i] > 0 else []
        waits = self._deps(q, reads, writes, extra)
        self.dcnt[i] += 16
        tok = (("d", i), self.dcnt[i])
        self._mark(tok, reads, writes)

        def fn(e, out=out, in_=in_, kw=kw):
            return e.dma_start(out=out, in_=in_, **kw)
        self.ops[q].append((waits, fn, i))

    def finish(self):
        waits = [(("d", i), c) for i, c in enumerate(self.dcnt) if c > 0]
        waits += [(e, self.cnt[e]) for e in self.ENGS if e != "sp" and self.cnt[e] > 0]
        self.ops["sp"].append((waits, None, None))

    def replay(self, block):
        prog = self

        def run(ename, e):
            for waits, fn, di in prog.ops[ename]:
                for k, v in waits:
                    e.wait_ge(prog._semof(k), v)
                if fn is None:
                    continue
                ins = fn(e)
                if di is None:
                    ins.then_inc(prog.sem[ename], 1)
                else:
                    ins.then_inc(prog.dsem[di], 16)

        @block.tensor
        def _(e):
            run("pe", e)

        @block.scalar
        def _(e):
            run("act", e)

        @block.vector
        def _(e):
            run("dve", e)

        @block.gpsimd
        def _(e):
            run("pool", e)

        @block.sync
        def _(e):
            run("sp", e)


def make_consts():
    t = np.arange(128)
    c = {}
    c["identf"] = np.eye(128, dtype=np.float32)
    c["onesf"] = np.ones((128, 128), np.float32)
    seq = t // 8
    valid = []
    for kind in range(2):
        v = (t[:, None] <= t[None, :])
        if kind == 1:
            v = v & (seq[:, None] == seq[None, :])
        valid.append(v)
    c["tri"] = np.stack([v.astype(np.float32) for v in valid])
    c["maskneg"] = np.stack([np.where(v.T, 0.0, -1e30).astype(np.float32) for v in valid])
    c["bigmask"] = np.stack([np.where(v, 0.0, 1e30).astype(np.float32) for v in valid])
    last = [np.full(128, 127), seq * 8 + 7]
    c["sellast"] = np.stack([(t[None, :] == last[k][:, None]).astype(np.float32) for k in range(2)])
    c["seqcol"] = (seq[:, None] == np.arange(16)[None, :]).astype(np.float32)
    c["lastsel"] = (t[:, None] == (np.arange(16) * 8 + 7)[None, :]).astype(np.float32)
    selmod = np.zeros((2, 17, 128), np.float32)
    selmod[0, 0, :] = 1.0
    selmod[1, 1 + seq, t] = 1.0
    c["selmod"] = selmod
    bandcur = np.zeros((2, 4, 128, 128), np.float32)
    bandprev = np.zeros((4, 128, 128), np.float32)
    bandbuf = np.zeros((2, 4, 128, 128), np.float32)
    for wi, w in enumerate(WINS):
        s = t[:, None]
        tt = t[None, :]
        bandcur[0, wi] = ((s <= tt) & (s > tt - w))
        bandcur[1, wi] = ((s <= tt) & (s > tt - w) & (seq[:, None] == seq[None, :]))
        bandprev[wi] = ((s - 128) > (tt - w))
        for ab in range(2):
            for bb in range(8):
                for r in range(15):
                    row = bb * 15 + r
                    for i in range(8):
                        tok = (ab * 8 + bb) * 8 + i
                        if (r - 15) > (i - w):
                            bandbuf[ab, wi, row, tok] = 1.0
    c["bandcur"] = bandcur
    c["bandprev"] = bandprev
    c["bandbuf"] = bandbuf
    rc = np.zeros((3, 128, 4), np.float32)
    for wi, w in enumerate(WINS):
        rc[0, :, wi] = 1.0 / w
        rc[1, :, wi] = 1.0 / np.minimum(t + 1, w)
        rc[2, :, wi] = 1.0 / w
    c["rc"] = rc
    return c


_CONSTS = None


def build_nc():
    nc = bass.Bass("TRN2", target_bir_lowering=False)
    es = ExitStack()
    with es:
        din = lambda name, shape: nc.dram_tensor(name, list(shape), F32, kind="ExternalInput").ap()
        dout = lambda name, shape: nc.dram_tensor(name, list(shape), F32, kind="ExternalOutput").ap()
        xin = din("xin", (NT, 128, D))
        cc = din("cc", (17, D))
        w_mod = din("w_mod", (2, D, 3 * D))
        b_modT = din("b_modT", (2, 128, 16))
        b_modg = din("b_modg", (2, 1, D))
        norm_gT = din("norm_gT", (2, 128, 8))
        w_in = din("w_in", (2, D, NIN))
        b_if = din("b_if", (2, 1, 8))
        sgu_ln_g = din("sgu_ln_g", (2, 1, D))
        sgu_ln_b = din("sgu_ln_b", (2, 1, D))
        wsgu = din("wsgu", (2, 2, 4, 128, 128))
        bsgu = din("bsgu", (2, 2, 128, 4))
        mnorm_gT = din("mnorm_gT", (2, 128, 8))
        w_pool = din("w_pool", (2, 4, 256, 256))
        pscaleT = din("pscaleT", (2, 128, 8))
        w_br = [din("w_br_a", (2, D, D)), din("w_br_b", (2, D, D)), din("w_br_c", (2, D, D))]
        w_out = din("w_out", (2, D, D))
        fin_g = din("fin_g", (1, D))
        C0 = din("C0", (2, 16, 4, 256, 256))
        n0 = din("n0", (2, 16, D))
        n0tok = din("n0tok", (2, 128, D))
        m0tok = din("m0tok", (2, 128, 4))
        pool0 = din("pool0", (2, 2, 120, D))
        pool0raw = din("pool0raw", (2, 16, 15, D))
        cst = {k: din("c_" + k, v.shape) for k, v in make_consts().items()}
        y = dout("y", (NT, 128, D))
        oCp = dout("oCp", (2, 4, 256, 256))
        onp = dout("onp", (2, 4, 256))
        omp = dout("omp", (2, 4))
        obp = dout("obp", (2, 15, D))
        oCs = dout("oCs", (2, 16, 4, 256, 256))
        ons = dout("ons", (2, 16, D))
        oms = dout("oms", (2, 16, 4))
        obs = dout("obs", (2, 16, 15, D))
        ovs = dout("ovs", (2, 128, D))

        P = Prog(nc, es)
        B = P.buf
        sb = lambda name, shape, dt=F32: es.enter_context(nc.sbuf_tensor(name, list(shape), dt))
        ps = lambda name, shape, dt=F32: es.enter_context(nc.psum_tensor(name, list(shape), dt))

        xg = sb("xg", (128, GMAX, D))
        hT = sb("hT", (128, 8, GMAX * 128), BF16)
        F1 = sb("F1", (128, GMAX, D))
        VV = sb("VV", (128, GMAX, D), BF16)
        GT = sb("GT", (128, GMAX, 512))
        MG = sb("MG", (128, GMAX, D))
        QB = sb("QB", (128, GMAX, D), BF16)
        KB = sb("KB", (128, GMAX, D), BF16)
        GP = sb("GP", (128, GMAX, 8))
        YT = sb("YT", (128, 8, GMAX * 128), BF16)
        wring = [sb("wr%d" % i, (128, 8, 512), BF16) for i in range(NW)]
        tmpA = [sb("tmpA%d" % i, (128, 512)) for i in range(2)]
        xnbs = [sb("xnb%d" % i, (128, D), BF16) for i in range(2)]
        ybfs = [sb("ybf%d" % i, (128, D), BF16) for i in range(2)]
        Cst = [sb("Cst%d" % l, (128, 2, 4, 256)) for l in range(2)]
        Cbf = [sb("Cbf%d" % l, (128, 2, 4, 256), BF16) for l in range(2)]
        nst = [sb("nst%d" % l, (128, 2, 4)) for l in range(2)]
        nbf = [sb("nbf%d" % l, (128, 2, 4), BF16) for l in range(2)]
        mrep = [sb("mrep%d" % l, (128, 4)) for l in range(2)]
        pprev = [sb("pprev%d" % l, (128, D), BF16) for l in range(2)]
        QT = sb("QT", (128, 8, 128), BF16)
        KT = sb("KT", (128, 8, 128), BF16)
        QsT = sb("QsT", (128, 8, 128), BF16)
        SWT = sb("SWT", (128, 512), BF16)
        WTt = sb("WTt", (128, 512))
        DG = sb("DG", (128, 512))
        AM = sb("AM", (128, 512))
        sm = sb("sm", (128, 64))
        ZQ = [Cbf[0][:, i].rearrange("p h (a t) -> p (h a) t", a=2) for i in range(2)]
        KWb = [Cbf[1][:, i].rearrange("p h e -> p (h e)") for i in range(2)]
        t4 = lambda ap: ap.rearrange("p (h e) -> p h e", h=4)
        C0f = [t4(xg[:, 3, :]), t4(F1[:, 3, :]), t4(MG[:, 3, :])]
        C0b = [t4(VV[:, 1, :]), t4(VV[:, 2, :]), t4(VV[:, 3, :])]
        Cstage = [t4(GT[:, 1:3, :].rearrange("p a d -> p (a d)")), t4(GT[:, 3, :].rearrange("p d -> p d") if False else KB[:, 1:3, :].rearrange("p a d -> p (a d)").bitcast(F32))]
        fence_t = sb("fence_t", (128, 1))
        DEC = sb("DEC", (128, 64))
        WL = sb("WL", (128, 64))
        s16 = sb("s16", (16, 16))
        pbufA = QB[:, 1, :]
        pbufB = QB[:, 2, :]
        identf = sb("identf", (128, 128))
        identb = sb("identb", (128, 128), BF16)
        onesf = sb("onesf", (128, 128))
        onesb = sb("onesb", (128, 128), BF16)
        tri = sb("tri", (128, 2, 128))
        maskneg = sb("maskneg", (128, 2, 128))
        bigmask = sb("bigmask", (128, 2, 128))
        sellast = sb("sellast", (128, 2, 128))
        seqcol = sb("seqcol", (128, 16))
        seqcolb = sb("seqcolb", (128, 16), BF16)
        lastsel = sb("lastsel", (128, 16))
        selmod = sb("selmod", (49, 2, 128))
        bandcur = sb("bandcur", (128, 2, 4, 128), BF16)
        bandprev = sb("bandprev", (128, 4, 128), BF16)
        bandbuf = sb("bandbuf", (128, 2, 4, 128), BF16)
        rc = sb("rc", (128, 3, 4))
        WTs = sb("WTs", (128, 2, 2, 4, 128), BF16)
        bsg = sb("bsg", (128, 2, 2, 4))
        lng = sb("lng", (128, D))
        lnb = sb("lnb", (128, D))
        bifb = sb("bifb", (128, 2, 8))
        nrmgT = sb("nrmgT", (128, 2, 8))
        mngT = sb("mngT", (128, 2, 8))
        pscT = sb("pscT", (128, 2, 8))
        bmT = sb("bmT", (128, 2, 16))
        AT = sb("AT", (128, 2, 8, 17))
        SH = sb("SH", (128, 2, 8, 17))
        gate17 = sb("gate17", (49, D))
        gbc = sb("gbc", (128, D))
        ccT = sb("ccT", (128, 8, 17), BF16)
        epsc = sb("epsc", (128, 1))
        tmpscr = sb("tmpscr", (128, D))
        lnst = sb("lnst", (128, 16, 6))
        lnmv = sb("lnmv", (128, 16, 2))
        lnrs = sb("lnrs", (128, 16))
        rs = sb("rs", (128, 4, 4))
        sm2 = sb("sm2", (128, 4, 12))
        pm = [ps("pm%d" % i, (128, 512)) for i in range(3)]
        pt = [ps("pt%d" % i, (128, 8, 128), BF16) for i in range(2)]
        pa = [ps("pa%d" % i, (128, 512)) for i in range(3)]
        ptf = [pt[i][:].rearrange("p c t -> p (c t)").bitcast(F32) for i in range(2)]
        Bpm = [B("pm%d" % i) for i in range(3)]
        Bpt = [B("pt%d" % i) for i in range(2)]
        Bpa = [B("pa%d" % i) for i in range(3)]
        rr = {"pm": 0, "pt": 0, "w": 0, "tmp": 0}

        def nxt(key, n):
            i = rr[key]
            rr[key] = (i + 1) % n
            return i

        def act(out, in_, func, reads, writes, **kw_):
            P.emit("act", lambda e: e.activation(out=out, in_=in_, func=func, **kw_), reads, writes)

        def tt(out, in0, in1, op, reads, writes, eng="dve"):
            P.emit(eng, lambda e: e.tensor_tensor(out=out, in0=in0, in1=in1, op=op), reads, writes)

        def ts(out, in0, s1, s2, op0, op1, reads, writes, eng="dve"):
            if op1 is None:
                P.emit(eng, lambda e: e.tensor_scalar(out=out, in0=in0, scalar1=s1, scalar2=None, op0=op0), reads, writes)
            else:
                P.emit(eng, lambda e: e.tensor_scalar(out=out, in0=in0, scalar1=s1, scalar2=s2, op0=op0, op1=op1), reads, writes)

        def stt(out, in0, scalar, in1, op0, op1, reads, writes):
            P.emit("dve", lambda e: e.scalar_tensor_tensor(out=out, in0=in0, scalar=scalar, in1=in1, op0=op0, op1=op1), reads, writes)

        def cp(out, in_, reads, writes, eng="dve"):
            if eng == "act":
                P.emit("act", lambda e: e.activation(out=out, in_=in_, func=AF.Copy), reads, writes)
            else:
                P.emit(eng, lambda e: e.tensor_copy(out=out, in_=in_), reads, writes)

        def red(out, in_, op, reads, writes):
            P.emit("dve", lambda e: e.tensor_reduce(out=out, in_=in_, axis=AX.X, op=op), reads, writes)

        def mmg(mms, reads, writes):
            def fn(e, mms=mms):
                for (o, l, r, st, sp) in mms:
                    ins = e.matmul(o, lhsT=l, rhs=r, start=st, stop=sp)
                return ins
            P.emit("pe", fn, reads, writes)

        def transposes(pti, src_fn, n, reads):
            def fn(e):
                for c in range(n):
                    ins = e.transpose(out=pt[pti][:, c, :], in_=src_fn(c), identity=identb[:])
                return ins
            P.emit("pe", fn, list(reads) + [B("identb")], [Bpt[pti]])

        def bcast3(ap2, n):
            return ap2.unsqueeze(2).to_broadcast([ap2.shape[0], ap2.shape[1], n])

        def ld(dst, src, name, q="sp"):
            P.dma(q, dst, src, writes=[B(name)])
        ld(identf[:], cst["identf"], "identf")
        ld(onesf[:], cst["onesf"], "onesf")
        ld(identb[:], cst["identf"], "identb", "pool")
        ld(onesb[:], cst["onesf"], "onesb", "pool")
        ld(tri[:], cst["tri"].rearrange("k s t -> s k t"), "tri")
        ld(maskneg[:], cst["maskneg"].rearrange("k s t -> s k t"), "maskneg")
        ld(bigmask[:], cst["bigmask"].rearrange("k s t -> s k t"), "bigmask")
        ld(sellast[:], cst["sellast"].rearrange("k s t -> s k t"), "sellast")
        ld(seqcol[:], cst["seqcol"], "seqcol")
        ld(seqcolb[:], cst["seqcol"], "seqcolb", "pool")
        ld(lastsel[:], cst["lastsel"], "lastsel")
        ld(selmod[0:17], cst["selmod"].rearrange("k r t -> r k t"), "selmod")
        ld(selmod[32:49], cst["selmod"].rearrange("k r t -> r k t"), "selmod")
        ld(bandcur[:].rearrange("p k w t -> p (k w) t"), cst["bandcur"].rearrange("k w s t -> s (k w) t"), "bandcur", "pool")
        ld(bandprev[:], cst["bandprev"].rearrange("w s t -> s w t"), "bandprev", "pool")
        ld(bandbuf[:].rearrange("p k w t -> p (k w) t"), cst["bandbuf"].rearrange("k w s t -> s (k w) t"), "bandbuf", "pool")
        ld(rc[:], cst["rc"].rearrange("k t w -> t k w"), "rc")
        ld(bsg[:].rearrange("p l k g -> p (l k) g"), bsgu.rearrange("l k p g -> p (l k) g"), "bsg")
        for l in range(2):
            ld(bifb[:, l, :], b_if[l].to_broadcast([128, 8]), "bifb")
        ld(nrmgT[:], norm_gT.rearrange("l p c -> p l c"), "nrmgT")
        ld(mngT[:], mnorm_gT.rearrange("l p c -> p l c"), "mngT")
        ld(pscT[:], pscaleT.rearrange("l p c -> p l c"), "pscT")
        ld(bmT[:], b_modT.rearrange("l p c -> p l c"), "bmT")
        P.emit("pool", lambda e: e.memset(epsc[:], EPS), writes=[B("epsc")])
        for l in range(2):
            P.emit("pool", lambda e, l=l: e.memset(Cst[l][:], 0.0), writes=[B("Cst%d" % l)])
            P.emit("pool", lambda e, l=l: e.memset(Cbf[l][:], 0.0), writes=[B("Cbf%d" % l)])
            P.emit("pool", lambda e, l=l: e.memset(nst[l][:], 0.0), writes=[B("nst%d" % l)])
            P.emit("pool", lambda e, l=l: e.memset(nbf[l][:], 0.0), writes=[B("nbf%d" % l)])
            P.emit("pool", lambda e, l=l: e.memset(mrep[l][:], 0.0), writes=[B("mrep%d" % l)])
            P.emit("pool", lambda e, l=l: e.memset(pprev[l][:], 0.0), writes=[B("pprev%d" % l)])

        wscr = nc.dram_tensor("wscr", [2, NBLK, 128, 8, 512], BF16, kind="Internal").ap()
        cur = {"l": None, "blk": 0, "first": True, "nw": NW}
        def _slot(t3):
            return t3.rearrange("p a d -> p (a d)").bitcast(BF16).rearrange("p (k n) -> p k n", k=8)
        wring_x = list(wring) + [_slot(MG[:, 1:3, :]), _slot(xg[:, 1:3, :]), _slot(F1[:, 1:3, :])]

        def wload(src_ap, ncols, pre=False):
            i = rr["w"]
            rr["w"] = (i + 1) % cur["nw"]
            slot, bslot = wring_x[i], B("wr%d" % i)
            if cur["l"] is None:
                P.dma("pool", slot[:, :, 0:ncols], src_ap.rearrange("(k p) n -> p k n", p=128), writes=[bslot], dsem=NW + i)
                return slot, bslot
            l_, blk = cur["l"], cur["blk"]
            cur["blk"] += 1
            assert blk < NBLK
            bscr = B("wscr_%d_%d" % (l_, blk))
            if cur["first"]:
                src = src_ap if pre else src_ap.rearrange("(k p) n -> p k n", p=128)
                P.dma("pool", slot[:, :, 0:ncols], src, writes=[bslot], dsem=NW + i)
                P.dma("sp", wscr[l_, blk][:, :, 0:ncols], slot[:, :, 0:ncols], reads=[bslot], writes=[bscr])
            else:
                P.dma("sp", slot[:, :, 0:ncols], wscr[l_, blk][:, :, 0:ncols], reads=[bscr], writes=[bslot], dsem=(i if i < NW else 2 * NW + i))
            return slot, bslot

        for l in range(2):
            for kind in range(2):
                for g in range(4):
                    t_ = tmpA[nxt("tmp", 2)]
                    bt = B(t_.name)
                    P.dma("sp", t_[:, 0:128], wsgu[l, kind, g], writes=[bt])
                    pi = nxt("pm", 3)
                    P.emit("pe", lambda e, pi=pi, t_=t_: e.transpose(out=pm[pi][:, 0:128], in_=t_[:, 0:128], identity=identf[:]), [bt, B("identf")], [Bpm[pi]])
                    tt(WTs[:, l, kind, g, :], pm[pi][:, 0:128], tri[:, kind, :], ALU.mult, [B("tri")], [B("WTs"), Bpm[pi]])

        ccs = tmpscr[0:17, :]
        ccb = xnbs[0][0:17, :]
        bg17 = F1[0:17, 0, :]
        P.dma("sp", ccs, cc, writes=[B("tmpscr")])
        act(ccb, ccs, AF.Silu, [B("tmpscr")], [B("xnb0")])

        def cctr(e):
            for c in range(8):
                ins = e.transpose(out=pt[0][:, c, 0:17], in_=ccb[:, c * 128:(c + 1) * 128], identity=identb[0:17, 0:17])
            return ins
        P.emit("pe", cctr, [B("xnb0"), B("identb")], [Bpt[0]])
        cp(ccT[:], pt[0][:, :, 0:17], [], [B("ccT"), Bpt[0]])
        for l in range(2):
            bg17l = F1[32 * l:32 * l + 17, 0, :]
            P.dma("sp", bg17l, b_modg[l].to_broadcast([17, D]), writes=[B("F1_0")])
            for blk in range(6):
                wt, bw = wload(w_mod[l][:, blk * 512:(blk + 1) * 512], 512)
                if blk < 4:
                    for j in range(4):
                        ch = blk * 4 + j
                        pi = nxt("pm", 3)
                        mmg([(pm[pi][:, 0:17], wt[:, k, j * 128:(j + 1) * 128], ccT[:, k, :], k == 0, k == 7) for k in range(8)],
                            [bw, B("ccT")], [Bpm[pi]])
                        if ch < 8:
                            ts(SH[:, l, ch, :], pm[pi][:, 0:17], bmT[:, l, ch:ch + 1], None, ALU.add, None, [B("bmT")], [B("SH"), Bpm[pi]])
                        else:
                            c8 = ch - 8
                            ts(AT[:, l, c8, :], pm[pi][:, 0:17], bmT[:, l, ch:ch + 1], 1.0, ALU.add, ALU.add, [B("bmT")], [B("AT"), Bpm[pi]])
                            ts(AT[:, l, c8, :], AT[:, l, c8, :], nrmgT[:, l, c8:c8 + 1], None, ALU.mult, None, [B("nrmgT"), B("AT")], [B("AT")])
                else:
                    cb = blk - 4
                    pi = nxt("pm", 3)
                    mmg([(pm[pi][32 * l:32 * l + 17, :], ccT[:, k, :], wt[:, k, :], k == 0, k == 7) for k in range(8)], [bw, B("ccT")], [Bpm[pi]])
                    tt(gate17[32 * l:32 * l + 17, cb * 512:(cb + 1) * 512], pm[pi][32 * l:32 * l + 17, :], bg17l[:, cb * 512:(cb + 1) * 512], ALU.add, [B("F1_0")], [B("gate17"), Bpm[pi]])

        def rms_stats(src, reads, slot):
            Br = B("rs%d" % slot)
            act(tmpscr[:], src, AF.Square, reads, [B("tmpscr"), Br], accum_out=rs[:, slot, 0:1])
            act(rs[:, slot, 1:2], rs[:, slot, 0:1], AF.Sqrt, [Br, B("epsc")], [Br], scale=1.0 / D, bias=epsc[:])
            P.emit("dve", lambda e: e.reciprocal(out=rs[:, slot, 2:3], in_=rs[:, slot, 1:2]), [Br], [Br])
            return rs[:, slot, 2:3], Br

        def pipe2(n, stA, stB):
            for i in range(n + 1):
                if i < n:
                    stA(i)
                if i >= 1:
                    stB(i - 1)

        def to_T(dst, gi, src_bf, src_bufs, scale_ap=None, scale_bufs=()):
            pi = nxt("pt", 2)
            transposes(pi, lambda c: src_bf[:, c * 128:(c + 1) * 128], 8, src_bufs)
            dv = dst[:, :, gi * 128:(gi + 1) * 128]
            dbuf = B("YT_%d" % gi)
            if scale_ap is None:
                cp(dv, pt[pi][:], [], [dbuf, Bpt[pi]])
            else:
                tt(dv, pt[pi][:], bcast3(scale_ap, 128), ALU.mult, list(scale_bufs), [dbuf, Bpt[pi]])
            return dbuf

        def make_h_A(l, gi, kind):
            xb_ = B("xg%d" % gi)
            rstd, Br = rms_stats(xg[:, gi, :], [xb_], gi)
            xnb, Bx = xnbs[gi % 2], B("xnb%d" % (gi % 2))
            ts(xnb[:], xg[:, gi, :], rstd, None, ALU.mult, None, [xb_, Br], [Bx])

        def make_h_B(l, gi, kind):
            xnb, Bx = xnbs[gi % 2], B("xnb%d" % (gi % 2))
            pi = nxt("pt", 2)
            transposes(pi, lambda c: xnb[:, c * 128:(c + 1) * 128], 8, [Bx])
            hb_ = B("hT_%d" % gi)
            if kind == 0:
                for c in range(4):
                    act(hT[:, c, gi * 128:(gi + 1) * 128], pt[pi][:, c, :], AF.Identity, [B("AT"), B("SH")], [hb_, Bpt[pi]],
                        scale=AT[:, l, c, 0:1], bias=SH[:, l, c, 0:1])
                t_ = tmpA[nxt("tmp", 2)]
                tv = t_[:].rearrange("p (c t) -> p c t", c=4)
                tt(tv, pt[pi][:, 4:8, :], AT[:, l, 4:8, 0:1].to_broadcast([128, 4, 128]), ALU.mult, [B("AT")], [B(t_.name), Bpt[pi]])
                tt(hT[:, 4:8, gi * 128:(gi + 1) * 128], tv, SH[:, l, 4:8, 0:1].to_broadcast([128, 4, 128]), ALU.add, [B("SH"), B(t_.name)], [hb_])
            else:
                for c in range(8):
                    t_ = tmpA[nxt("tmp", 2)]
                    tv = t_[:, 0:128].rearrange("p (b i) -> p b i", i=8)
                    tt(tv, pt[pi][:, c, :].rearrange("p (b i) -> p b i", i=8), bcast3(AT[:, l, c, 1:17], 8), ALU.mult, [B("AT")], [B(t_.name), Bpt[pi]])
                    tt(hT[:, c, gi * 128:(gi + 1) * 128].rearrange("p (b i) -> p b i", i=8), tv, bcast3(SH[:, l, c, 1:17], 8), ALU.add, [B("SH"), B(t_.name)], [hb_])

        def proj(l, col0, ncols, tiles_gi, evac, src=None, srcT=None, srcbufs=None):
            W = w_in[l] if src is None else src
            T_ = hT if srcT is None else srcT
            nblk = (ncols + 511) // 512
            for cb in range(nblk):
                n = min(512, ncols - cb * 512)
                wt, bw = wload(W[:, col0 + cb * 512: col0 + cb * 512 + n], n)
                for gi in tiles_gi:
                    pi = nxt("pm", 3)
                    tb = (B("hT_%d" % gi) if srcbufs is None else srcbufs[gi])
                    mmg([(pm[pi][:, 0:n], T_[:, k, gi * 128:(gi + 1) * 128], wt[:, k, 0:n], k == 0, k == 7) for k in range(8)],
                        [bw, tb], [Bpm[pi]])
                    evac(gi, cb, pm[pi], Bpm[pi], n)

        def mlstm_head(l, gi, kind):
            Bg = B("GP%d" % gi)
            h4 = lambda ap: ap.rearrange("p (h s) -> p h s", h=4)
            m4 = lambda m: m[:, kind, :].unsqueeze(1).to_broadcast([128, 4, 128])
            ig = GP[:, gi, 0:4]
            fg = GP[:, gi, 4:8]
            lsp, A_, CM = sm2[:, gi, 0:4], sm2[:, gi, 4:8], sm2[:, gi, 8:12]
            Bl, Ba, Bc = B("h_lsp%d" % gi), B("h_a%d" % gi), B("h_cm%d" % gi)
            psm, Bps = pa[2], Bpa[2]
            c0 = 32 * gi
            act(lsp, fg, AF.Exp, [Bg], [Bl], scale=-1.0)
            act(lsp, lsp, AF.Ln, [Bl], [Bl], bias=1.0)
            mmg([(psm[:, c0:c0 + 4], tri[:, kind, :], lsp, True, True), (psm[:, c0 + 4:c0 + 8], onesf[:], lsp, True, True)],
                [B("tri"), B("onesf"), Bl], [Bps])
            tt(A_, ig, psm[:, c0:c0 + 4], ALU.add, [Bg], [Ba, Bps])
            tt(h4(DG[:]), identf[:].unsqueeze(1).to_broadcast([128, 4, 128]), bcast3(A_, 128), ALU.mult, [B("identf"), Ba], [B("DG")])
            mmg([(pa[0][:], onesf[:], DG[:], True, True)], [B("onesf"), B("DG")], [Bpa[0]])
            tt(h4(AM[:]), h4(pa[0][:]), m4(maskneg), ALU.add, [B("maskneg")], [B("AM"), Bpa[0]])
            red(CM, h4(AM[:]), ALU.max, [B("AM")], [Bc])

        def mlstm_tile(l, gi, tile, kind):
            first = (tile == 1)
            Bq, Bk, Bv, Bg = B("QB%d" % gi), B("KB%d" % gi), B("VV%d" % gi), B("GP%d" % gi)
            S = lambda a, b_: sm[:, a:b_]
            Qs = xnbs[gi % 2]
            kw = ybfs[gi % 2]
            BQs, Bkw = B("xnb%d" % (gi % 2)), B("ybf%d" % (gi % 2))
            h4 = lambda ap: ap.rearrange("p (h s) -> p h s", h=4)
            m4 = lambda m: m[:, kind, :].unsqueeze(1).to_broadcast([128, 4, 128])
            A_, CM = sm2[:, gi, 4:8], sm2[:, gi, 8:12]
            Ba, Bc = B("h_a%d" % gi), B("h_cm%d" % gi)
            psm, Bps = pa[2], Bpa[2]
            c0 = 32 * gi
            use_inter = not (kind == 0 and first)
            if kind == 0:
                cp(S(48, 52), mrep[l][:], [B("mrep%d" % l)], [B("sm_mprev")])
            else:
                P.dma("sp", S(48, 52), m0tok[l], writes=[B("sm_mprev")])
            tt(S(12, 16), CM, S(48, 52), ALU.max, [Bc, B("sm_mprev")], [B("sm_g")])
            tt(S(20, 24), S(48, 52), S(12, 16), ALU.subtract, [B("sm_mprev"), B("sm_g")], [B("sm_winter")])
            act(S(20, 24), S(20, 24), AF.Exp, [B("sm_winter")], [B("sm_winter")])
            pi = nxt("pt", 2)
            transposes(pi, lambda c: QB[:, gi, c * 128:(c + 1) * 128], 8, [Bq])
            cp(QT[:], pt[pi][:], [], [B("QT"), Bpt[pi]])
            pi = nxt("pt", 2)
            transposes(pi, lambda c: KB[:, gi, c * 128:(c + 1) * 128], 8, [Bk])
            cp(KT[:], pt[pi][:], [], [B("KT"), Bpt[pi]], eng="act")
            mmg([(pm[2][:, h * 128:(h + 1) * 128], KT[:, 2 * h + hf, :], QT[:, 2 * h + hf, :], hf == 0, hf == 1) for h in range(4) for hf in range(2)],
                [B("KT"), B("QT")], [Bpm[2]])
            if use_inter:
                tt(Qs[:].rearrange("p (h d) -> p h d", h=4), QB[:, gi, :].rearrange("p (h d) -> p h d", h=4), bcast3(S(20, 24), 256), ALU.mult,
                   [Bq, B("sm_winter")], [BQs])
            tt(h4(DG[:]), identf[:].unsqueeze(1).to_broadcast([128, 4, 128]), bcast3(S(12, 16), 128), ALU.mult,
               [B("identf"), B("sm_g")], [B("DG")])
            if use_inter:
                pi = nxt("pt", 2)
                transposes(pi, lambda c: Qs[:, c * 128:(c + 1) * 128], 8, [BQs])
                cp(QsT[:], pt[pi][:], [], [B("QsT"), Bpt[pi]], eng="act")
            mms = []
            for h in range(4):
                mms.append((pa[1][:, h * 128:(h + 1) * 128], onesf[:], DG[:, h * 128:(h + 1) * 128], h == 0, False))
            for h in range(4):
                mms.append((pa[1][:, h * 128:(h + 1) * 128], identf[:], bigmask[:, kind, :], False, h == 3))
            mmg(mms, [B("onesf"), B("DG"), B("identf"), B("bigmask")], [Bpa[1]])
            for h in range(4):
                act(WTt[:, h * 128:(h + 1) * 128], pa[1][:, h * 128:(h + 1) * 128], AF.Exp, [Ba], [B("WTt"), Bpa[1]],
                    scale=-1.0, bias=sm2[:, gi, 4 + h:5 + h])
            tt(SWT[:], pm[2][:], WTt[:], ALU.mult, [B("WTt")], [B("SWT"), Bpm[2]])
            tt(h4(AM[:]), h4(pa[1][:]), m4(sellast), ALU.mult, [B("sellast")], [B("AM"), Bpa[1]])
            red(S(16, 20), h4(AM[:]), ALU.add, [B("AM")], [B("sm_glast")])
            if kind == 0:
                tt(mrep[l][:], S(16, 20), psm[:, c0 + 4:c0 + 8], ALU.subtract, [B("sm_glast"), B("sm_mprev")], [B("mrep%d" % l), Bps])
                tt(S(40, 44), S(48, 52), S(16, 20), ALU.subtract, [B("sm_mprev"), B("sm_glast")], [B("sm_decay")])
                act(S(40, 44), S(40, 44), AF.Exp, [B("sm_decay")], [B("sm_decay")])
                tt(S(44, 48), A_, S(16, 20), ALU.subtract, [Ba, B("sm_glast")], [B("sm_wend")])
                act(S(44, 48), S(44, 48), AF.Exp, [B("sm_wend")], [B("sm_wend")])
                tt(kw[:].rearrange("p (h d) -> p h d", h=4), KB[:, gi, :].rearrange("p (h d) -> p h d", h=4), bcast3(S(44, 48), 256), ALU.mult,
                   [Bk, B("sm_wend")], [Bkw])
            NB = [(pm[0], Bpm[0]), (pm[1], Bpm[1]), (pa[0], Bpa[0]), (pa[1], Bpa[1])]
            for h in range(4):
                bank, bbuf = NB[h]
                o = bank[:, 0:256]
                mms = []
                rds = [B("SWT"), Bv]
                if kind == 0 and use_inter:
                    mms.append((o, SWT[:, h * 128:(h + 1) * 128], VV[:, gi, h * 256:(h + 1) * 256], True, False))
                    mms.append((o, QsT[:, 2 * h, :], Cbf[l][:, 0, h, :], False, False))
                    mms.append((o, QsT[:, 2 * h + 1, :], Cbf[l][:, 1, h, :], False, True))
                    rds += [B("QsT"), B("Cbf%d" % l)]
                elif kind == 0:
                    mms.append((o, SWT[:, h * 128:(h + 1) * 128], VV[:, gi, h * 256:(h + 1) * 256], True, True))
                else:
                    mms.append((o, SWT[:, h * 128:(h + 1) * 128], VV[:, gi, h * 256:(h + 1) * 256], True, False))
                mmg(mms, rds, [bbuf])
            if kind == 1:
                tt(S(44, 48), A_, S(16, 20), ALU.subtract, [Ba, B("sm_glast")], [B("sm_wend")])
                act(S(44, 48), S(44, 48), AF.Exp, [B("sm_wend")], [B("sm_wend")])
                tt(kw[:].rearrange("p (h d) -> p h d", h=4), KB[:, gi, :].rearrange("p (h d) -> p h d", h=4), bcast3(S(44, 48), 256), ALU.mult,
                   [Bk, B("sm_wend")], [Bkw])
                tt(WL[:].rearrange("p (b h) -> p b h", h=4), lastsel[:].unsqueeze(2).to_broadcast([128, 16, 4]), S(20, 24).unsqueeze(1).to_broadcast([128, 16, 4]), ALU.mult,
                   [B("lastsel"), B("sm_winter")], [B("WL")])
                mmg([(pm[2][:, 0:64], onesf[:], WL[:], True, True)], [B("onesf"), B("WL")], [Bpm[2]])
                cp(DEC[:], pm[2][:, 0:64], [], [B("DEC"), Bpm[2]])
                for zi in range(2):
                    P.emit("pool", lambda e, zi=zi: e.memset(ZQ[zi][:], 0.0), [], [B("ZQ%d" % zi)])
                UB = [(pm[2], Bpm[2]), (ptf[0], Bpt[0]), (ptf[1], Bpt[1])]
                rnd = 0
                for b in range(16):
                    zi = b % 2
                    if b >= 2:
                        P.emit("pool", lambda e, zi=zi, b=b: e.memset(ZQ[zi][:, :, (b - 2) * 8:(b - 1) * 8], 0.0), [], [B("ZQ%d" % zi)])
                    cp(ZQ[zi][:, :, b * 8:(b + 1) * 8], QsT[:, :, b * 8:(b + 1) * 8], [B("QsT")], [B("ZQ%d" % zi)], eng="pool")
                    ts(KWb[zi][:], kw[:], seqcol[:, b:b + 1], None, ALU.mult, None, [Bkw, B("seqcol")], [B("KWb%d" % zi)])
                    for k in range(2):
                        fi = (2 * b + k) % len(C0f)
                        ci = (2 * b + k) % len(C0b)
                        si = (2 * b + k) % len(Cstage)
                        P.dma("pool", C0f[fi][:], C0[l, b][:, k * 128:(k + 1) * 128, :].rearrange("h p e -> p h e"), writes=[B("C0f%d" % fi)])
                        cp(C0b[ci][:], C0f[fi][:], [B("C0f%d" % fi)], [B("C0b%d" % ci)], eng="act")
                        for h in range(4):
                            bank, bbuf = NB[h]
                            mmg([(bank[:, 0:256], ZQ[zi][:, 2 * h + k, :], C0b[ci][:, h, :], False, (b == 15 and k == 1))],
                                [B("ZQ%d" % zi), B("C0b%d" % ci)], [bbuf])
                        for pr in range(2):
                            ub, ubuf = UB[rnd % 3]
                            rnd += 1
                            mms = []
                            for hh in range(2):
                                h = pr * 2 + hh
                                mms.append((ub[:, hh * 256:(hh + 1) * 256], KWb[zi][:, h * 256 + k * 128: h * 256 + k * 128 + 128], VV[:, gi, h * 256:(h + 1) * 256], True, True))
                            mmg(mms, [B("KWb%d" % zi), Bv], [ubuf])
                            for hh in range(2):
                                h = pr * 2 + hh
                                stt(Cstage[si][:, h, :], C0f[fi][:, h, :], DEC[:, b * 4 + h: b * 4 + h + 1], ub[:, hh * 256:(hh + 1) * 256], ALU.mult, ALU.add,
                                    [B("DEC"), B("C0f%d" % fi)], [B("Cstage%d" % si), ubuf])
                        P.dma("sp", oCs[l, b][:, k * 128:(k + 1) * 128, :].rearrange("h p e -> p h e"), Cstage[si][:], reads=[B("Cstage%d" % si)])
            mms = []
            rds = [B("SWT"), B("onesb")]
            for h in range(4):
                o = psm[:, 8 + h:9 + h]
                if kind == 0 and use_inter:
                    mms.append((o, SWT[:, h * 128:(h + 1) * 128], onesb[:, 0:1], True, False))
                    mms.append((o, QsT[:, 2 * h, :], nbf[l][:, 0, h:h + 1], False, False))
                    mms.append((o, QsT[:, 2 * h + 1, :], nbf[l][:, 1, h:h + 1], False, True))
                    rds += [B("QsT"), B("nbf%d" % l)]
                else:
                    mms.append((o, SWT[:, h * 128:(h + 1) * 128], onesb[:, 0:1], True, True))
            mmg(mms, rds, [Bps])
            cp(S(32, 36), psm[:, 8:12], [], [B("sm_den"), Bps])
            if kind == 1:
                P.dma("sp", tmpscr[:], n0tok[l], writes=[B("tmpscr")])
                tt(tmpscr[:], tmpscr[:], QB[:, gi, :], ALU.mult, [Bq, B("tmpscr")], [B("tmpscr")])
                red(S(52, 56), tmpscr[:].rearrange("p (h d) -> p h d", h=4), ALU.add, [B("tmpscr")], [B("sm_qn")])
                tt(S(52, 56), S(52, 56), S(20, 24), ALU.mult, [B("sm_qn"), B("sm_winter")], [B("sm_qn")])
                tt(S(32, 36), S(32, 36), S(52, 56), ALU.add, [B("sm_den"), B("sm_qn")], [B("sm_den")])
            tt(S(24, 28), S(12, 16), psm[:, c0:c0 + 4], ALU.subtract, [B("sm_g")], [B("sm_mt"), Bps])
            act(S(28, 32), S(24, 28), AF.Exp, [B("sm_mt")], [B("sm_emt")], scale=-1.0)
            stt(S(36, 40), S(32, 36), -1.0, S(32, 36), ALU.mult, ALU.max, [B("sm_den")], [B("sm_r")])
            tt(S(36, 40), S(36, 40), S(28, 32), ALU.max, [B("sm_r"), B("sm_emt")], [B("sm_r")])
            P.emit("dve", lambda e: e.reciprocal(out=S(36, 40), in_=S(36, 40)), [B("sm_r")], [B("sm_r")])
            Bf = B("F1_%d" % gi)
            for h in range(4):
                bank, bbuf = NB[h]
                act(F1[:, gi, h * 256:(h + 1) * 256], bank[:, 0:256], AF.Copy, [B("sm_r")], [Bf, bbuf], scale=S(36 + h, 37 + h))
            if kind == 0:
                UBk = [(ptf[0], Bpt[0]), (ptf[1], Bpt[1]), (pm[2], Bpm[2])]
                rnd = 0
                for hf in range(2):
                    for pr in range(2):
                        ub, ubuf = UBk[rnd % 3]
                        rnd += 1
                        mms = []
                        for hh in range(2):
                            h = pr * 2 + hh
                            mms.append((ub[:, hh * 256:(hh + 1) * 256], kw[:, h * 256 + hf * 128: h * 256 + hf * 128 + 128], VV[:, gi, h * 256:(h + 1) * 256], True, True))
                        mmg(mms, [Bkw, Bv], [ubuf])
                        for hh in range(2):
                            h = pr * 2 + hh
                            stt(Cst[l][:, hf, h, :], Cst[l][:, hf, h, :], S(40 + h, 41 + h), ub[:, hh * 256:(hh + 1) * 256], ALU.mult, ALU.add,
                                [B("sm_decay"), B("Cst%d" % l)], [B("Cst%d" % l), ubuf])
                    cp(Cbf[l][:, hf], Cst[l][:, hf], [B("Cst%d" % l)], [B("Cbf%d" % l)], eng="act")
                mmg([(psm[:, 16 + hf * 4 + h: 17 + hf * 4 + h], kw[:, h * 256 + hf * 128: h * 256 + hf * 128 + 128], onesb[:, 0:1], True, True)
                     for hf in range(2) for h in range(4)], [Bkw, B("onesb")], [Bps])
                tt(nst[l][:], nst[l][:], S(40, 44).unsqueeze(1).to_broadcast([128, 2, 4]), ALU.mult, [B("nst%d" % l), B("sm_decay")], [B("nst%d" % l)])
                tt(nst[l][:], nst[l][:], psm[:, 16:24].rearrange("p (k h) -> p k h", k=2), ALU.add, [B("nst%d" % l)], [B("nst%d" % l), Bps])
                cp(nbf[l][:], nst[l][:], [B("nst%d" % l)], [B("nbf%d" % l)])
            else:
                mmg([(pa[0][0:16, 0:4], lastsel[:], S(20, 24), True, True), (pa[0][0:16, 4:8], lastsel[:], S(24, 28), True, True)],
                    [B("lastsel"), B("sm_winter"), B("sm_mt")], [Bpa[0]])
                cp(s16[:, 0:8], pa[0][0:16, 0:8], [], [B("s16"), Bpa[0]])
                P.dma("sp", oms[l], s16[:, 4:8], reads=[B("s16")])
                n16 = tmpscr[0:16, :]
                P.dma("sp", n16, n0[l], writes=[B("tmpscr")])
                tt(n16.rearrange("p (h d) -> p h d", h=4), n16.rearrange("p (h d) -> p h d", h=4), s16[:, 0:4].unsqueeze(2).to_broadcast([16, 4, 256]), ALU.mult,
                   [B("tmpscr"), B("s16")], [B("tmpscr")])
                for cb in range(2):
                    mmg([(pa[1][0:16, :], seqcolb[:], kw[:, cb * 512:(cb + 1) * 512], True, True)], [B("seqcolb"), Bkw], [Bpa[1]])
                    tt(n16[:, cb * 512:(cb + 1) * 512], n16[:, cb * 512:(cb + 1) * 512], pa[1][0:16, :], ALU.add, [B("tmpscr")], [B("tmpscr"), Bpa[1]])
                P.dma("sp", ons[l], n16, reads=[B("tmpscr")])

        def layer(l, tiles, kind):
            G = len(tiles)
            gis = list(range(G))
            cur["l"] = l
            cur["blk"] = 0
            cur["first"] = (tiles[0] == 1)
            P.dma("sp", lng[:], sgu_ln_g[l].to_broadcast([128, D]), writes=[B("lng")])
            P.dma("sp", lnb[:], sgu_ln_b[l].to_broadcast([128, D]), writes=[B("lnb")])
            for cb in range(2):
                pi = nxt("pm", 3)
                mmg([(pm[pi][:], selmod[32 * l:32 * l + 17, kind, :], gate17[32 * l:32 * l + 17, cb * 512:(cb + 1) * 512], True, True)], [B("selmod"), B("gate17")], [Bpm[pi]])
                cp(gbc[:, cb * 512:(cb + 1) * 512], pm[pi][:], [], [B("gbc"), Bpm[pi]], eng="act")
            if l == 0:
                pipe2(G, lambda gi: make_h_A(l, gi, kind), lambda gi: make_h_B(l, gi, kind))
            def ev_va(gi, cb, bank, bb, n):
                act(F1[:, gi, cb * 512:(cb + 1) * 512], bank[:], AF.Gelu_apprx_tanh, [], [B("F1_%d" % gi), bb])
            proj(l, 4096, 1024, gis, ev_va)
            for gi in gis:
                Bf = B("F1_%d" % gi)
                P.emit("dve", lambda e, gi=gi: e.bn_stats(out=lnst[:, 2 * gi, :], in_=F1[:, gi, 0:512]), [Bf], [B("lnst")])
                P.emit("dve", lambda e, gi=gi: e.bn_stats(out=lnst[:, 2 * gi + 1, :], in_=F1[:, gi, 512:1024]), [Bf], [B("lnst")])
                P.emit("dve", lambda e, gi=gi: e.bn_aggr(out=lnmv[:, gi, :], in_=lnst[:, 2 * gi:2 * gi + 2, :]), [B("lnst")], [B("lnmv")])
            act(lnrs[:, 0:G], lnmv[:, 0:G, 1], AF.Sqrt, [B("lnmv"), B("epsc")], [B("lnrs")], bias=epsc[:])
            P.emit("dve", lambda e: e.reciprocal(out=lnrs[:, 0:G], in_=lnrs[:, 0:G]), [B("lnrs")], [B("lnrs")])
            def sgu_A(gi):
                Bf = B("F1_%d" % gi)
                ts(F1[:, gi, :], F1[:, gi, :], lnmv[:, gi, 0:1], lnrs[:, gi:gi + 1], ALU.subtract, ALU.mult, [Bf, B("lnmv"), B("lnrs")], [Bf])
                tt(F1[:, gi, :], F1[:, gi, :], lng[:], ALU.mult, [Bf, B("lng")], [Bf])
                tt(F1[:, gi, :], F1[:, gi, :], lnb[:], ALU.add, [Bf, B("lnb")], [Bf])
                if kind == 1:
                    P.dma("sp", ovs[l], F1[:, gi, :], reads=[Bf])
                cp(VV[:, gi, :], F1[:, gi, :], [Bf], [B("VV%d" % gi)], eng="act")

            def sgu_B(gi):
                Bf = B("F1_%d" % gi)
                for cb in range(2):
                    pi = nxt("pm", 3)
                    mmg([(pm[pi][:, gg * 256:(gg + 1) * 256], WTs[:, l, kind, cb * 2 + gg, :], VV[:, gi, (cb * 2 + gg) * 256:(cb * 2 + gg + 1) * 256], True, True) for gg in range(2)],
                        [B("WTs"), B("VV%d" % gi)], [Bpm[pi]])
                    for gg in range(2):
                        g4 = cb * 2 + gg
                        act(F1[:, gi, g4 * 256:(g4 + 1) * 256], pm[pi][:, gg * 256:(gg + 1) * 256], AF.Identity, [B("bsg")], [Bf, Bpm[pi]],
                            bias=bsg[:, l, kind, g4:g4 + 1])

            def ev_q(gi, cb, bank, bb, n):
                cp(QB[:, gi, cb * 512:(cb + 1) * 512], bank[:], [], [B("QB%d" % gi), bb], eng="act")

            def ev_k(gi, cb, bank, bb, n):
                act(KB[:, gi, cb * 512:(cb + 1) * 512], bank[:], AF.Copy, [], [B("KB%d" % gi), bb], scale=1.0 / 16.0)
            steps = []
            for i in range(G + 1):
                if i < G:
                    steps.append(lambda i=i: sgu_A(i))
                if i >= 1:
                    steps.append(lambda i=i: sgu_B(i - 1))
            qpos, kpos = min(2, len(steps) - 1), min(5, len(steps))
            for j, st in enumerate(steps):
                if j == qpos:
                    proj(l, 6144, 1024, gis, ev_q)
                if j == kpos:
                    proj(l, 7168, 1024, gis, ev_k)
                st()
            if kpos >= len(steps):
                proj(l, 7168, 1024, gis, ev_k)

            def ev_mul(func):
                def ev(gi, cb, bank, bb, n):
                    t_ = tmpA[nxt("tmp", 2)]
                    act(t_[:], bank[:], func, [], [B(t_.name), bb])
                    tt(F1[:, gi, cb * 512:(cb + 1) * 512], F1[:, gi, cb * 512:(cb + 1) * 512], t_[:], ALU.mult, [B(t_.name), B("F1_%d" % gi)], [B("F1_%d" % gi)])
                return ev
            proj(l, 3072, 1024, gis, ev_mul(AF.Gelu_apprx_tanh))
            proj(l, 5120, 1024, gis, ev_mul(AF.Silu))

            def gate_proj(bi, cb):
                def ev_gate(gi, cb_, bank, bb, n):
                    act(GT[:, gi, :], bank[:], AF.Sigmoid, [], [B("GT%d" % gi), bb])
                proj(l, bi * 1024 + cb * 512, 512, gis, ev_gate)

            def branch_out(bi, scaleT=None, scale_bufs=(), gate0_done=False):
                ybufs = {}
                for gi in gis:
                    ybf, By = ybfs[gi % 2], B("ybf%d" % (gi % 2))
                    cp(ybf[:], F1[:, gi, :], [B("F1_%d" % gi)], [By], eng="act")
                    ybufs[gi] = to_T(YT, gi, ybf, [By], scaleT, scale_bufs)
                for cb in range(2):
                    if not (cb == 0 and gate0_done):
                        gate_proj(bi, cb)

                    def ev_br(gi, cb_, bank, bb, n, cb=cb):
                        mgv = MG[:, gi, cb * 512:(cb + 1) * 512]
                        if bi == 0:
                            tt(mgv, bank[:], GT[:, gi, :], ALU.mult, [B("GT%d" % gi)], [B("MG%d" % gi), bb])
                        else:
                            t_ = tmpA[nxt("tmp", 2)]
                            tt(t_[:], bank[:], GT[:, gi, :], ALU.mult, [B("GT%d" % gi)], [B(t_.name), bb])
                            tt(mgv, mgv, t_[:], ALU.add, [B(t_.name), B("MG%d" % gi)], [B("MG%d" % gi)])
                    proj(l, cb * 512, 512, gis, ev_br, src=w_br[bi][l], srcT=YT, srcbufs=ybufs)
            branch_out(0)

            def ev_v(gi, cb, bank, bb, n):
                cp(VV[:, gi, cb * 512:(cb + 1) * 512], bank[:], [], [B("VV%d" % gi), bb])

            def ev_g(gi, cb, bank, bb, n):
                tt(GP[:, gi, :], bank[:, 0:8], bifb[:, l, :], ALU.add, [B("bifb")], [B("GP%d" % gi), bb])
            proj(l, 8192, 1024, gis, ev_v)
            proj(l, 11264, 8, gis, ev_g)
            for gi in gis:
                mlstm_head(l, gi, kind)
            for gi in gis:
                mlstm_tile(l, gi, tiles[gi], kind)
            proj(l, 9216, 1024, gis, ev_mul(AF.Sigmoid))
            gate_proj(1, 0)
            for gi in gis:
                Bf = B("F1_%d" % gi)
                for h in range(4):
                    u = gi * 4 + h
                    P.emit("dve", lambda e, gi=gi, h=h, u=u: e.bn_stats(out=lnst[:, u, :], in_=F1[:, gi, h * 256:(h + 1) * 256]), [Bf], [B("lnst")])
                    P.emit("dve", lambda e, u=u: e.bn_aggr(out=lnmv[:, u, :], in_=lnst[:, u, :]), [B("lnst")], [B("lnmv")])
            act(lnrs[:, 0:4 * G], lnmv[:, 0:4 * G, 1], AF.Sqrt, [B("lnmv"), B("epsc")], [B("lnrs")], bias=epsc[:])
            P.emit("dve", lambda e: e.reciprocal(out=lnrs[:, 0:4 * G], in_=lnrs[:, 0:4 * G]), [B("lnrs")], [B("lnrs")])
            for gi in gis:
                Bf = B("F1_%d" % gi)
                for h in range(4):
                    u = gi * 4 + h
                    ts(F1[:, gi, h * 256:(h + 1) * 256], F1[:, gi, h * 256:(h + 1) * 256], lnmv[:, u, 0:1], lnrs[:, u:u + 1], ALU.subtract, ALU.mult,
                       [Bf, B("lnmv"), B("lnrs")], [Bf])
            proj(l, 10240, 1024, gis, ev_mul(AF.Silu))
            branch_out(1, mngT[:, l, :], [B("mngT")], gate0_done=True)

            def ev_p(gi, cb, bank, bb, n):
                cp(F1[:, gi, cb * 512:(cb + 1) * 512], bank[:], [], [B("F1_%d" % gi), bb], eng="act")
            proj(l, 11272, 1024, gis, ev_p)
            wpl, Bwpl = wload(w_pool[l].rearrange("g (k p) o -> p (g k) o", p=128), 256, pre=True)
            gate_proj(2, 0)
            if kind == 1:
                P.dma("pool", pbufA[0:120, :], pool0[l, 0], writes=[B("pbufA")])
                P.dma("pool", pbufB[0:120, :], pool0[l, 1], writes=[B("pbufB")])
            def pool_A(gi):
                Bf = B("F1_%d" % gi)
                tile = tiles[gi]
                ybf, By = ybfs[gi % 2], B("ybf%d" % (gi % 2))
                xnb, Bx = xnbs[gi % 2], B("xnb%d" % (gi % 2))
                QTp, Bqt = (QT, B("QT")) if gi % 2 == 0 else (KT, B("KT"))
                if gi == 0:
                    prv, Bprv = pprev[l], B("pprev%d" % l)
                else:
                    prv, Bprv = ybfs[(gi - 1) % 2], B("ybf%d" % ((gi - 1) % 2))
                cp(ybf[:], F1[:, gi, :], [Bf], [By], eng="act")
                if kind == 1:
                    for b in range(16):
                        P.dma("sp", obs[l, b, 7:15, :], F1[b * 8:(b + 1) * 8, gi, :], reads=[Bf])
                    P.dma("sp", obs[l][:, 0:7, :], pool0raw[l][:, 8:15, :])
                elif tile == 16:
                    P.dma("sp", obp[l], F1[113:128, gi, :], reads=[Bf])
                rci = 0 if kind == 1 else (1 if tile == 1 else 2)
                for cb in range(2):
                    pi = nxt("pm", 3)
                    mms = []
                    rds = [B("bandcur"), By]
                    for gg in range(2):
                        wi = cb * 2 + gg
                        cs = slice(wi * 256, (wi + 1) * 256)
                        o = pm[pi][:, gg * 256:(gg + 1) * 256]
                        if kind == 0:
                            if tile == 1:
                                mms.append((o, bandcur[:, 0, wi, :], ybf[:, cs], True, True))
                            else:
                                mms.append((o, bandcur[:, 0, wi, :], ybf[:, cs], True, False))
                                mms.append((o, bandprev[:, wi, :], prv[:, cs], False, True))
                                rds += [B("bandprev"), Bprv]
                        else:
                            mms.append((o, bandcur[:, 1, wi, :], ybf[:, cs], True, False))
                            mms.append((o, bandbuf[0:120, 0, wi, :], pbufA[0:120, cs], False, False))
                            mms.append((o, bandbuf[0:120, 1, wi, :], pbufB[0:120, cs], False, True))
                            rds += [B("bandbuf"), B("pbufA"), B("pbufB")]
                    mmg(mms, rds, [Bpm[pi]])
                    for gg in range(2):
                        wi = cb * 2 + gg
                        cs = slice(wi * 256, (wi + 1) * 256)
                        stt(xnb[:, cs], pm[pi][:, gg * 256:(gg + 1) * 256], rc[:, rci, wi:wi + 1], F1[:, gi, cs], ALU.mult, ALU.subtract,
                            [B("rc"), Bf], [Bx, Bpm[pi]])

            def pool_B(gi):
                Bf = B("F1_%d" % gi)
                tile = tiles[gi]
                ybf, By = ybfs[gi % 2], B("ybf%d" % (gi % 2))
                xnb, Bx = xnbs[gi % 2], B("xnb%d" % (gi % 2))
                QTp, Bqt = (QT, B("QT")) if gi % 2 == 0 else (KT, B("KT"))
                if kind == 0 and gi == G - 1:
                    cp(pprev[l][:], ybf[:], [By], [B("pprev%d" % l)], eng="act")
                pi = nxt("pt", 2)
                transposes(pi, lambda c, xnb=xnb: xnb[:, c * 128:(c + 1) * 128], 8, [Bx])
                cp(QTp[:], pt[pi][:], [], [Bqt, Bpt[pi]])
                for cb in range(2):
                    pj = nxt("pm", 3)
                    mmg([(pm[pj][:, gg * 256:(gg + 1) * 256], QTp[:, (cb * 2 + gg) * 2 + k, :], wpl[:, (cb * 2 + gg) * 2 + k, 0:256], k == 0, k == 1) for gg in range(2) for k in range(2)],
                        [Bqt, Bwpl], [Bpm[pj]])
                    cp(F1[:, gi, cb * 512:(cb + 1) * 512], pm[pj][:], [], [Bf, Bpm[pj]], eng="act")
            pipe2(G, pool_A, pool_B)
            proj(l, 12296, 1024, gis, ev_mul(AF.Silu))
            branch_out(2, pscT[:, l, :], [B("pscT")], gate0_done=True)

            mbufs = {}
            for gi in gis:
                ybf, By = ybfs[gi % 2], B("ybf%d" % (gi % 2))
                cp(ybf[:], MG[:, gi, :], [B("MG%d" % gi)], [By], eng="act")
                mbufs[gi] = to_T(YT, gi, ybf, [By])

            if l == 1:
                P.dma("sp", lng[:], fin_g.to_broadcast([128, D]), writes=[B("lng")])

            def ev_out(gi, cb, bank, bb, n):
                t_ = tmpA[nxt("tmp", 2)]
                tt(t_[:], bank[:], gbc[:, cb * 512:(cb + 1) * 512], ALU.mult, [B("gbc")], [B(t_.name), bb])
                xv = xg[:, gi, cb * 512:(cb + 1) * 512]
                tt(xv, xv, t_[:], ALU.add, [B(t_.name), B("xg%d" % gi)], [B("xg%d" % gi)])
                if cb != 1:
                    return
                if l == 0:
                    make_h_A(1, gi, kind)
                    if gi >= 1:
                        make_h_B(1, gi - 1, kind)
                    if gi == G - 1:
                        make_h_B(1, gi, kind)
                else:
                    xb_ = B("xg%d" % gi)
                    rstd, Br = rms_stats(xg[:, gi, :], [xb_], gi)
                    stt(MG[:, gi, :], xg[:, gi, :], rstd, lng[:], ALU.mult, ALU.mult, [xb_, Br, B("lng")], [B("MG%d" % gi)])
                    P.dma("sp", y[tiles[gi]], MG[:, gi, :], reads=[B("MG%d" % gi)])
            proj(l, 0, 1024, gis, ev_out, src=w_out[l], srcT=YT, srcbufs=mbufs)

        def prompt_state_out():
            for l in range(2):
                for k in range(2):
                    P.dma("sp", oCp[l][:, k * 128:(k + 1) * 128, :].rearrange("h p e -> p h e"), Cst[l][:, k], reads=[B("Cst%d" % l)])
                    P.dma("sp", onp[l][:, k * 128:(k + 1) * 128].rearrange("h p -> p h"), nst[l][:, k, :], reads=[B("nst%d" % l)], allow_slow_non_contiguous=True)
                P.dma("sp", omp[l:l + 1, :], mrep[l][0:1, :], reads=[B("mrep%d" % l)])
            fb = ["Cst0", "Cst1", "Cbf0", "Cbf1", "pprev0", "pprev1", "QB1", "QB2", "ZQ0", "ZQ1", "KWb0", "KWb1", "C0f0", "C0f1", "C0f2",
                  "C0b0", "C0b1", "C0b2", "Cstage0", "Cstage1", "pbufA", "pbufB", "xg1", "xg2", "xg3", "F1_1", "F1_2", "F1_3", "MG1", "MG2", "MG3",
                  "VV1", "VV2", "VV3", "GT1", "GT2", "GT3", "KB1", "KB2", "wr0", "wr1", "wr2", "wr3", "wr4"]
            P.emit("pool", lambda e: e.memset(fence_t[:], 0.0), [], [B(n) for n in fb])

        for tiles in GROUPS:
            kind = 1 if tiles[0] == 0 else 0
            if kind == 1:
                prompt_state_out()
                cur["nw"] = len(wring_x)
            for gi, tile in enumerate(tiles):
                P.dma("sp", xg[:, gi, :], xin[tile], writes=[B("xg%d" % gi)])
            for l in range(2):
                layer(l, tiles, kind)
        P.finish()
        with nc.Block() as block:
            P.replay(block)
    return nc


def _host_inputs(inp, core):
    f = lambda a: np.ascontiguousarray(a, dtype=np.float32)
    bs = slice(core * 16, core * 16 + 16)
    m = {}
    xs = inp["x_sample"][bs].reshape(1, 128, D)
    xp = inp["x_prompt"][core].reshape(16, 128, D)
    m["xin"] = f(np.concatenate([xs, xp], axis=0))
    m["cc"] = f(np.concatenate([inp["c_prompt"][core:core + 1], inp["c_sample"][bs]], axis=0))
    m["w_mod"] = f(inp["w_mod"])
    bm = np.asarray(inp["b_mod"])
    m["b_modT"] = f(bm[:, :2048].reshape(2, 16, 128).transpose(0, 2, 1))
    m["b_modg"] = f(bm[:, 2048:].reshape(2, 1, D))
    tT = lambda a: f(np.asarray(a).reshape(2, 8, 128).transpose(0, 2, 1))
    m["norm_gT"] = tT(inp["norm_g"])
    m["w_in"] = f(inp["w_in"])
    m["b_if"] = f(np.asarray(inp["b_if"]).reshape(2, 1, 8))
    m["sgu_ln_g"] = f(np.asarray(inp["sgu_ln_g"]).reshape(2, 1, D))
    m["sgu_ln_b"] = f(np.asarray(inp["sgu_ln_b"]).reshape(2, 1, D))
    ws = np.asarray(inp["w_sgu"])
    wsg = np.zeros((2, 2, 4, 128, 128), np.float32)
    wsg[:, 0] = ws
    for b in range(16):
        wsg[:, 1, :, b * 8:(b + 1) * 8, b * 8:(b + 1) * 8] = ws[:, :, :8, :8]
    m["wsgu"] = wsg
    bsu = np.asarray(inp["b_sgu"])
    bsg = np.zeros((2, 2, 128, 4), np.float32)
    bsg[:, 0] = bsu.transpose(0, 2, 1)
    bsg[:, 1] = np.tile(bsu[:, :, :8], (1, 1, 16)).transpose(0, 2, 1)
    m["bsgu"] = bsg
    m["mnorm_gT"] = tT(inp["mlstm_norm_g"])
    m["w_pool"] = f(inp["w_pool"])
    m["pscaleT"] = tT(inp["pool_scale"])
    m["w_br_a"] = f(inp["w_br_a"])
    m["w_br_b"] = f(inp["w_br_b"])
    m["w_br_c"] = f(inp["w_br_c"])
    m["w_out"] = f(inp["w_out"])
    m["fin_g"] = f(np.asarray(inp["final_norm_g"]).reshape(1, D))
    m["C0"] = f(inp["state_mlstm_C"][:, bs])
    n0 = np.asarray(inp["state_mlstm_n"])[:, bs].reshape(2, 16, D)
    m["n0"] = f(n0)
    m["n0tok"] = f(np.repeat(n0, 8, axis=1))
    m["m0tok"] = f(np.repeat(np.asarray(inp["state_mlstm_m"])[:, bs], 8, axis=1))
    sp = np.asarray(inp["state_pool"])[:, bs]
    m["pool0"] = f(sp.reshape(2, 2, 120, D))
    m["pool0raw"] = f(sp)
    global _CONSTS
    if _CONSTS is None:
        _CONSTS = make_consts()
    for k, v in _CONSTS.items():
        m["c_" + k] = v
    return m


_NC = None


def kernel(**inputs):
    global _NC
    inp = {k: np.asarray(v) for k, v in inputs.items()}
    if _NC is None:
        _NC = build_nc()
    in_maps = [_host_inputs(inp, c) for c in range(8)]
    res = run_bass_kernel_spmd(_NC, in_maps, core_ids=list(range(8)))
    R = res.results
    y_prompt = np.stack([r["y"][1:].reshape(2048, D) for r in R])
    y_sample = np.concatenate([r["y"][0].reshape(16, 8, D) for r in R], axis=0)
    Cp = np.stack([r["oCp"] for r in R], axis=1)
    npp = np.stack([r["onp"] for r in R], axis=1)
    mp = np.stack([r["omp"] for r in R], axis=1)
    bp = np.stack([r["obp"] for r in R], axis=1)
    Cs = np.concatenate([r["oCs"] for r in R], axis=1)
    ns = np.concatenate([r["ons"].reshape(2, 16, 4, 256) for r in R], axis=1)
    ms = np.concatenate([r["oms"] for r in R], axis=1)
    bsn = np.concatenate([r["obs"] for r in R], axis=1)
    vs = np.concatenate([r["ovs"].reshape(2, 16, 8, D) for r in R], axis=1)
    outs = (y_prompt, y_sample, Cp, npp, mp, bp, Cs, ns, ms, bsn, vs)
    return tuple(np.ascontiguousarray(o, dtype=np.float32) for o in outs)
```
